# Optimizing a Trainium2 kernel written in Bass

```python
import jax, jax.numpy as jnp
from jax import lax
import numpy as np

D_MODEL = 2048
BATCH = 1
SEQ = 16384
DEPTH = 4

CONV_WIDTH = D_MODEL // 2
CONV_K = 3
N_HEADS = 16
N_KV_GROUPS = 4
HEADS_PER_GROUP = N_HEADS // N_KV_GROUPS
HEAD_DIM = D_MODEL // N_HEADS
KV_WIDTH = N_KV_GROUPS * HEAD_DIM
ROPE_DIM = HEAD_DIM // 4
ROPE_THETA = 500000.0
CMP_BLOCK = 32
CMP_STRIDE = 16
CMP_HIDDEN = 2 * HEAD_DIM
SEL_BLOCK = 64
SEL_TOP = 16
WINDOW = 512
Q_BLOCK = 128
D_FF = 4 * D_MODEL
NORM_EPS = 1e-6
NEG = -1e30
FORCE = 1e30
MIX_SPLITS = (CONV_WIDTH, CONV_WIDTH, CONV_WIDTH, N_HEADS * HEAD_DIM, 6 * KV_WIDTH, 3 * N_HEADS, D_MODEL, D_MODEL)
IN_WIDTH = 3 * CONV_WIDTH + N_HEADS * HEAD_DIM + 6 * KV_WIDTH + 3 * N_HEADS + 2 * D_MODEL

kernel_name = "hybrid_shortconv_nsa_adaln_block"


def rms_norm(x, g):
    xf = x.astype(jnp.float32)
    y = xf * lax.rsqrt(jnp.mean(xf * xf, axis=-1, keepdims=True) + NORM_EPS)
    return (y * g.astype(jnp.float32)).astype(x.dtype)


def rope_tables(positions):
    inv_freq = ROPE_THETA ** (-jnp.arange(0, ROPE_DIM, 2, dtype=jnp.float32) / ROPE_DIM)
    ang = positions.astype(jnp.float32)[..., None] * inv_freq
    return jnp.cos(ang), jnp.sin(ang)


def partial_rope(x, cos, sin):
    half = ROPE_DIM // 2
    xr = x[..., :ROPE_DIM].astype(jnp.float32)
    x1, x2 = xr[..., :half], xr[..., half:]
    c = cos[:, :, None, :]
    s = sin[:, :, None, :]
    rot = jnp.concatenate([x1 * c - x2 * s, x2 * c + x1 * s], axis=-1).astype(x.dtype)
    return jnp.concatenate([rot, x[..., ROPE_DIM:]], axis=-1)


def short_conv_mixer(b_gate, c_gate, xa, conv_w, w_out):
    u = c_gate * xa
    z = lax.conv_general_dilated(u, conv_w[:, None, :].astype(u.dtype), window_strides=(1,),
                                 padding=[(CONV_K - 1, 0)], dimension_numbers=('NWC', 'WIO', 'NWC'),
                                 feature_group_count=CONV_WIDTH)
    return (b_gate * z) @ w_out


def compress(kv, pe, w1, b1, w2, b2):
    B, S, G, dh = kv.shape
    n_cmp = (S - CMP_BLOCK) // CMP_STRIDE + 1
    idx = jnp.arange(n_cmp)[:, None] * CMP_STRIDE + jnp.arange(CMP_BLOCK)[None, :]
    blocks = kv[:, idx] + pe[None, None, :, None, :]
    blocks = blocks.transpose(0, 1, 3, 2, 4).reshape(B, n_cmp, G, CMP_BLOCK * dh)
    h = jax.nn.gelu(blocks @ w1 + b1)
    return h @ w2 + b2


def nsa_attention(q, kc, vc, ks, vs, kw, vw, gate):
    B, S, H, dh = q.shape
    G, R = N_KV_GROUPS, HEADS_PER_GROUP
    n_cmp = kc.shape[1]
    n_slc = S // SEL_BLOCK
    top = min(SEL_TOP, n_slc)
    n_qb = S // Q_BLOCK
    scale = dh ** -0.5
    qg = q.reshape(B, S, G, R, dh)
    gate = gate.reshape(B, S, G, R, 3)
    cmp_end = jnp.arange(n_cmp) * CMP_STRIDE + CMP_BLOCK - 1
    cs = jnp.arange(n_cmp)[:, None] * CMP_STRIDE
    ss = jnp.arange(n_slc)[None, :] * SEL_BLOCK
    overlap = jnp.clip(jnp.minimum(cs + CMP_BLOCK, ss + SEL_BLOCK) - jnp.maximum(cs, ss), 0, None).astype(jnp.float32) / CMP_BLOCK
    ks_blk = ks.reshape(B, n_slc, SEL_BLOCK, G, dh).transpose(0, 3, 1, 2, 4)
    vs_blk = vs.reshape(B, n_slc, SEL_BLOCK, G, dh).transpose(0, 3, 1, 2, 4)
    kw_pad = jnp.pad(kw, ((0, 0), (WINDOW, 0), (0, 0), (0, 0)))
    vw_pad = jnp.pad(vw, ((0, 0), (WINDOW, 0), (0, 0), (0, 0)))
    b_ix = jnp.arange(B)[:, None, None, None]
    g_ix = jnp.arange(G)[None, :, None, None]
    blk_ids = jnp.arange(n_slc)

    def one_block(i):
        s0 = i * Q_BLOCK
        t = s0 + jnp.arange(Q_BLOCK)
        qb = lax.dynamic_slice_in_dim(qg, s0, Q_BLOCK, axis=1)
        gb = lax.dynamic_slice_in_dim(gate, s0, Q_BLOCK, axis=1)
        sc = jnp.einsum('btgrd,bngd->bgrtn', qb, kc, preferred_element_type=jnp.float32) * scale
        m_c = cmp_end[None, :] <= t[:, None]
        p_c = jnp.where(m_c, jax.nn.softmax(jnp.where(m_c, sc, NEG), axis=-1), 0.0)
        o_c = jnp.einsum('bgrtn,bngd->btgrd', p_c.astype(vc.dtype), vc)
        imp = jnp.einsum('bgrtn,nj->bgtj', p_c, overlap)
        cur = t // SEL_BLOCK
        forced = (blk_ids[None, :] == cur[:, None]) | (blk_ids[None, :] == 0)
        valid = blk_ids[None, :] <= cur[:, None]
        imp = jnp.where(forced, FORCE, jnp.where(valid, imp, NEG))
        _, sel = lax.top_k(imp, top)
        k_sel = ks_blk[b_ix, g_ix, sel]
        v_sel = vs_blk[b_ix, g_ix, sel]
        kpos = sel[..., None] * SEL_BLOCK + jnp.arange(SEL_BLOCK)
        m_s = (kpos <= t[None, None, :, None, None]).reshape(B, G, 1, Q_BLOCK, top * SEL_BLOCK)
        s_s = jnp.einsum('btgrd,bgtkld->bgrtkl', qb, k_sel, preferred_element_type=jnp.float32)
        s_s = s_s.reshape(B, G, R, Q_BLOCK, top * SEL_BLOCK) * scale
        p_s = jax.nn.softmax(jnp.where(m_s, s_s, NEG), axis=-1).reshape(B, G, R, Q_BLOCK, top, SEL_BLOCK)
        o_s = jnp.einsum('bgrtkl,bgtkld->btgrd', p_s.astype(v_sel.dtype), v_sel)
        kwb = lax.dynamic_slice_in_dim(kw_pad, s0, Q_BLOCK + WINDOW, axis=1)
        vwb = lax.dynamic_slice_in_dim(vw_pad, s0, Q_BLOCK + WINDOW, axis=1)
        wpos = s0 - WINDOW + jnp.arange(Q_BLOCK + WINDOW)
        diff = t[:, None] - wpos[None, :]
        m_w = (diff >= 0) & (diff < WINDOW) & (wpos[None, :] >= 0)
        s_w = jnp.einsum('btgrd,bkgd->bgrtk', qb, kwb, preferred_element_type=jnp.float32) * scale
        p_w = jax.nn.softmax(jnp.where(m_w, s_w, NEG), axis=-1)
        o_w = jnp.einsum('bgrtk,bkgd->btgrd', p_w.astype(vwb.dtype), vwb)
        return gb[..., 0:1] * o_c + gb[..., 1:2] * o_s + gb[..., 2:3] * o_w

    out = lax.map(one_block, jnp.arange(n_qb))
    return out.transpose(1, 0, 2, 3, 4, 5).reshape(B, S, H * dh)


def hybrid_mixer(h, cos, sin, w_in, conv_w, w_conv_out, cmp_pe, cmp_w1, cmp_b1, cmp_w2, cmp_b2, w_nsa_out, w_out):
    B, S, _ = h.shape
    points = [int(p) for p in np.cumsum(MIX_SPLITS)[:-1]]
    bg, cg, xa, q, kv, g_nsa, g_a, g_b = jnp.split(h @ w_in, points, axis=-1)
    y_a = short_conv_mixer(bg, cg, xa, conv_w, w_conv_out)
    q = partial_rope(q.reshape(B, S, N_HEADS, HEAD_DIM), cos, sin)
    kv = kv.reshape(B, S, 6, N_KV_GROUPS, HEAD_DIM)
    kc_raw = partial_rope(kv[:, :, 0], cos, sin)
    vc_raw = kv[:, :, 1]
    ks = partial_rope(kv[:, :, 2], cos, sin)
    vs = kv[:, :, 3]
    kw = partial_rope(kv[:, :, 4], cos, sin)
    vw = kv[:, :, 5]
    kc = compress(kc_raw, cmp_pe[0], cmp_w1[0], cmp_b1[0], cmp_w2[0], cmp_b2[0])
    vc = compress(vc_raw, cmp_pe[1], cmp_w1[1], cmp_b1[1], cmp_w2[1], cmp_b2[1])
    gate = jax.nn.sigmoid(g_nsa).reshape(B, S, N_HEADS, 3)
    y_b = nsa_attention(q, kc, vc, ks, vs, kw, vw, gate) @ w_nsa_out
    merged = jax.nn.sigmoid(g_a) * y_a + jax.nn.sigmoid(g_b) * y_b
    return merged @ w_out


def setup_inputs(seed: int = 0) -> dict:
    key = jax.random.key(seed)
    ks = jax.random.split(key, 20)
    f32 = jnp.float32
    nrm = lambda k, shape, s: jax.random.normal(k, shape, f32) * s
    return {
        "x": nrm(ks[0], (BATCH, SEQ, D_MODEL), 1.0),
        "c": nrm(ks[1], (BATCH, D_MODEL), 1.0),
        "positions": jnp.broadcast_to(jnp.arange(SEQ, dtype=jnp.int32), (BATCH, SEQ)),
        "ada_w": nrm(ks[2], (DEPTH, D_MODEL, 6 * D_MODEL), 0.5 * D_MODEL ** -0.5),
        "ada_b": nrm(ks[3], (DEPTH, 6 * D_MODEL), 0.02),
        "norm_gains": 1.0 + nrm(ks[4], (DEPTH, 4, D_MODEL), 0.05),
        "w_in": nrm(ks[5], (DEPTH, D_MODEL, IN_WIDTH), D_MODEL ** -0.5),
        "conv_w": nrm(ks[6], (DEPTH, CONV_K, CONV_WIDTH), CONV_K ** -0.5),
        "w_conv_out": nrm(ks[7], (DEPTH, CONV_WIDTH, D_MODEL), CONV_WIDTH ** -0.5),
        "cmp_pe": nrm(ks[8], (DEPTH, 2, CMP_BLOCK, HEAD_DIM), 0.1),
        "cmp_w1": nrm(ks[9], (DEPTH, 2, CMP_BLOCK * HEAD_DIM, CMP_HIDDEN), (CMP_BLOCK * HEAD_DIM) ** -0.5),
        "cmp_b1": nrm(ks[10], (DEPTH, 2, CMP_HIDDEN), 0.02),
        "cmp_w2": nrm(ks[11], (DEPTH, 2, CMP_HIDDEN, HEAD_DIM), CMP_HIDDEN ** -0.5),
        "cmp_b2": nrm(ks[12], (DEPTH, 2, HEAD_DIM), 0.02),
        "w_nsa_out": nrm(ks[13], (DEPTH, N_HEADS * HEAD_DIM, D_MODEL), (N_HEADS * HEAD_DIM) ** -0.5),
        "w_out": nrm(ks[14], (DEPTH, D_MODEL, D_MODEL), D_MODEL ** -0.5),
        "w_mlp_up": nrm(ks[15], (DEPTH, D_MODEL, D_FF), D_MODEL ** -0.5),
        "w_mlp_down": nrm(ks[16], (DEPTH, D_FF, D_MODEL), D_FF ** -0.5),
    }


def reference(x, c, positions, ada_w, ada_b, norm_gains, w_in, conv_w, w_conv_out, cmp_pe, cmp_w1, cmp_b1,
              cmp_w2, cmp_b2, w_nsa_out, w_out, w_mlp_up, w_mlp_down):
    cos, sin = rope_tables(positions)
    c_act = jax.nn.silu(c)
    for l in range(DEPTH):
        mod = c_act @ ada_w[l] + ada_b[l]
        sh1, sc1, g1, sh2, sc2, g2 = jnp.split(mod, 6, axis=-1)
        h = rms_norm(x, norm_gains[l, 0]) * (1.0 + sc1[:, None, :]) + sh1[:, None, :]
        y = hybrid_mixer(h, cos, sin, w_in[l], conv_w[l], w_conv_out[l], cmp_pe[l], cmp_w1[l], cmp_b1[l],
                         cmp_w2[l], cmp_b2[l], w_nsa_out[l], w_out[l])
        x = x + g1[:, None, :] * rms_norm(y, norm_gains[l, 1])
        h = rms_norm(x, norm_gains[l, 2]) * (1.0 + sc2[:, None, :]) + sh2[:, None, :]
        u = jnp.square(jax.nn.relu(h @ w_mlp_up[l]))
        x = x + g2[:, None, :] * rms_norm(u @ w_mlp_down[l], norm_gains[l, 3])
    return x
```

```python
import math
from contextlib import ExitStack
import numpy as np
import ml_dtypes
import concourse.bass as bass
import concourse.mybir as mybir
from concourse.bass_utils import run_bass_kernel_spmd


F32 = mybir.dt.float32
BF16 = mybir.dt.bfloat16
I32 = mybir.dt.int32
AF = mybir.ActivationFunctionType
ALU = mybir.AluOpType
AX = mybir.AxisListType


class T:
    def __init__(self, name, h):
        self.name = name
        self.h = h
        self.w = None
        self.r = []
        self.sem = None
        self.dcnt = 0
        self.multi = False
        self.ws = []

    def __getitem__(self, k):
        return self.h[k]


class Sched:
    ENG = ("pe", "act", "dve", "pool", "sp")

    def __init__(self, nc, es, same_engine_sync=True):
        self.nc = nc
        self.es = es
        self.ops = {e: [] for e in self.ENG}
        self.cnt = {e: 0 for e in self.ENG}
        self.esem = {e: es.enter_context(nc.semaphore("sem_" + e)) for e in self.ENG}
        self.waited = {e: {} for e in self.ENG}
        self.same = same_engine_sync
        self.nsem = 5
        self.ninst = 0

    def sb(self, name, shape, dt):
        return T(name, self.es.enter_context(self.nc.sbuf_tensor("sb_" + name, list(shape), dt)))

    def ps(self, name, shape, dt=F32):
        return T(name, self.es.enter_context(self.nc.psum_tensor("ps_" + name, list(shape), dt)))

    def dram(self, name, shape, dt, kind="Internal"):
        t = T(name, self.nc.dram_tensor(name, list(shape), dt, kind=kind).ap())
        t.multi = True
        return t

    def _deps(self, eng, reads, writes):
        deps = []
        for t in reads:
            if t.w is not None:
                deps.append(t.w)
            deps.extend(t.ws)
        for t in writes:
            if not t.multi:
                if t.w is not None:
                    deps.append(t.w)
            deps.extend(t.r)
        waits = []
        wd = self.waited[eng]
        own = self.esem[eng]
        best = {}
        for (sem, val) in deps:
            if sem is own and (not self.same or eng == "pe"):
                continue
            if wd.get(id(sem), 0) >= val:
                continue
            if id(sem) not in best or best[id(sem)][1] < val:
                best[id(sem)] = (sem, val)
        for k, (sem, val) in best.items():
            wd[k] = val
            waits.append((sem, val))
        return waits

    def _stamp(self, stamp, reads, writes):
        for t in writes:
            if t.multi:
                t.ws.append(stamp)
                if len(t.ws) > 12:
                    t.ws = self._compact(t.ws)
                continue
            t.w = stamp
            t.r = []
        for t in reads:
            if t in writes:
                continue
            t.r.append(stamp)
            if len(t.r) > 12:
                t.r = self._compact(t.r)

    @staticmethod
    def _compact(lst):
        m = {}
        for (s, v) in lst:
            if id(s) not in m or m[id(s)][1] < v:
                m[id(s)] = (s, v)
        return list(m.values())

    def op(self, eng, fn, reads=(), writes=()):
        waits = self._deps(eng, reads, writes)
        self.cnt[eng] += 1
        stamp = (self.esem[eng], self.cnt[eng])
        self.ops[eng].append((fn, waits, (self.esem[eng], 1)))
        self._stamp(stamp, reads, writes)
        self.ninst += 1

    def dma(self, q, fn, semt, reads=(), writes=()):
        waits = self._deps(q, reads, writes)
        if semt.sem is None:
            semt.sem = self.es.enter_context(self.nc.semaphore("dsem_" + semt.name))
            self.nsem += 1
        semt.dcnt += 16
        stamp = (semt.sem, semt.dcnt)
        self.ops[q].append((fn, waits, (semt.sem, 16)))
        self._stamp(stamp, reads, writes)
        self.ninst += 1

    def final_wait(self, eng, tiles):
        deps = []
        for t in tiles:
            if t.w is not None:
                deps.append(t.w)
            deps.extend(t.ws)
        self.ops[eng].append((None, self._compact(deps), None))

    def emit(self):
        nc = self.nc
        emap = {"pe": "tensor", "act": "scalar", "dve": "vector", "pool": "gpsimd", "sp": "sync"}
        with nc.Block() as block:
            for e in self.ENG:
                ops = self.ops[e]

                def body(engine, ops=ops):
                    for (fn, waits, inc) in ops:
                        for (sem, val) in waits:
                            engine.wait_ge(sem, val)
                        if fn is not None:
                            ins = fn(engine)
                            ins.then_inc(inc[0], inc[1])
                getattr(block, emap[e])(body)


D = 2048; TC = 2048; NT = 512; NTILE = 4; KC = 16
OFF_BG, OFF_CG, OFF_XA, OFF_Q, OFF_KV, OFF_GN, OFF_GA, OFF_GB = 0, 1024, 2048, 3072, 5120, 8192, 8240, 10288
INW = 12336
EPS = 1e-6
TWO_PI = 2.0 * math.pi


class Pool:
    def __init__(self, S, name, n, shape, dt, psum=False):
        self.t = [(S.ps if psum else S.sb)("%s%d" % (name, i), shape, dt) for i in range(n)]
        self.i = 0

    def get(self):
        t = self.t[self.i % len(self.t)]
        self.i += 1
        return t


def load_x_tile(S, xT_d, xt, ti):
    src = xT_d.h.rearrange("(kc p) t -> p kc t", p=128)
    for q4 in range(4):
        S.dma("sp", lambda e, q4=q4: e.dma_start(out=xt[:, q4 * 4:(q4 + 1) * 4, :], in_=src[:, q4 * 4:(q4 + 1) * 4, ti * NT:(ti + 1) * NT]),
              xt, reads=[xT_d], writes=[xt])


def rms_affine(S, xt, hT, tcol0, a_t, b_t, ones_b, P):
    ss = P["ps"].get()
    for kc in range(KC):
        s = P["sq"].get()
        S.op("act", lambda e, s=s, kc=kc: e.activation(out=s[:], in_=xt[:, kc, :], func=AF.Square), reads=[xt], writes=[s])
        S.op("pe", lambda e, s=s, kc=kc: e.matmul(ss[:], ones_b[:], s[:], start=(kc == 0), stop=(kc == KC - 1)), reads=[s, ones_b], writes=[ss])
    r = P["rstd"].get()
    S.op("act", lambda e: e.activation(out=r[:], in_=ss[:], func=AF.Sqrt, bias=P["eps"][:], scale=1.0 / D), reads=[ss, P["eps"]], writes=[r])
    S.op("dve", lambda e: e.reciprocal(out=r[:], in_=r[:]), reads=[r], writes=[r])
    for kc in range(KC):
        t = P["tmp"].get()
        S.op("dve", lambda e, t=t, kc=kc: e.scalar_tensor_tensor(out=t[:], in0=xt[:, kc, :], scalar=a_t[:, kc:kc + 1], in1=r[:], op0=ALU.mult, op1=ALU.mult), reads=[xt, a_t, r], writes=[t])
        S.op("act", lambda e, t=t, kc=kc: e.activation(out=hT[:, kc, tcol0:tcol0 + NT], in_=t[:], func=AF.Identity, bias=b_t[:, kc:kc + 1], scale=1.0), reads=[t, b_t], writes=[hT])


def build_A(S, io, L):
    nc = S.nc
    xT_d = io["xT"]
    ones_b = S.sb("ones_b", [128, 128], BF16)
    S.op("dve", lambda e: e.memset(ones_b[:], 1.0), writes=[ones_b])
    modT = S.sb("modT", [128, 96], F32)
    gT = S.sb("gT", [128, 64], F32)
    S.dma("sp", lambda e: e.dma_start(out=modT[:], in_=io["modT"][:]), modT, reads=[io["modT"]], writes=[modT])
    S.dma("sp", lambda e: e.dma_start(out=gT[:], in_=io["gainsT"][:]), gT, reads=[io["gainsT"]], writes=[gT])
    a1 = S.sb("a1", [128, 16], F32)
    S.op("dve", lambda e: e.scalar_tensor_tensor(out=a1[:], in0=modT[:, 16:32], scalar=1.0, in1=gT[:, 0:16], op0=ALU.add, op1=ALU.mult), reads=[modT, gT], writes=[a1])
    b1 = S.sb("b1", [128, 16], F32)
    S.op("dve", lambda e: e.tensor_copy(out=b1[:], in_=modT[:, 0:16]), reads=[modT], writes=[b1])
    posi = S.sb("posi", [32, TC], I32)
    S.dma("sp", lambda e: e.dma_start(out=posi[:], in_=io["pos"][:]), posi, reads=[io["pos"]], writes=[posi])
    invf = S.sb("invf", [32, 2], F32)
    S.dma("sp", lambda e: e.dma_start(out=invf[:], in_=io["invf"][:]), invf, reads=[io["invf"]], writes=[invf])
    posf = S.sb("posf", [32, TC], F32)
    S.op("dve", lambda e: e.tensor_copy(out=posf[:], in_=posi[:]), reads=[posi], writes=[posf])
    ang = S.sb("ang", [32, TC], F32)
    S.op("dve", lambda e: e.tensor_scalar(out=ang[:], in0=posf[:], scalar1=invf[:, 0:1], scalar2=None, op0=ALU.mult), reads=[posf, invf], writes=[ang])
    Ct = S.sb("Ct", [32, TC], F32)
    Sn = S.sb("Snt", [32, TC], F32)
    C1 = 6.28125
    C2 = TWO_PI - C1
    yk = posf
    ki = posi
    S.op("dve", lambda e: e.tensor_scalar(out=yk[:], in0=ang[:], scalar1=1.0 / TWO_PI, scalar2=None, op0=ALU.mult), reads=[ang], writes=[yk])
    S.op("dve", lambda e: e.tensor_copy(out=ki[:], in_=yk[:]), reads=[yk], writes=[ki])
    S.op("dve", lambda e: e.tensor_copy(out=yk[:], in_=ki[:]), reads=[ki], writes=[yk])
    S.op("dve", lambda e: e.scalar_tensor_tensor(out=ang[:], in0=yk[:], scalar=-C1, in1=ang[:], op0=ALU.mult, op1=ALU.add), reads=[yk, ang], writes=[ang])
    S.op("dve", lambda e: e.scalar_tensor_tensor(out=ang[:], in0=yk[:], scalar=-C2, in1=ang[:], op0=ALU.mult, op1=ALU.add), reads=[yk, ang], writes=[ang])
    S.op("dve", lambda e: e.tensor_scalar(out=Sn[:], in0=ang[:], scalar1=-math.pi, scalar2=math.pi, op0=ALU.max, op1=ALU.min), reads=[ang], writes=[Sn])
    S.op("act", lambda e: e.activation(out=Sn[:], in_=Sn[:], func=AF.Sin), reads=[Sn], writes=[Sn])
    S.op("dve", lambda e: e.tensor_scalar(out=Sn[:], in0=Sn[:], scalar1=invf[:, 1:2], scalar2=None, op0=ALU.mult), reads=[Sn, invf], writes=[Sn])
    S.op("dve", lambda e: e.tensor_single_scalar(out=yk[:], in_=ang[:], scalar=math.pi / 2, op=ALU.is_gt), reads=[ang], writes=[yk])
    S.op("dve", lambda e: e.scalar_tensor_tensor(out=Ct[:], in0=yk[:], scalar=-TWO_PI, in1=ang[:], op0=ALU.mult, op1=ALU.add), reads=[yk, ang], writes=[Ct])
    S.op("dve", lambda e: e.tensor_scalar(out=Ct[:], in0=Ct[:], scalar1=math.pi / 2, scalar2=math.pi, op0=ALU.add, op1=ALU.min), reads=[Ct], writes=[Ct])
    S.op("dve", lambda e: e.tensor_scalar(out=Ct[:], in0=Ct[:], scalar1=-math.pi, scalar2=None, op0=ALU.max), reads=[Ct], writes=[Ct])
    S.op("act", lambda e: e.activation(out=Ct[:], in_=Ct[:], func=AF.Sin), reads=[Ct], writes=[Ct])

    hT = S.sb("hT", [128, KC, TC], BF16)
    xts = Pool(S, "xt", 1, [128, KC, NT], F32)
    P = {"ps": Pool(S, "ps", 8, [128, 512], F32, psum=True), "sq": Pool(S, "sq", 3, [128, NT], BF16),
         "rstd": Pool(S, "rstd", 2, [128, NT], F32), "tmp": Pool(S, "tmp", 3, [128, NT], F32)}
    P["eps"] = S.sb("epsc", [128, 1], F32)
    S.op("dve", lambda e: e.memset(P["eps"][:], EPS), writes=[P["eps"]])
    for ti in range(NTILE):
        xt = xts.get()
        load_x_tile(S, xT_d, xt, ti)
        rms_affine(S, xt, hT, ti * NT, a1, b1, ones_b, P)

    wts = Pool(S, "wt", 2, [128, KC, 512], BF16)
    stg = Pool(S, "stg", 4, [128, NT], F32)
    stb = Pool(S, "stb", 3, [128, NT], BF16)
    swp = Pool(S, "swp", 2, [32, NT], F32)
    rt2 = Pool(S, "rt2", 2, [32, NT], F32)
    w_in = io["w_in"]
    wsrc = w_in.h[L].rearrange("(kc p) n -> p kc n", p=128)

    def load_w(blocks):
        wt = wts.get()
        o = 0
        for (c0, w) in blocks:
            for half in range(2):
                S.dma("pool", lambda e, wt=wt, o=o, c0=c0, w=w, half=half: e.dma_start(out=wt[:, half * 8:(half + 1) * 8, o:o + w], in_=wsrc[:, half * 8:(half + 1) * 8, c0:c0 + w]),
                      wt, reads=[w_in], writes=[wt])
            o += w
        return wt

    def mm_fm(wt, off, M, ti):
        ps = P["ps"].get()
        for kc in range(KC):
            S.op("pe", lambda e, kc=kc: e.matmul(ps[:M, :], wt[:, kc, off:off + M], hT[:, kc, ti * NT:(ti + 1) * NT], start=(kc == 0), stop=(kc == KC - 1)), reads=[wt, hT], writes=[ps])
        return ps

    def mm_tm(wt, W, tb):
        ps = P["ps"].get()
        for kc in range(KC):
            S.op("pe", lambda e, kc=kc: e.matmul(ps[:, :W], hT[:, kc, tb * 128:(tb + 1) * 128], wt[:, kc, 0:W], start=(kc == 0), stop=(kc == KC - 1)), reads=[wt, hT], writes=[ps])
        return ps

    def store(dst_d, dst_ap, src_t, src_ap):
        S.dma("sp", lambda e: e.dma_start(out=dst_ap, in_=src_ap), src_t, reads=[src_t], writes=[dst_d])

    def rope_store(ps, dst_d, dst_ap, ti, do_rope=True):
        st = stg.get()
        S.op("act", lambda e: e.activation(out=st[:], in_=ps[:], func=AF.Copy), reads=[ps], writes=[st])
        if do_rope:
            sw = swp.get()
            S.dma("sp", lambda e: e.dma_start(out=sw[0:16, :], in_=st[16:32, :]), sw, reads=[st], writes=[sw])
            S.dma("sp", lambda e: e.dma_start(out=sw[16:32, :], in_=st[0:16, :]), sw, reads=[st], writes=[sw])
            t2 = rt2.get()
            tsl = slice(ti * NT, (ti + 1) * NT)
            S.op("dve", lambda e: e.tensor_tensor(out=t2[:], in0=sw[:], in1=Sn[:, tsl], op=ALU.mult), reads=[sw, Sn], writes=[t2])
            S.op("dve", lambda e: e.tensor_tensor(out=st[0:32, :], in0=st[0:32, :], in1=Ct[:, tsl], op=ALU.mult), reads=[st, Ct], writes=[st])
            S.op("dve", lambda e: e.tensor_tensor(out=st[0:32, :], in0=st[0:32, :], in1=t2[:], op=ALU.add), reads=[st, t2], writes=[st])
        sb = stb.get()
        S.op("act", lambda e: e.activation(out=sb[:], in_=st[:], func=AF.Copy), reads=[st], writes=[sb])
        store(dst_d, dst_ap, sb, sb[:])

    bgT, uT, qT, kT, vtok, gn, gaT, gbT = (io[k] for k in ("bgT", "uT", "qT", "kT", "vtok", "gn", "gaT", "gbT"))
    for jb in range(2):
        wt = load_w([(OFF_BG + jb * 512, 512)])
        for ti in range(NTILE):
            for c4 in range(4):
                ps = mm_fm(wt, c4 * 128, 128, ti)
                st = stg.get()
                S.op("act", lambda e, st=st, ps=ps: e.activation(out=st[:], in_=ps[:], func=AF.Copy), reads=[ps], writes=[st])
                ch = jb * 4 + c4
                store(bgT, bgT[ch * 128:(ch + 1) * 128, ti * NT:(ti + 1) * NT], st, st[:])
    for jb in range(4):
        wt = load_w([(OFF_CG + jb * 256, 256), (OFF_XA + jb * 256, 256)])
        for ti in range(NTILE):
            for c2 in range(2):
                pc = mm_fm(wt, c2 * 128, 128, ti)
                px = mm_fm(wt, 256 + c2 * 128, 128, ti)
                st = stg.get()
                S.op("act", lambda e, st=st, pc=pc: e.activation(out=st[:], in_=pc[:], func=AF.Copy), reads=[pc], writes=[st])
                S.op("dve", lambda e, st=st, px=px: e.tensor_tensor(out=st[:], in0=px[:], in1=st[:], op=ALU.mult), reads=[px, st], writes=[st])
                ch = jb * 2 + c2
                store(uT, uT[ch * 128:(ch + 1) * 128, ti * NT:(ti + 1) * NT], st, st[:])
    for jb in range(4):
        wt = load_w([(OFF_Q + jb * 512, 512)])
        for ti in range(NTILE):
            for c4 in range(4):
                ps = mm_fm(wt, c4 * 128, 128, ti)
                hd = jb * 4 + c4
                rope_store(ps, qT, qT[hd * 128:(hd + 1) * 128, ti * NT:(ti + 1) * NT], ti)
    for (kt_i, kvi, rope) in ((0, 0, True), (1, 1, False), (2, 2, True), (3, 4, True)):
        wt = load_w([(OFF_KV + kvi * 512, 512)])
        for ti in range(NTILE):
            for g in range(4):
                ps = mm_fm(wt, g * 128, 128, ti)
                rope_store(ps, kT, kT[kt_i, g, :, ti * NT:(ti + 1) * NT], ti, do_rope=rope)
    for (vt_i, kvi) in ((0, 3), (1, 5)):
        wt = load_w([(OFF_KV + kvi * 512, 512)])
        for tb in range(TC // 128):
            ps = mm_tm(wt, 512, tb)
            sb = stb.get()
            S.op("act", lambda e, sb=sb, ps=ps: e.activation(out=sb[:], in_=ps[:], func=AF.Copy), reads=[ps], writes=[sb])
            store(vtok, vtok[vt_i, tb * 128:(tb + 1) * 128, :], sb, sb[:])
    wt = load_w([(OFF_GN, 48)])
    for tb in range(TC // 128):
        ps = mm_tm(wt, 48, tb)
        st = stg.get()
        S.op("act", lambda e, st=st, ps=ps: e.activation(out=st[:, 0:48], in_=ps[:, 0:48], func=AF.Sigmoid), reads=[ps], writes=[st])
        store(gn, gn[tb * 128:(tb + 1) * 128, :], st, st[:, 0:48])
    for (dst, off) in ((gaT, OFF_GA), (gbT, OFF_GB)):
        for jb in range(4):
            wt = load_w([(off + jb * 512, 512)])
            for ti in range(NTILE):
                for c4 in range(4):
                    ps = mm_fm(wt, c4 * 128, 128, ti)
                    st = stg.get()
                    S.op("act", lambda e, st=st, ps=ps: e.activation(out=st[:], in_=ps[:], func=AF.Sigmoid), reads=[ps], writes=[st])
                    ch = jb * 4 + c4
                    store(dst, dst[ch * 128:(ch + 1) * 128, ti * NT:(ti + 1) * NT], st, st[:])
    S.final_wait("sp", [bgT, uT, qT, kT, vtok, gn, gaT, gbT])


SEQ = 16384
NKT = 128
SCALE = 128 ** -0.5
NSLOT = 16
GELU_C = 2.0 * math.sqrt(2.0 / math.pi)


def build_B2(S, io, slots=None, groups=None):
    slots = list(range(NSLOT)) if slots is None else slots
    groups = list(range(4)) if groups is None else groups
    Kfull, Vfull, qT, gn_d, attnT = io["Kfull"], io["Vfull"], io["qT"], io["gn"], io["attnT"]
    def const(name, shape, dt, src_ap, src_t, q="sp"):
        t = S.sb(name, shape, dt)
        S.dma(q, lambda e: e.dma_start(out=t[:], in_=src_ap), t, reads=[src_t], writes=[t])
        return t
    Ov = const("Ov", [128, 8, 256], BF16, io["Ov"][:], io["Ov"])
    Eall = const("Eall", [128, 64, 128], BF16, io["Eall"][:], io["Eall"])
    DB4 = const("DB4", [128, 8, 512], BF16, io["DB4"][:], io["DB4"])
    WB4 = const("WB4", [128, 12, 512], BF16, io["WB4"][:], io["WB4"])
    tidxrow = const("tidxrow", [128, TC], F32, io["tidxrow"][:], io["tidxrow"])
    curcol = const("curcol", [128, 16], F32, io["curcol"][:], io["curcol"])
    nthr = const("nthr", [128, 8], F32, io["nthr"][:], io["nthr"])
    blkrow = const("blkrow", [128, 256], F32, io["blkrow"][:], io["blkrow"])
    baserow = const("baserow", [128, 256], F32, io["baserow"][:], io["baserow"])
    gn = const("gnsb", [128, NSLOT, 48], F32, gn_d.h.rearrange("(j p) c -> p j c", p=128), gn_d)
    ident = S.sb("ident", [128, 128], BF16)
    S.op("dve", lambda e: e.memset(ident[:], 0.0), writes=[ident])
    S.op("pool", lambda e: e.affine_select(out=ident[:], in_=ident[:], pattern=[[-1, 128]], compare_op=ALU.not_equal, fill=1.0, base=0, channel_multiplier=1), reads=[ident], writes=[ident])

    bigK = S.sb("bigK", [128, SEQ + 32], BF16)
    vsa = S.sb("vsa", [128, NKT, 129], BF16)
    S.op("dve", lambda e: e.memset(bigK[:, SEQ:SEQ + 32], 0.0), writes=[bigK])
    S.op("dve", lambda e: e.memset(vsa[:, :, 128:129], 1.0), writes=[vsa])
    kcT = S.sb("kcT", [128, 4, 1024], BF16)
    vca = S.sb("vca", [128, 4, 8, 129], BF16)
    S.op("dve", lambda e: e.memset(vca[:, :, :, 128:129], 1.0), writes=[vca])

    psS = Pool(S, "psS", 3, [128, 512], F32, psum=True)
    psO = [S.ps("psO%d" % i, [128, 512], F32) for i in range(2)]
    psI = [S.ps("psI%d" % i, [128, 512], F32) for i in range(2)]
    psT = S.ps("psT", [128, 1024], BF16)

    qw = S.sb("qw", [128, 8192], BF16)
    class _V:
        def __init__(self, tt, pat, **kw):
            self.t = tt; self.pat = pat; self.kw = kw
        def __getitem__(self, k):
            return self.t[:].rearrange(self.pat, **self.kw)[k]
    w1v = _V(qw, "p (l h) -> p l h", h=256)
    QTv = _V(qw, "p (r t) -> p r t", t=TC)
    w2b = S.sb("w2b", [128, 2, 128], BF16)
    peT = S.sb("peT", [128, 32], BF16)
    b1T = S.sb("b1T", [128, 2], F32)
    b2c = S.sb("b2c", [128, 1], F32)
    b2row = S.sb("b2row", [128, 128], F32)
    cb = S.sb("cb", [128, 2], F32)
    Hg = S.sb("Hg", [128, 2, 1024], BF16)
    hs = Pool(S, "hs", 1, [128, 512], F32)
    hx = Pool(S, "hx", 1, [128, 512], F32)
    for ty in range(2):
        S.dma("pool", lambda e, ty=ty: e.dma_start(out=w1v[:, 0:16, :], in_=io["cmp_w1"].h[ty].rearrange("(l d) h -> d l h", d=128)[:, 0:16, :]), qw, reads=[io["cmp_w1"]], writes=[qw])
        S.dma("pool", lambda e, ty=ty: e.dma_start(out=w1v[:, 16:32, :], in_=io["cmp_w1"].h[ty].rearrange("(l d) h -> d l h", d=128)[:, 16:32, :]), qw, reads=[io["cmp_w1"]], writes=[qw])
        S.dma("pool", lambda e, ty=ty: e.dma_start(out=w2b[:], in_=io["cmp_w2"].h[ty].rearrange("(hc p) d -> p hc d", p=128)), w2b, reads=[io["cmp_w2"]], writes=[w2b])
        S.dma("pool", lambda e, ty=ty: e.dma_start(out=peT[:], in_=io["cmp_pe"].h[ty].rearrange("l d -> d l"), allow_slow_non_contiguous=True), peT, reads=[io["cmp_pe"]], writes=[peT])
        S.dma("sp", lambda e, ty=ty: e.dma_start(out=b1T[:], in_=io["cmp_b1"].h[ty].rearrange("(hc p) -> p hc", p=128), allow_slow_non_contiguous=True), b1T, reads=[io["cmp_b1"]], writes=[b1T])
        S.dma("sp", lambda e, ty=ty: e.dma_start(out=b2c[:], in_=io["cmp_b2"].h[ty].rearrange("(d o) -> d o", o=1), allow_slow_non_contiguous=True), b2c, reads=[io["cmp_b2"]], writes=[b2c])
        S.dma("sp", lambda e, ty=ty: e.dma_start(out=b2row[:], in_=io["cmp_b2"].h[ty:ty + 1, :].partition_broadcast(128)), b2row, reads=[io["cmp_b2"]], writes=[b2row])
        for hc in range(2):
            ps = psS.get()
            for l in range(32):
                S.op("pe", lambda e, ps=ps, l=l, hc=hc: e.matmul(ps[:, 0:1], w1v[:, l, hc * 128:(hc + 1) * 128], peT[:, l:l + 1], start=(l == 0), stop=(l == 31)), reads=[qw, peT], writes=[ps])
            S.op("dve", lambda e, ps=ps, hc=hc: e.tensor_tensor(out=cb[:, hc:hc + 1], in0=ps[:, 0:1], in1=b1T[:, hc:hc + 1], op=ALU.add), reads=[ps, b1T], writes=[cb])
        for g in range(4):
            for hf in range(2):
                S.dma("sp", lambda e, ty=ty, g=g, hf=hf: e.dma_start(out=bigK[:, hf * 8192:(hf + 1) * 8192], in_=Kfull.h[ty, g, :, hf * 8192:(hf + 1) * 8192]), bigK, reads=[Kfull], writes=[bigK])
            for nh in range(2):
                for hc in range(2):
                    ps = psS.get()
                    for l in range(32):
                        a0 = nh * 512 * 16 + l
                        S.op("pe", lambda e, ps=ps, l=l, hc=hc, a0=a0: e.matmul(ps[:], w1v[:, l, hc * 128:(hc + 1) * 128], bigK[:, a0:a0 + 512 * 16:16], start=(l == 0), stop=(l == 31)), reads=[qw, bigK], writes=[ps])
                    h = hs.get(); x2 = hx.get()
                    S.op("act", lambda e, ps=ps, h=h, hc=hc: e.activation(out=h[:], in_=ps[:], func=AF.Identity, bias=cb[:, hc:hc + 1], scale=1.0), reads=[ps, cb], writes=[h])
                    S.op("dve", lambda e, h=h, x2=x2: e.tensor_tensor(out=x2[:], in0=h[:], in1=h[:], op=ALU.mult), reads=[h], writes=[x2])
                    S.op("dve", lambda e, x2=x2: e.tensor_scalar(out=x2[:], in0=x2[:], scalar1=0.044715, scalar2=1.0, op0=ALU.mult, op1=ALU.add), reads=[x2], writes=[x2])
                    S.op("dve", lambda e, h=h, x2=x2: e.tensor_tensor(out=x2[:], in0=x2[:], in1=h[:], op=ALU.mult), reads=[h, x2], writes=[x2])
                    S.op("act", lambda e, x2=x2: e.activation(out=x2[:], in_=x2[:], func=AF.Sigmoid, scale=GELU_C), reads=[x2], writes=[x2])
                    S.op("dve", lambda e, h=h, x2=x2, hc=hc, nh=nh: e.tensor_tensor(out=Hg[:, hc, nh * 512:(nh + 1) * 512], in0=x2[:], in1=h[:], op=ALU.mult), reads=[h, x2], writes=[Hg])
            if ty == 0:
                for nh in range(2):
                    ps = psS.get()
                    for hc in range(2):
                        S.op("pe", lambda e, ps=ps, hc=hc, nh=nh: e.matmul(ps[:], w2b[:, hc, :], Hg[:, hc, nh * 512:(nh + 1) * 512], start=(hc == 0), stop=(hc == 1)), reads=[w2b, Hg], writes=[ps])
                    S.op("act", lambda e, ps=ps, g=g, nh=nh: e.activation(out=kcT[:, g, nh * 512:(nh + 1) * 512], in_=ps[:], func=AF.Identity, bias=b2c[:], scale=1.0), reads=[ps, b2c], writes=[kcT])
            else:
                for ncn in range(8):
                    ps = psS.get()
                    for hc in range(2):
                        S.op("pe", lambda e, ps=ps, hc=hc, ncn=ncn: e.matmul(ps[:, 0:128], Hg[:, hc, ncn * 128:(ncn + 1) * 128], w2b[:, hc, :], start=(hc == 0), stop=(hc == 1)), reads=[w2b, Hg], writes=[ps])
                    S.op("dve", lambda e, ps=ps, g=g, ncn=ncn: e.tensor_tensor(out=vca[:, g, ncn, 0:128], in0=ps[:, 0:128], in1=b2row[:], op=ALU.add), reads=[ps, b2row], writes=[vca])

    kws = Pool(S, "kws", 2, [128, 12, 128], BF16)
    vws = Pool(S, "vws", 2, [128, 12, 129], BF16)
    for t in vws.t:
        S.op("dve", lambda e, t=t: e.memset(t[:, :, 128:129], 1.0), writes=[t])
    PTs = Pool(S, "PT", 3, [128, 4, 128], BF16)
    Osb = Pool(S, "Osb", 2, [128, 4, 129], F32)
    oacc = Pool(S, "oacc", 2, [128, 4, 128], F32)
    obf = Pool(S, "obf", 2, [128, 4, 128], BF16)
    aT = Pool(S, "aT", 2, [128, 4, 128], BF16)
    NBT4 = Pool(S, "NBT4", 2, [128, 2, 512], BF16)
    cm = Pool(S, "cm", 2, [128, 128], BF16)
    small = Pool(S, "small", 8, [128, 8], F32)
    impp = Pool(S, "imp", 2, [128, 256], F32)
    vv = Pool(S, "vv", 2, [128, 256], F32)
    ff = Pool(S, "ff", 2, [128, 256], F32)
    wk = Pool(S, "wk", 2, [128, 256], F32)
    nbb = Pool(S, "nbb", 2, [128, 256], BF16)
    m8p = Pool(S, "m8", 4, [128, 8], F32)

    def Oview(r):
        return psO[r // 2][:, (r % 2) * 129:(r % 2) * 129 + 129]

    def finish_branch(b, g, j, oa, first):
        osb = Osb.get()
        for k in range(2):
            S.op("act", lambda e, k=k: e.activation(out=osb[:, 2 * k:2 * k + 2, :], in_=psO[k][:, 0:258].rearrange("p (r c) -> p r c", c=129), func=AF.Copy), reads=[psO[k]], writes=[osb])
        if "dbg" in io:
            S.dma("sp", lambda e: e.dma_start(out=io["dbg"].h[b, j * 128:(j + 1) * 128, :], in_=osb[:].rearrange("p r c -> p (r c)")), osb, reads=[osb], writes=[io["dbg"]])
        sm = small.get()
        S.op("dve", lambda e: e.tensor_scalar(out=sm[:, 0:4], in0=osb[:, :, 128], scalar1=1e-30, scalar2=None, op0=ALU.max), reads=[osb], writes=[sm])
        S.op("dve", lambda e: e.reciprocal(out=sm[:, 0:4], in_=sm[:, 0:4]), reads=[sm], writes=[sm])
        c0 = 4 * g * 3 + b
        S.op("dve", lambda e: e.tensor_tensor(out=sm[:, 4:8], in0=sm[:, 0:4], in1=gn[:, j, c0:c0 + 10:3], op=ALU.mult), reads=[sm, gn], writes=[sm])
        for r in range(4):
            if first:
                S.op("dve", lambda e, r=r: e.tensor_scalar(out=oa[:, r, :], in0=osb[:, r, 0:128], scalar1=sm[:, 4 + r:5 + r], scalar2=None, op0=ALU.mult), reads=[osb, sm], writes=[oa])
            else:
                S.op("dve", lambda e, r=r: e.scalar_tensor_tensor(out=oa[:, r, :], in0=osb[:, r, 0:128], scalar=sm[:, 4 + r:5 + r], in1=oa[:, r, :], op0=ALU.mult, op1=ALU.add), reads=[osb, sm, oa], writes=[oa])
        return sm

    def pv(PT, vaug_t, vaug_ap, first, last):
        for r in range(4):
            S.op("pe", lambda e, r=r: e.matmul(Oview(r), PT[:, r, :], vaug_ap, start=(first and r % 2 == 0), stop=last, skip_group_check=True), reads=[PT, vaug_t], writes=[psO[r // 2]])

    def slot(g, j):
        tsl = slice(j * 128, (j + 1) * 128)
        oa = oacc.get()
        ncn = j // 2 + 1
        for nci in range(ncn):
            ps = psS.get()
            S.op("pe", lambda e, ps=ps, nci=nci, g=g: e.matmul(ps[:].rearrange("p (r t) -> p r t", t=128), kcT[:, g, nci * 128:(nci + 1) * 128], QTv[:, :, tsl], start=True, stop=True), reads=[kcT, qw], writes=[ps])
            PT = PTs.get()
            S.op("act", lambda e, ps=ps, PT=PT: e.activation(out=PT[:], in_=ps[:].rearrange("p (r t) -> p r t", t=128), func=AF.Exp, scale=SCALE), reads=[ps], writes=[PT])
            c = cm.get()
            S.op("dve", lambda e, c=c, nci=nci: e.tensor_scalar(out=c[:], in0=tidxrow[:, tsl], scalar1=nthr[:, nci:nci + 1], scalar2=None, op0=ALU.is_ge), reads=[tidxrow, nthr], writes=[c])
            S.op("dve", lambda e, c=c, PT=PT: e.tensor_tensor(out=PT[:], in0=PT[:], in1=c[:].unsqueeze(1).broadcast_to([128, 4, 128]), op=ALU.mult), reads=[PT, c], writes=[PT])
            pv(PT, vca, vca[:, g, nci, :], nci == 0, nci == ncn - 1)
            for r in range(4):
                S.op("pe", lambda e, r=r, PT=PT, nci=nci: e.matmul(psI[r // 2][:, (r % 2) * 256:(r % 2) * 256 + 256], PT[:, r, :], Ov[:, nci, :], start=(nci == 0 and r % 2 == 0), stop=(nci == ncn - 1), skip_group_check=True), reads=[PT, Ov], writes=[psI[r // 2]])
        sm = finish_branch(0, g, j, oa, True)
        imp = impp.get()
        for r in range(4):
            src = psI[r // 2][:, (r % 2) * 256:(r % 2) * 256 + 256]
            if r == 0:
                S.op("dve", lambda e, src=src: e.tensor_scalar(out=imp[:], in0=src, scalar1=sm[:, 0:1], scalar2=None, op0=ALU.mult), reads=[psI[0], sm], writes=[imp])
            else:
                S.op("dve", lambda e, src=src, r=r: e.scalar_tensor_tensor(out=imp[:], in0=src, scalar=sm[:, r:r + 1], in1=imp[:], op0=ALU.mult, op1=ALU.add), reads=[psI[r // 2], sm, imp], writes=[imp])
        v = vv.get(); f = ff.get(); w = wk.get(); m8a = m8p.get(); m8b = m8p.get(); nb = nbb.get()
        S.op("dve", lambda e: e.tensor_scalar(out=v[:], in0=blkrow[:], scalar1=curcol[:, j:j + 1], scalar2=None, op0=ALU.is_le), reads=[blkrow, curcol], writes=[v])
        S.op("dve", lambda e: e.tensor_tensor(out=imp[:], in0=imp[:], in1=baserow[:], op=ALU.subtract), reads=[imp, baserow], writes=[imp])
        S.op("dve", lambda e: e.tensor_tensor(out=imp[:], in0=imp[:], in1=v[:], op=ALU.mult), reads=[imp, v], writes=[imp])
        S.op("dve", lambda e: e.tensor_tensor(out=imp[:], in0=imp[:], in1=baserow[:], op=ALU.add), reads=[imp, baserow], writes=[imp])
        S.op("dve", lambda e: e.tensor_scalar(out=f[:], in0=blkrow[:], scalar1=curcol[:, j:j + 1], scalar2=1e30, op0=ALU.is_equal, op1=ALU.mult), reads=[blkrow, curcol], writes=[f])
        S.op("dve", lambda e: e.tensor_tensor(out=imp[:], in0=imp[:], in1=f[:], op=ALU.max), reads=[imp, f], writes=[imp])
        S.op("dve", lambda e: e.memset(imp[:, 0:1], 2e30), reads=[imp], writes=[imp])
        S.op("dve", lambda e: e.max(out=m8a[:], in_=imp[:]), reads=[imp], writes=[m8a])
        S.op("dve", lambda e: e.match_replace(out=w[:], in_to_replace=m8a[:], in_values=imp[:], imm_value=-1e30), reads=[imp, m8a], writes=[w])
        S.op("dve", lambda e: e.max(out=m8b[:], in_=w[:]), reads=[w], writes=[m8b])
        S.op("dve", lambda e: e.tensor_scalar(out=f[:], in0=imp[:], scalar1=m8b[:, 7:8], scalar2=None, op0=ALU.is_ge), reads=[imp, m8b], writes=[f])
        S.op("dve", lambda e: e.tensor_tensor(out=f[:], in0=f[:], in1=v[:], op=ALU.mult), reads=[f, v], writes=[f])
        if "dbg2" in io:
            S.dma("sp", lambda e: e.dma_start(out=io["dbg2"].h[j * 128:(j + 1) * 128, :], in_=f[:]), f, reads=[f], writes=[io["dbg2"]])
        S.op("dve", lambda e: e.tensor_scalar(out=nb[:], in0=f[:], scalar1=-1.0, scalar2=30000.0, op0=ALU.add, op1=ALU.mult), reads=[f], writes=[nb])
        for c2 in range(2):
            S.op("pe", lambda e, c2=c2: e.transpose(psT[:, c2 * 128:(c2 + 1) * 128], nb[:, c2 * 128:(c2 + 1) * 128], ident[:]), reads=[nb, ident], writes=[psT])
        nbt = NBT4.get()
        for r in range(4):
            eng = "act" if r % 2 == 0 else "dve"
            if eng == "act":
                S.op("act", lambda e, r=r: e.activation(out=nbt[:, :, r * 128:(r + 1) * 128], in_=psT[:, 0:256].rearrange("p (c t) -> p c t", t=128), func=AF.Copy), reads=[psT], writes=[nbt])
            else:
                S.op("dve", lambda e, r=r: e.tensor_copy(out=nbt[:, :, r * 128:(r + 1) * 128], in_=psT[:, 0:256].rearrange("p (c t) -> p c t", t=128)), reads=[psT], writes=[nbt])
        nkt = 8 * j + 8
        for kt in range(nkt):
            ps = psS.get()
            diag = kt >= 8 * j
            S.op("pe", lambda e, ps=ps, kt=kt: e.matmul(ps[:].rearrange("p (r t) -> p r t", t=128), bigK[:, kt * 128:(kt + 1) * 128], QTv[:, :, tsl], start=True, stop=False), reads=[bigK, qw], writes=[ps])
            S.op("pe", lambda e, ps=ps, kt=kt, diag=diag: e.matmul(ps[:], Eall[:, kt % 64, :], nbt[:, kt // 64, :], start=False, stop=(not diag)), reads=[Eall, nbt], writes=[ps])
            if diag:
                S.op("pe", lambda e, ps=ps, kt=kt: e.matmul(ps[:], ident[:], DB4[:, kt - 8 * j, :], start=False, stop=True), reads=[ident, DB4], writes=[ps])
            PT = PTs.get()
            S.op("act", lambda e, ps=ps, PT=PT: e.activation(out=PT[:], in_=ps[:].rearrange("p (r t) -> p r t", t=128), func=AF.Exp, scale=SCALE), reads=[ps], writes=[PT])
            pv(PT, vsa, vsa[:, kt, :], kt == 0, kt == nkt - 1)
        finish_branch(1, g, j, oa, False)
        kw = kws.get(); vw = vws.get()
        m0 = 4 if j == 0 else 0
        k0 = 8 * j - 4 + m0
        nm = 12 - m0
        S.dma("sp", lambda e, g=g: e.dma_start(out=kw[:, m0:12, :], in_=Kfull.h[3, g, :, k0 * 128:(k0 + nm) * 128].rearrange("d (m k) -> d m k", k=128)), kw, reads=[Kfull], writes=[kw])
        S.dma("sp", lambda e, g=g: e.dma_start(out=vw[:, m0:12, 0:128], in_=Vfull.h[1, k0 * 128:(k0 + nm) * 128, g * 128:(g + 1) * 128].rearrange("(m p) d -> p m d", p=128)), vw, reads=[Vfull], writes=[vw])
        for m in range(m0, 12):
            ps = psS.get()
            S.op("pe", lambda e, ps=ps, m=m: e.matmul(ps[:].rearrange("p (r t) -> p r t", t=128), kw[:, m, :], QTv[:, :, tsl], start=True, stop=False), reads=[kw, qw], writes=[ps])
            S.op("pe", lambda e, ps=ps, m=m: e.matmul(ps[:], ident[:], WB4[:, m, :], start=False, stop=True), reads=[ident, WB4], writes=[ps])
            PT = PTs.get()
            S.op("act", lambda e, ps=ps, PT=PT: e.activation(out=PT[:], in_=ps[:].rearrange("p (r t) -> p r t", t=128), func=AF.Exp, scale=SCALE), reads=[ps], writes=[PT])
            pv(PT, vw, vw[:, m, :], m == m0, m == 11)
        finish_branch(2, g, j, oa, False)
        ob = obf.get()
        S.op("act", lambda e: e.activation(out=ob[:], in_=oa[:], func=AF.Copy), reads=[oa], writes=[ob])
        for r in range(4):
            S.op("pe", lambda e, r=r: e.transpose(psT[:, 256 + r * 128:256 + (r + 1) * 128], ob[:, r, :], ident[:]), reads=[ob, ident], writes=[psT])
        at = aT.get()
        S.op("dve", lambda e: e.tensor_copy(out=at[:], in_=psT[:, 256:768].rearrange("p (r t) -> p r t", t=128)), reads=[psT], writes=[at])
        S.dma("sp", lambda e, g=g: e.dma_start(out=attnT.h[g * 512:(g + 1) * 512, tsl].rearrange("(r d) t -> d r t", d=128), in_=at[:]), at, reads=[at], writes=[attnT])


    for g in groups:
        for hf in range(2):
            S.dma("sp", lambda e, g=g, hf=hf: e.dma_start(out=bigK[:, hf * 8192:(hf + 1) * 8192], in_=Kfull.h[2, g, :, hf * 8192:(hf + 1) * 8192]), bigK, reads=[Kfull], writes=[bigK])
        for q8 in range(8):
            S.dma("sp", lambda e, g=g, q8=q8: e.dma_start(out=vsa[:, q8 * 16:(q8 + 1) * 16, 0:128], in_=Vfull.h[0, q8 * 2048:(q8 + 1) * 2048, g * 128:(g + 1) * 128].rearrange("(kt p) d -> p kt d", p=128)), vsa, reads=[Vfull], writes=[vsa])
        S.dma("sp", lambda e, g=g: e.dma_start(out=QTv[:], in_=qT.h[g * 512:(g + 1) * 512, :].rearrange("(r d) t -> d r t", d=128)), qw, reads=[qT], writes=[qw])
        for j in slots:
            slot(g, j)
    S.final_wait("sp", [attnT])


DFF = 8192


def build_B3(S, io):
    xT_d, bgT, uT, utail, attnT, gaT, gbT, xo = (io[k] for k in ("xT", "bgT", "uT", "utail", "attnT", "gaT", "gbT", "xT_out"))
    ones_b = S.sb("ones_b", [128, 128], BF16)
    S.op("dve", lambda e: e.memset(ones_b[:], 1.0), writes=[ones_b])
    modT = S.sb("modT", [128, 96], F32)
    gT = S.sb("gT", [128, 64], F32)
    S.dma("sp", lambda e: e.dma_start(out=modT[:], in_=io["modT"][:]), modT, reads=[io["modT"]], writes=[modT])
    S.dma("sp", lambda e: e.dma_start(out=gT[:], in_=io["gainsT"][:]), gT, reads=[io["gainsT"]], writes=[gT])
    cw = S.sb("cw", [128, 3, 8], F32)
    for k in range(3):
        S.dma("sp", lambda e, k=k: e.dma_start(out=cw[:, k, :], in_=io["conv_w"].h[k].rearrange("(cc p) -> p cc", p=128), allow_slow_non_contiguous=True), cw, reads=[io["conv_w"]], writes=[cw])
    ag1 = S.sb("ag1", [128, 16], F32); a2 = S.sb("a2", [128, 16], F32); b2 = S.sb("b2", [128, 16], F32); ag3 = S.sb("ag3", [128, 16], F32)
    S.op("dve", lambda e: e.tensor_tensor(out=ag1[:], in0=modT[:, 32:48], in1=gT[:, 16:32], op=ALU.mult), reads=[modT, gT], writes=[ag1])
    S.op("dve", lambda e: e.scalar_tensor_tensor(out=a2[:], in0=modT[:, 64:80], scalar=1.0, in1=gT[:, 32:48], op0=ALU.add, op1=ALU.mult), reads=[modT, gT], writes=[a2])
    S.op("dve", lambda e: e.tensor_copy(out=b2[:], in_=modT[:, 48:64]), reads=[modT], writes=[b2])
    S.op("dve", lambda e: e.tensor_tensor(out=ag3[:], in0=modT[:, 80:96], in1=gT[:, 48:64], op=ALU.mult), reads=[modT, gT], writes=[ag3])

    xt = S.sb("xt", [128, KC, NT], F32)
    ysb = S.sb("ysb", [128, KC, NT], F32)
    hid = S.sb("hid", [128, 64, NT], BF16)
    mg = S.sb("mg", [128, KC, NT], BF16)
    wts = Pool(S, "wt", 2, [128, KC, 512], BF16)
    P = {"ps": Pool(S, "ps", 8, [128, 512], F32, psum=True), "sq": Pool(S, "sq", 2, [128, NT], BF16),
         "rstd": Pool(S, "rstd", 1, [128, NT], F32), "tmp": Pool(S, "tmp", 3, [128, NT], F32)}
    P["eps"] = S.sb("epsc", [128, 1], F32)
    S.op("dve", lambda e: e.memset(P["eps"][:], EPS), writes=[P["eps"]])
    uext = Pool(S, "uext", 2, [128, 4, 130], F32)
    bgc = Pool(S, "bgc", 2, [128, NT], F32)
    zt = Pool(S, "zt", 2, [128, 4, 128], F32)
    gch = Pool(S, "gch", 4, [128, NT], F32)

    def load_w(wd, r0, nk, c0):
        wt = wts.get()
        src = wd.h[r0:r0 + nk * 128, c0:c0 + 512].rearrange("(kc p) n -> p kc n", p=128)
        hk = nk // 2
        for half in range(2):
            S.dma("pool", lambda e, half=half: e.dma_start(out=wt[:, half * hk:(half + 1) * hk, :], in_=src[:, half * hk:(half + 1) * hk, :]), wt, reads=[wd], writes=[wt])
        return wt

    def post_norm_residual(coef):
        ss = P["ps"].get()
        for kc in range(KC):
            s = P["sq"].get()
            S.op("act", lambda e, s=s, kc=kc: e.activation(out=s[:], in_=ysb[:, kc, :], func=AF.Square), reads=[ysb], writes=[s])
            S.op("pe", lambda e, s=s, kc=kc: e.matmul(ss[:], ones_b[:], s[:], start=(kc == 0), stop=(kc == KC - 1)), reads=[s, ones_b], writes=[ss])
        r = P["rstd"].get()
        S.op("act", lambda e: e.activation(out=r[:], in_=ss[:], func=AF.Sqrt, bias=P["eps"][:], scale=1.0 / D), reads=[ss, P["eps"]], writes=[r])
        S.op("dve", lambda e: e.reciprocal(out=r[:], in_=r[:]), reads=[r], writes=[r])
        for kc in range(KC):
            t = P["tmp"].get()
            S.op("dve", lambda e, t=t, kc=kc: e.scalar_tensor_tensor(out=t[:], in0=ysb[:, kc, :], scalar=coef[:, kc:kc + 1], in1=r[:], op0=ALU.mult, op1=ALU.mult), reads=[ysb, coef, r], writes=[t])
            S.op("pool", lambda e, t=t, kc=kc: e.tensor_tensor(out=xt[:, kc, :], in0=xt[:, kc, :], in1=t[:], op=ALU.add), reads=[xt, t], writes=[xt])

    def tile(ti):
        tsl = slice(ti * NT, (ti + 1) * NT)
        load_x_tile(S, xT_d, xt, ti)
        for q4 in range(4):
            S.dma("sp", lambda e, q4=q4: e.dma_start(out=hid[:, q4 * 4:(q4 + 1) * 4, :], in_=attnT.h[q4 * 512:(q4 + 1) * 512, tsl].rearrange("(hc p) t -> p hc t", p=128)), hid, reads=[attnT], writes=[hid])
        for cc in range(8):
            ue = uext.get(); bg = bgc.get(); z = zt.get()
            S.dma("sp", lambda e, ue=ue, cc=cc: e.dma_start(out=ue[:, :, 2:130], in_=uT.h[cc * 128:(cc + 1) * 128, tsl].rearrange("p (b t) -> p b t", t=128)), ue, reads=[uT], writes=[ue])
            S.dma("sp", lambda e, ue=ue, cc=cc: e.dma_start(out=ue[:, :, 0:2], in_=utail.h[cc * 128:(cc + 1) * 128, ti * 4:(ti + 1) * 4, :]), ue, reads=[utail], writes=[ue])
            S.dma("sp", lambda e, bg=bg, cc=cc: e.dma_start(out=bg[:], in_=bgT.h[cc * 128:(cc + 1) * 128, tsl]), bg, reads=[bgT], writes=[bg])
            S.op("dve", lambda e, ue=ue, z=z, cc=cc: e.tensor_scalar(out=z[:], in0=ue[:, :, 2:130], scalar1=cw[:, 2, cc:cc + 1], scalar2=None, op0=ALU.mult), reads=[ue, cw], writes=[z])
            S.op("dve", lambda e, ue=ue, z=z, cc=cc: e.scalar_tensor_tensor(out=z[:], in0=ue[:, :, 1:129], scalar=cw[:, 1, cc:cc + 1], in1=z[:], op0=ALU.mult, op1=ALU.add), reads=[ue, cw, z], writes=[z])
            S.op("dve", lambda e, ue=ue, z=z, cc=cc: e.scalar_tensor_tensor(out=z[:], in0=ue[:, :, 0:128], scalar=cw[:, 0, cc:cc + 1], in1=z[:], op0=ALU.mult, op1=ALU.add), reads=[ue, cw, z], writes=[z])
            S.op("dve", lambda e, bg=bg, z=z, cc=cc: e.tensor_tensor(out=hid[:, 16 + cc, :], in0=z[:].rearrange("p b t -> p (b t)"), in1=bg[:], op=ALU.mult), reads=[z, bg], writes=[hid])
        for og in range(4):
            wa = load_w(io["w_conv_out"], 0, 8, og * 512)
            wb = load_w(io["w_nsa_out"], 0, 16, og * 512)
            for c4 in range(4):
                oc = og * 4 + c4
                pa = P["ps"].get(); pb = P["ps"].get()
                for cc in range(8):
                    S.op("pe", lambda e, cc=cc, pa=pa, c4=c4, wa=wa: e.matmul(pa[:], wa[:, cc, c4 * 128:(c4 + 1) * 128], hid[:, 16 + cc, :], start=(cc == 0), stop=(cc == 7)), reads=[wa, hid], writes=[pa])
                for hc in range(16):
                    S.op("pe", lambda e, hc=hc, pb=pb, c4=c4, wb=wb: e.matmul(pb[:], wb[:, hc, c4 * 128:(c4 + 1) * 128], hid[:, hc, :], start=(hc == 0), stop=(hc == 15)), reads=[wb, hid], writes=[pb])
                ga = gch.get(); gb = gch.get()
                S.dma("sp", lambda e, ga=ga, oc=oc: e.dma_start(out=ga[:], in_=gaT.h[oc * 128:(oc + 1) * 128, tsl]), ga, reads=[gaT], writes=[ga])
                S.dma("sp", lambda e, gb=gb, oc=oc: e.dma_start(out=gb[:], in_=gbT.h[oc * 128:(oc + 1) * 128, tsl]), gb, reads=[gbT], writes=[gb])
                S.op("dve", lambda e, ga=ga, pa=pa: e.tensor_tensor(out=ga[:], in0=pa[:], in1=ga[:], op=ALU.mult), reads=[pa, ga], writes=[ga])
                S.op("dve", lambda e, gb=gb, pb=pb: e.tensor_tensor(out=gb[:], in0=pb[:], in1=gb[:], op=ALU.mult), reads=[pb, gb], writes=[gb])
                S.op("pool", lambda e, ga=ga, gb=gb, oc=oc: e.tensor_tensor(out=mg[:, oc, :], in0=ga[:], in1=gb[:], op=ALU.add), reads=[ga, gb], writes=[mg])
        for og in range(4):
            wo = load_w(io["w_out"], 0, 16, og * 512)
            for c4 in range(4):
                ps = P["ps"].get()
                for kc in range(KC):
                    S.op("pe", lambda e, kc=kc, ps=ps, c4=c4, wo=wo: e.matmul(ps[:], wo[:, kc, c4 * 128:(c4 + 1) * 128], mg[:, kc, :], start=(kc == 0), stop=(kc == KC - 1)), reads=[wo, mg], writes=[ps])
                S.op("act", lambda e, ps=ps, og=og, c4=c4: e.activation(out=ysb[:, og * 4 + c4, :], in_=ps[:], func=AF.Copy), reads=[ps], writes=[ysb])
        post_norm_residual(ag1)
        if "xmid" in io:
            for q4 in range(4):
                S.dma("sp", lambda e, q4=q4: e.dma_start(out=io["xmid"].h.rearrange("(kc p) t -> p kc t", p=128)[:, q4 * 4:(q4 + 1) * 4, tsl], in_=xt[:, q4 * 4:(q4 + 1) * 4, :]), xt, reads=[xt], writes=[io["xmid"]])
        rms_affine(S, xt, mg, 0, a2, b2, ones_b, P)
        for fg in range(16):
            wu = load_w(io["w_mlp_up"], 0, 16, fg * 512)
            for c4 in range(4):
                ps = P["ps"].get()
                for kc in range(KC):
                    S.op("pe", lambda e, kc=kc, ps=ps, c4=c4, wu=wu: e.matmul(ps[:], wu[:, kc, c4 * 128:(c4 + 1) * 128], mg[:, kc, :], start=(kc == 0), stop=(kc == KC - 1)), reads=[wu, mg], writes=[ps])
                t = P["tmp"].get()
                S.op("act", lambda e, ps=ps, t=t: e.activation(out=t[:], in_=ps[:], func=AF.Relu), reads=[ps], writes=[t])
                S.op("dve", lambda e, t=t, fg=fg, c4=c4: e.tensor_tensor(out=hid[:, fg * 4 + c4, :], in0=t[:], in1=t[:], op=ALU.mult), reads=[t], writes=[hid])
        for og in range(4):
            pss = [P["ps"].get() for _ in range(4)]
            for slab in range(4):
                wd = load_w(io["w_mlp_down"], slab * 2048, 16, og * 512)
                for c4 in range(4):
                    for fcl in range(16):
                        S.op("pe", lambda e, fcl=fcl, c4=c4, wd=wd, slab=slab, pss=pss: e.matmul(pss[c4][:], wd[:, fcl, c4 * 128:(c4 + 1) * 128], hid[:, slab * 16 + fcl, :], start=(slab == 0 and fcl == 0), stop=(slab == 3 and fcl == 15)), reads=[wd, hid], writes=[pss[c4]])
            for c4 in range(4):
                S.op("act", lambda e, c4=c4, og=og, pss=pss: e.activation(out=ysb[:, og * 4 + c4, :], in_=pss[c4][:], func=AF.Copy), reads=[pss[c4]], writes=[ysb])
        post_norm_residual(ag3)
        for q4 in range(4):
            S.dma("sp", lambda e, q4=q4: e.dma_start(out=xo.h.rearrange("(kc p) t -> p kc t", p=128)[:, q4 * 4:(q4 + 1) * 4, tsl], in_=xt[:, q4 * 4:(q4 + 1) * 4, :]), xt, reads=[xt], writes=[xo])

    for ti in range(NTILE):
        tile(ti)
    S.final_wait("sp", [xo] + ([io["xmid"]] if "xmid" in io else []))


MCOLS = 6144


def build_M(S, io):
    cT = S.sb("cT", [128, 16], F32)
    S.dma("sp", lambda e: e.dma_start(out=cT[:], in_=io["cT"][:]), cT, reads=[io["cT"]], writes=[cT])
    S.op("act", lambda e: e.activation(out=cT[:], in_=cT[:], func=AF.Silu), reads=[cT], writes=[cT])
    brow = S.sb("brow", [1, MCOLS], F32)
    S.dma("sp", lambda e: e.dma_start(out=brow[:], in_=io["ada_b"][:]), brow, reads=[io["ada_b"]], writes=[brow])
    orow = S.sb("orow", [1, MCOLS], F32)
    wts = Pool(S, "wm", 2, [128, 16, 512], F32)
    pss = Pool(S, "psm", 2, [128, 512], F32, psum=True)
    wsrc = io["ada_w"].h.rearrange("(kc p) n -> p kc n", p=128)
    for gi in range(MCOLS // 512):
        wt = wts.get()
        for half in range(2):
            S.dma("sp", lambda e, wt=wt, half=half, gi=gi: e.dma_start(out=wt[:, half * 8:(half + 1) * 8, :], in_=wsrc[:, half * 8:(half + 1) * 8, gi * 512:(gi + 1) * 512]), wt, reads=[io["ada_w"]], writes=[wt])
        ps = pss.get()
        for kc in range(16):
            S.op("pe", lambda e, wt=wt, ps=ps, kc=kc: e.matmul(ps[0:1, :], cT[:, kc:kc + 1], wt[:, kc, :], start=(kc == 0), stop=(kc == 15)), reads=[cT, wt], writes=[ps])
        S.op("dve", lambda e, ps=ps, gi=gi: e.tensor_tensor(out=orow[:, gi * 512:(gi + 1) * 512], in0=ps[0:1, :], in1=brow[:, gi * 512:(gi + 1) * 512], op=ALU.add), reads=[ps, brow], writes=[orow])
    S.dma("sp", lambda e: e.dma_start(out=io["mod"][:], in_=orow[:]), orow, reads=[orow], writes=[io["mod"]])
    S.final_wait("sp", [io["mod"]])

bf = ml_dtypes.bfloat16
SEQ = 16384; TC = 2048

def tok_idx(c):
    return np.concatenate([np.arange((8 * j + c) * 128, (8 * j + c + 1) * 128) for j in range(16)])

def consts_common():
    n = np.arange(1024)[:, None]; jb = np.arange(256)[None, :]
    ov = np.clip(np.minimum(16 * n + 32, 64 * jb + 64) - np.maximum(16 * n, 64 * jb), 0, None).astype(np.float32) / 32
    ov[1023:] = 0
    Ov = np.ascontiguousarray(ov.reshape(8, 128, 256).transpose(1, 0, 2)).astype(bf)
    Eall = np.zeros((128, 64, 128), np.float32)
    for i in range(64):
        Eall[2 * i, i, 0:64] = 1; Eall[2 * i + 1, i, 64:128] = 1
    p = np.arange(128)[:, None]
    nthr = (16 * (np.arange(8)[None, :] * 128 + p) + 31).astype(np.float32)
    blkrow = np.broadcast_to(np.arange(256, dtype=np.float32)[None, :], (128, 256)).copy()
    baserow = -(blkrow + 2)
    invf = (500000.0 ** (-np.arange(0, 32, 2, dtype=np.float32) / 32)).astype(np.float32)
    invf2 = np.zeros((32, 2), np.float32); invf2[:, 0] = np.tile(invf, 2); invf2[:16, 1] = -1; invf2[16:, 1] = 1
    return {"Ov": Ov, "Eall": Eall.astype(bf), "nthr": nthr, "blkrow": blkrow, "baserow": baserow, "invf": invf2}

def consts_core(c):
    i = np.arange(128)[:, None]; ip = np.arange(128)[None, :]
    NEG = -30000.0
    causal = np.where(i <= ip, 0.0, NEG).astype(np.float32)
    band = np.where(i > ip, 0.0, NEG).astype(np.float32)
    full = np.zeros((128, 128), np.float32); none = np.full((128, 128), NEG, np.float32)
    DB = np.zeros((128, 8, 4, 128), np.float32)
    for kk in range(8):
        DB[:, kk] = (causal if kk == c else full)[:, None, :]
    WB = np.zeros((128, 12, 4, 128), np.float32)
    for m in range(12):
        d = m - 4 - c
        t = none if d < -4 else band if d == -4 else full if d < 0 else causal if d == 0 else none
        WB[:, m] = t[:, None, :]
    ti = tok_idx(c)
    tidxrow = np.broadcast_to(ti.astype(np.float32)[None, :], (128, TC)).copy()
    curcol = (ti.reshape(16, 128).T // 64).astype(np.float32)
    return {"DB4": DB.reshape(128, 8, 512).astype(bf), "WB4": WB.reshape(128, 12, 512).astype(bf), "tidxrow": tidxrow, "curcol": np.ascontiguousarray(curcol)}

def gelu_tanh(x):
    return 0.5 * x * (1 + np.tanh(np.sqrt(2 / np.pi) * (x + 0.044715 * x ** 3)))

def ref_compress(rawT, pe, w1, b1, w2, b2):
    kv = rawT.T
    idx = np.arange(1023)[:, None] * 16 + np.arange(32)[None, :]
    blocks = (kv[idx] + pe[None]).reshape(1023, 4096)
    h = gelu_tanh(blocks @ w1 + b1)
    return h @ w2 + b2

def ref_attn_block(gb, q, kc, vc, ks, vs, kw, vw, gate):
    scale = 128 ** -0.5
    t = gb * 128 + np.arange(128)
    sc = np.einsum('trd,nd->rtn', q, kc) * scale
    cmp_end = np.arange(1023) * 16 + 31
    m_c = cmp_end[None, :] <= t[:, None]
    scm = np.where(m_c[None], sc, -1e30)
    e = np.exp(scm - scm.max(-1, keepdims=True)); p = e / e.sum(-1, keepdims=True)
    p_c = np.where(m_c[None], p, 0.0)
    o_c = np.einsum('rtn,nd->trd', p_c, vc)
    n = np.arange(1023)[:, None]; jb = np.arange(256)[None, :]
    ov = np.clip(np.minimum(16 * n + 32, 64 * jb + 64) - np.maximum(16 * n, 64 * jb), 0, None) / 32.0
    imp = np.einsum('rtn,nj->tj', p_c, ov)
    cur = t // 64
    blk = np.arange(256)
    forced = (blk[None, :] == cur[:, None]) | (blk[None, :] == 0)
    valid = blk[None, :] <= cur[:, None]
    imp = np.where(forced, 1e30, np.where(valid, imp, -1e30))
    sel = np.argsort(-imp, axis=-1, kind='stable')[:, :16]
    o_s = np.zeros((128, 4, 128));
    for i in range(128):
        kpos = (sel[i][:, None] * 64 + np.arange(64)[None, :]).reshape(-1)
        msk = kpos <= t[i]
        s = (q[i] @ ks[kpos].T) * scale
        s = np.where(msk[None], s, -1e30)
        e = np.exp(s - s.max(-1, keepdims=True)); pp = e / e.sum(-1, keepdims=True)
        o_s[i] = pp @ vs[kpos]
    o_w = np.zeros((128, 4, 128))
    for i in range(128):
        lo = max(0, t[i] - 511)
        s = (q[i] @ kw[lo:t[i] + 1].T) * scale
        e = np.exp(s - s.max(-1, keepdims=True)); pp = e / e.sum(-1, keepdims=True)
        o_w[i] = pp @ vw[lo:t[i] + 1]
    return gate[..., 0:1] * o_c + gate[..., 1:2] * o_s + gate[..., 2:3] * o_w, (o_c, o_s, o_w, sel)

_PROGS = {}
NCORES = 8


def _prog(name):
    if name in _PROGS:
        return _PROGS[name]
    nc = bass.Bass("TRN2", target_bir_lowering=False)
    with ExitStack() as es:
        S = Sched(nc, es)
        io = {}

        def din(n, shape, dt):
            io[n] = S.dram(n, shape, dt, kind="ExternalInput")

        def dout(n, shape, dt):
            io[n] = S.dram(n, shape, dt, kind="ExternalOutput")
        if name == "M":
            din("cT", [128, 16], F32); din("ada_w", [2048, MCOLS], F32); din("ada_b", [1, MCOLS], F32)
            dout("mod", [1, MCOLS], F32)
            build_M(S, io)
        elif name == "A":
            din("xT", [D, TC], F32); din("modT", [128, 96], F32); din("gainsT", [128, 64], F32)
            din("pos", [32, TC], I32); din("invf", [32, 2], F32); din("w_in", [1, D, INW], F32)
            dout("bgT", [1024, TC], F32); dout("uT", [1024, TC], F32); dout("qT", [2048, TC], BF16)
            dout("kT", [4, 4, 128, TC], BF16); dout("vtok", [2, TC, 512], BF16); dout("gn", [TC, 48], F32)
            dout("gaT", [2048, TC], F32); dout("gbT", [2048, TC], F32)
            build_A(S, io, 0)
        elif name == "B2":
            din("Kfull", [4, 4, 128, SEQ], BF16); din("Vfull", [2, SEQ, 512], BF16); din("qT", [2048, TC], BF16); din("gn", [TC, 48], F32)
            din("cmp_pe", [2, 32, 128], F32); din("cmp_w1", [2, 4096, 256], F32); din("cmp_b1", [2, 256], F32)
            din("cmp_w2", [2, 256, 128], F32); din("cmp_b2", [2, 128], F32)
            din("Ov", [128, 8, 256], BF16); din("Eall", [128, 64, 128], BF16); din("DB4", [128, 8, 512], BF16); din("WB4", [128, 12, 512], BF16)
            din("tidxrow", [128, TC], F32); din("curcol", [128, 16], F32); din("nthr", [128, 8], F32)
            din("blkrow", [128, 256], F32); din("baserow", [128, 256], F32)
            dout("attnT", [2048, TC], BF16)
            build_B2(S, io)
        elif name == "B3":
            din("xT", [D, TC], F32); din("bgT", [1024, TC], F32); din("uT", [1024, TC], F32); din("utail", [1024, 16, 2], F32)
            din("attnT", [2048, TC], BF16); din("gaT", [2048, TC], F32); din("gbT", [2048, TC], F32)
            din("modT", [128, 96], F32); din("gainsT", [128, 64], F32); din("conv_w", [3, 1024], F32)
            din("w_conv_out", [1024, 2048], F32); din("w_nsa_out", [2048, 2048], F32); din("w_out", [2048, 2048], F32)
            din("w_mlp_up", [2048, DFF], F32); din("w_mlp_down", [DFF, 2048], F32)
            dout("xT_out", [D, TC], F32)
            build_B3(S, io)
        S.emit()
    _PROGS[name] = nc
    return nc


def _run(name, in_maps):
    nc = _prog(name)
    res = run_bass_kernel_spmd(nc, in_maps, core_ids=list(range(NCORES)))
    return res.results


def kernel(x, c, positions, ada_w, ada_b, norm_gains, w_in, conv_w, w_conv_out, cmp_pe, cmp_w1, cmp_b1,
           cmp_w2, cmp_b2, w_nsa_out, w_out, w_mlp_up, w_mlp_down):
    f32 = lambda a: np.ascontiguousarray(np.asarray(a), dtype=np.float32)
    x = f32(x); c = f32(c); positions = np.asarray(positions).astype(np.int32)
    ada_w = np.asarray(ada_w); ada_b = np.asarray(ada_b); norm_gains = f32(norm_gains)
    DEPTH = ada_w.shape[0]
    toks = [tok_idx(cc) for cc in range(NCORES)]
    cc_ = consts_common()
    ccore = [consts_core(cc) for cc in range(NCORES)]
    cT = np.ascontiguousarray(c[0].reshape(16, 128).T)
    in_maps = []
    per_layer = 12288 // MCOLS
    for cc in range(NCORES):
        l, h = cc // per_layer, cc % per_layer
        in_maps.append({"cT": cT, "ada_w": f32(ada_w[l][:, h * MCOLS:(h + 1) * MCOLS]), "ada_b": f32(ada_b[l][None, h * MCOLS:(h + 1) * MCOLS])})
    resM = _run("M", in_maps)
    mods = [np.concatenate([resM[l * per_layer + h]["mod"][0] for h in range(per_layer)]) for l in range(DEPTH)]
    xT = [np.ascontiguousarray(x[0][toks[cc]].T) for cc in range(NCORES)]
    pos = [np.ascontiguousarray(np.broadcast_to(positions[0][toks[cc]][None, :], (32, TC))).astype(np.int32) for cc in range(NCORES)]
    for l in range(DEPTH):
        modT = np.ascontiguousarray(mods[l].reshape(96, 128).T)
        gainsT = np.ascontiguousarray(norm_gains[l].reshape(64, 128).T)
        w_in_l = f32(np.asarray(w_in)[l:l + 1])
        resA = _run("A", [{"xT": xT[cc], "modT": modT, "gainsT": gainsT, "pos": pos[cc], "invf": cc_["invf"], "w_in": w_in_l} for cc in range(NCORES)])
        Kfull = np.zeros((4, 4, 128, SEQ), bf); Vfull = np.zeros((2, SEQ, 512), bf)
        for cc in range(NCORES):
            Kfull[:, :, :, toks[cc]] = resA[cc]["kT"]
            Vfull[:, toks[cc], :] = resA[cc]["vtok"]
        utails = []
        for cc in range(NCORES):
            ut = np.zeros((1024, 16, 2), np.float32)
            for j in range(16):
                gb = 8 * j + cc
                if gb == 0:
                    continue
                pc, pj = (gb - 1) % 8, (gb - 1) // 8
                ut[:, j, :] = resA[pc]["uT"][:, pj * 128 + 126:pj * 128 + 128]
            utails.append(ut)
        cmpw = {k: f32(np.asarray(v)[l]) for k, v in (("cmp_pe", cmp_pe), ("cmp_w1", cmp_w1), ("cmp_b1", cmp_b1), ("cmp_w2", cmp_w2), ("cmp_b2", cmp_b2))}
        in_maps = []
        for cc in range(NCORES):
            m = {"Kfull": Kfull, "Vfull": Vfull, "qT": resA[cc]["qT"], "gn": resA[cc]["gn"]}
            m.update(cmpw)
            for k in ("Ov", "Eall", "nthr", "blkrow", "baserow"):
                m[k] = cc_[k]
            m.update(ccore[cc])
            in_maps.append(m)
        resB2 = _run("B2", in_maps)
        wl = {k: f32(np.asarray(v)[l]) for k, v in (("conv_w", conv_w), ("w_conv_out", w_conv_out), ("w_nsa_out", w_nsa_out), ("w_out", w_out), ("w_mlp_up", w_mlp_up), ("w_mlp_down", w_mlp_down))}
        in_maps = []
        for cc in range(NCORES):
            m = {"xT": xT[cc], "bgT": resA[cc]["bgT"], "uT": resA[cc]["uT"], "utail": utails[cc], "attnT": resB2[cc]["attnT"],
                 "gaT": resA[cc]["gaT"], "gbT": resA[cc]["gbT"], "modT": modT, "gainsT": gainsT}
            m.update(wl)
            in_maps.append(m)
        resB3 = _run("B3", in_maps)
        xT = [resB3[cc]["xT_out"] for cc in range(NCORES)]
    out = np.zeros((SEQ, D), np.float32)
    for cc in range(NCORES):
        out[toks[cc]] = xT[cc].T
    return out.reshape(1, SEQ, D)
```

```python
import math
from contextlib import ExitStack
import numpy as np
import ml_dtypes
import concourse.bass as bass
import concourse.mybir as mybir
from concourse.bass_utils import run_bass_kernel_spmd


F32 = mybir.dt.float32
BF16 = mybir.dt.bfloat16
I32 = mybir.dt.int32
AF = mybir.ActivationFunctionType
ALU = mybir.AluOpType
AX = mybir.AxisListType


class T:
    def __init__(self, name, h):
        self.name = name
        self.h = h
        self.w = None
        self.r = []
        self.sem = None
        self.dcnt = 0
        self.multi = False
        self.ws = []

    def __getitem__(self, k):
        return self.h[k]


class Sched:
    ENG = ("pe", "act", "dve", "pool", "sp")

    def __init__(self, nc, es, same_engine_sync=True):
        self.nc = nc
        self.es = es
        self.ops = {e: [] for e in self.ENG}
        self.cnt = {e: 0 for e in self.ENG}
        self.esem = {e: es.enter_context(nc.semaphore("sem_" + e)) for e in self.ENG}
        self.waited = {e: {} for e in self.ENG}
        self.same = same_engine_sync
        self.nsem = 5
        self.ninst = 0

    def sb(self, name, shape, dt):
        return T(name, self.es.enter_context(self.nc.sbuf_tensor("sb_" + name, list(shape), dt)))

    def ps(self, name, shape, dt=F32):
        return T(name, self.es.enter_context(self.nc.psum_tensor("ps_" + name, list(shape), dt)))

    def dram(self, name, shape, dt, kind="Internal"):
        t = T(name, self.nc.dram_tensor(name, list(shape), dt, kind=kind).ap())
        t.multi = True
        return t

    def _deps(self, eng, reads, writes):
        deps = []
        for t in reads:
            if t.w is not None:
                deps.append(t.w)
            deps.extend(t.ws)
        for t in writes:
            if not t.multi:
                if t.w is not None:
                    deps.append(t.w)
            deps.extend(t.r)
        waits = []
        wd = self.waited[eng]
        own = self.esem[eng]
        best = {}
        for (sem, val) in deps:
            if sem is own and (not self.same or eng == "pe"):
                continue
            if wd.get(id(sem), 0) >= val:
                continue
            if id(sem) not in best or best[id(sem)][1] < val:
                best[id(sem)] = (sem, val)
        for k, (sem, val) in best.items():
            wd[k] = val
            waits.append((sem, val))
        return waits

    def _stamp(self, stamp, reads, writes):
        for t in writes:
            if t.multi:
                t.ws.append(stamp)
                if len(t.ws) > 12:
                    t.ws = self._compact(t.ws)
                continue
            t.w = stamp
            t.r = []
        for t in reads:
            if t in writes:
                continue
            t.r.append(stamp)
            if len(t.r) > 12:
                t.r = self._compact(t.r)

    @staticmethod
    def _compact(lst):
        m = {}
        for (s, v) in lst:
            if id(s) not in m or m[id(s)][1] < v:
                m[id(s)] = (s, v)
        return list(m.values())

    def op(self, eng, fn, reads=(), writes=()):
        waits = self._deps(eng, reads, writes)
        self.cnt[eng] += 1
        stamp = (self.esem[eng], self.cnt[eng])
        self.ops[eng].append((fn, waits, (self.esem[eng], 1)))
        self._stamp(stamp, reads, writes)
        self.ninst += 1

    def dma(self, q, fn, semt, reads=(), writes=()):
        waits = self._deps(q, reads, writes)
        if semt.sem is None:
            semt.sem = self.es.enter_context(self.nc.semaphore("dsem_" + semt.name))
            self.nsem += 1
        semt.dcnt += 16
        stamp = (semt.sem, semt.dcnt)
        self.ops[q].append((fn, waits, (semt.sem, 16)))
        self._stamp(stamp, reads, writes)
        self.ninst += 1

    def final_wait(self, eng, tiles):
        deps = []
        for t in tiles:
            if t.w is not None:
                deps.append(t.w)
            deps.extend(t.ws)
        self.ops[eng].append((None, self._compact(deps), None))

    def emit(self):
        nc = self.nc
        emap = {"pe": "tensor", "act": "scalar", "dve": "vector", "pool": "gpsimd", "sp": "sync"}
        with nc.Block() as block:
            for e in self.ENG:
                ops = self.ops[e]

                def body(engine, ops=ops):
                    for (fn, waits, inc) in ops:
                        for (sem, val) in waits:
                            engine.wait_ge(sem, val)
                        if fn is not None:
                            ins = fn(engine)
                            ins.then_inc(inc[0], inc[1])
                getattr(block, emap[e])(body)


D = 2048; TC = 2048; NT = 512; NTILE = 4; KC = 16
OFF_BG, OFF_CG, OFF_XA, OFF_Q, OFF_KV, OFF_GN, OFF_GA, OFF_GB = 0, 1024, 2048, 3072, 5120, 8192, 8240, 10288
INW = 12336
EPS = 1e-6
TWO_PI = 2.0 * math.pi


class Pool:
    def __init__(self, S, name, n, shape, dt, psum=False):
        self.t = [(S.ps if psum else S.sb)("%s%d" % (name, i), shape, dt) for i in range(n)]
        self.i = 0

    def get(self):
        t = self.t[self.i % len(self.t)]
        self.i += 1
        return t


def load_x_tile(S, xT_d, xt, ti):
    src = xT_d.h.rearrange("(kc p) t -> p kc t", p=128)
    for q4 in range(4):
        S.dma("sp", lambda e, q4=q4: e.dma_start(out=xt[:, q4 * 4:(q4 + 1) * 4, :], in_=src[:, q4 * 4:(q4 + 1) * 4, ti * NT:(ti + 1) * NT]),
              xt, reads=[xT_d], writes=[xt])


def rms_affine(S, xt, hT, tcol0, a_t, b_t, ones_b, P):
    ss = P["ps"].get()
    for kc in range(KC):
        s = P["sq"].get()
        S.op("act", lambda e, s=s, kc=kc: e.activation(out=s[:], in_=xt[:, kc, :], func=AF.Square), reads=[xt], writes=[s])
        S.op("pe", lambda e, s=s, kc=kc: e.matmul(ss[:], ones_b[:], s[:], start=(kc == 0), stop=(kc == KC - 1)), reads=[s, ones_b], writes=[ss])
    r = P["rstd"].get()
    S.op("act", lambda e: e.activation(out=r[:], in_=ss[:], func=AF.Sqrt, bias=P["eps"][:], scale=1.0 / D), reads=[ss, P["eps"]], writes=[r])
    S.op("dve", lambda e: e.reciprocal(out=r[:], in_=r[:]), reads=[r], writes=[r])
    for kc in range(KC):
        t = P["tmp"].get()
        S.op("dve", lambda e, t=t, kc=kc: e.scalar_tensor_tensor(out=t[:], in0=xt[:, kc, :], scalar=a_t[:, kc:kc + 1], in1=r[:], op0=ALU.mult, op1=ALU.mult), reads=[xt, a_t, r], writes=[t])
        S.op("act", lambda e, t=t, kc=kc: e.activation(out=hT[:, kc, tcol0:tcol0 + NT], in_=t[:], func=AF.Identity, bias=b_t[:, kc:kc + 1], scale=1.0), reads=[t, b_t], writes=[hT])


def build_A(S, io, L):
    nc = S.nc
    xT_d = io["xT"]
    ones_b = S.sb("ones_b", [128, 128], BF16)
    S.op("dve", lambda e: e.memset(ones_b[:], 1.0), writes=[ones_b])
    modT = S.sb("modT", [128, 96], F32)
    gT = S.sb("gT", [128, 64], F32)
    S.dma("sp", lambda e: e.dma_start(out=modT[:], in_=io["modT"][:]), modT, reads=[io["modT"]], writes=[modT])
    S.dma("sp", lambda e: e.dma_start(out=gT[:], in_=io["gainsT"][:]), gT, reads=[io["gainsT"]], writes=[gT])
    a1 = S.sb("a1", [128, 16], F32)
    S.op("dve", lambda e: e.scalar_tensor_tensor(out=a1[:], in0=modT[:, 16:32], scalar=1.0, in1=gT[:, 0:16], op0=ALU.add, op1=ALU.mult), reads=[modT, gT], writes=[a1])
    b1 = S.sb("b1", [128, 16], F32)
    S.op("dve", lambda e: e.tensor_copy(out=b1[:], in_=modT[:, 0:16]), reads=[modT], writes=[b1])
    posi = S.sb("posi", [32, TC], I32)
    S.dma("sp", lambda e: e.dma_start(out=posi[:], in_=io["pos"][:]), posi, reads=[io["pos"]], writes=[posi])
    invf = S.sb("invf", [32, 2], F32)
    S.dma("sp", lambda e: e.dma_start(out=invf[:], in_=io["invf"][:]), invf, reads=[io["invf"]], writes=[invf])
    posf = S.sb("posf", [32, TC], F32)
    S.op("dve", lambda e: e.tensor_copy(out=posf[:], in_=posi[:]), reads=[posi], writes=[posf])
    ang = S.sb("ang", [32, TC], F32)
    S.op("dve", lambda e: e.tensor_scalar(out=ang[:], in0=posf[:], scalar1=invf[:, 0:1], scalar2=None, op0=ALU.mult), reads=[posf, invf], writes=[ang])
    Ct = S.sb("Ct", [32, TC], F32)
    Sn = S.sb("Snt", [32, TC], F32)
    C1 = 6.28125
    C2 = TWO_PI - C1
    yk = posf
    ki = posi
    S.op("dve", lambda e: e.tensor_scalar(out=yk[:], in0=ang[:], scalar1=1.0 / TWO_PI, scalar2=None, op0=ALU.mult), reads=[ang], writes=[yk])
    S.op("dve", lambda e: e.tensor_copy(out=ki[:], in_=yk[:]), reads=[yk], writes=[ki])
    S.op("dve", lambda e: e.tensor_copy(out=yk[:], in_=ki[:]), reads=[ki], writes=[yk])
    S.op("dve", lambda e: e.scalar_tensor_tensor(out=ang[:], in0=yk[:], scalar=-C1, in1=ang[:], op0=ALU.mult, op1=ALU.add), reads=[yk, ang], writes=[ang])
    S.op("dve", lambda e: e.scalar_tensor_tensor(out=ang[:], in0=yk[:], scalar=-C2, in1=ang[:], op0=ALU.mult, op1=ALU.add), reads=[yk, ang], writes=[ang])
    S.op("dve", lambda e: e.tensor_scalar(out=Sn[:], in0=ang[:], scalar1=-math.pi, scalar2=math.pi, op0=ALU.max, op1=ALU.min), reads=[ang], writes=[Sn])
    S.op("act", lambda e: e.activation(out=Sn[:], in_=Sn[:], func=AF.Sin), reads=[Sn], writes=[Sn])
    S.op("dve", lambda e: e.tensor_scalar(out=Sn[:], in0=Sn[:], scalar1=invf[:, 1:2], scalar2=None, op0=ALU.mult), reads=[Sn, invf], writes=[Sn])
    S.op("dve", lambda e: e.tensor_single_scalar(out=yk[:], in_=ang[:], scalar=math.pi / 2, op=ALU.is_gt), reads=[ang], writes=[yk])
    S.op("dve", lambda e: e.scalar_tensor_tensor(out=Ct[:], in0=yk[:], scalar=-TWO_PI, in1=ang[:], op0=ALU.mult, op1=ALU.add), reads=[yk, ang], writes=[Ct])
    S.op("dve", lambda e: e.tensor_scalar(out=Ct[:], in0=Ct[:], scalar1=math.pi / 2, scalar2=math.pi, op0=ALU.add, op1=ALU.min), reads=[Ct], writes=[Ct])
    S.op("dve", lambda e: e.tensor_scalar(out=Ct[:], in0=Ct[:], scalar1=-math.pi, scalar2=None, op0=ALU.max), reads=[Ct], writes=[Ct])
    S.op("act", lambda e: e.activation(out=Ct[:], in_=Ct[:], func=AF.Sin), reads=[Ct], writes=[Ct])

    hT = S.sb("hT", [128, KC, TC], BF16)
    xts = Pool(S, "xt", 1, [128, KC, NT], F32)
    P = {"ps": Pool(S, "ps", 8, [128, 512], F32, psum=True), "sq": Pool(S, "sq", 3, [128, NT], BF16),
         "rstd": Pool(S, "rstd", 2, [128, NT], F32), "tmp": Pool(S, "tmp", 3, [128, NT], F32)}
    P["eps"] = S.sb("epsc", [128, 1], F32)
    S.op("dve", lambda e: e.memset(P["eps"][:], EPS), writes=[P["eps"]])
    for ti in range(NTILE):
        xt = xts.get()
        load_x_tile(S, xT_d, xt, ti)
        rms_affine(S, xt, hT, ti * NT, a1, b1, ones_b, P)

    wts = Pool(S, "wt", 2, [128, KC, 512], BF16)
    stg = Pool(S, "stg", 4, [128, NT], F32)
    stb = Pool(S, "stb", 3, [128, NT], BF16)
    swp = Pool(S, "swp", 2, [32, NT], F32)
    rt2 = Pool(S, "rt2", 2, [32, NT], F32)
    w_in = io["w_in"]
    wsrc = w_in.h[L].rearrange("(kc p) n -> p kc n", p=128)

    def load_w(blocks):
        wt = wts.get()
        o = 0
        for (c0, w) in blocks:
            for half in range(2):
                S.dma("pool", lambda e, wt=wt, o=o, c0=c0, w=w, half=half: e.dma_start(out=wt[:, half * 8:(half + 1) * 8, o:o + w], in_=wsrc[:, half * 8:(half + 1) * 8, c0:c0 + w]),
                      wt, reads=[w_in], writes=[wt])
            o += w
        return wt

    def mm_fm(wt, off, M, ti):
        ps = P["ps"].get()
        for kc in range(KC):
            S.op("pe", lambda e, kc=kc: e.matmul(ps[:M, :], wt[:, kc, off:off + M], hT[:, kc, ti * NT:(ti + 1) * NT], start=(kc == 0), stop=(kc == KC - 1)), reads=[wt, hT], writes=[ps])
        return ps

    def mm_tm(wt, W, tb):
        ps = P["ps"].get()
        for kc in range(KC):
            S.op("pe", lambda e, kc=kc: e.matmul(ps[:, :W], hT[:, kc, tb * 128:(tb + 1) * 128], wt[:, kc, 0:W], start=(kc == 0), stop=(kc == KC - 1)), reads=[wt, hT], writes=[ps])
        return ps

    def store(dst_d, dst_ap, src_t, src_ap):
        S.dma("sp", lambda e: e.dma_start(out=dst_ap, in_=src_ap), src_t, reads=[src_t], writes=[dst_d])

    def rope_store(ps, dst_d, dst_ap, ti, do_rope=True):
        st = stg.get()
        S.op("act", lambda e: e.activation(out=st[:], in_=ps[:], func=AF.Copy), reads=[ps], writes=[st])
        if do_rope:
            sw = swp.get()
            S.dma("sp", lambda e: e.dma_start(out=sw[0:16, :], in_=st[16:32, :]), sw, reads=[st], writes=[sw])
            S.dma("sp", lambda e: e.dma_start(out=sw[16:32, :], in_=st[0:16, :]), sw, reads=[st], writes=[sw])
            t2 = rt2.get()
            tsl = slice(ti * NT, (ti + 1) * NT)
            S.op("dve", lambda e: e.tensor_tensor(out=t2[:], in0=sw[:], in1=Sn[:, tsl], op=ALU.mult), reads=[sw, Sn], writes=[t2])
            S.op("dve", lambda e: e.tensor_tensor(out=st[0:32, :], in0=st[0:32, :], in1=Ct[:, tsl], op=ALU.mult), reads=[st, Ct], writes=[st])
            S.op("dve", lambda e: e.tensor_tensor(out=st[0:32, :], in0=st[0:32, :], in1=t2[:], op=ALU.add), reads=[st, t2], writes=[st])
        sb = stb.get()
        S.op("act", lambda e: e.activation(out=sb[:], in_=st[:], func=AF.Copy), reads=[st], writes=[sb])
        store(dst_d, dst_ap, sb, sb[:])

    bgT, uT, qT, kT, vtok, gn, gaT, gbT = (io[k] for k in ("bgT", "uT", "qT", "kT", "vtok", "gn", "gaT", "gbT"))
    for jb in range(2):
        wt = load_w([(OFF_BG + jb * 512, 512)])
        for ti in range(NTILE):
            for c4 in range(4):
                ps = mm_fm(wt, c4 * 128, 128, ti)
                st = stg.get()
                S.op("act", lambda e, st=st, ps=ps: e.activation(out=st[:], in_=ps[:], func=AF.Copy), reads=[ps], writes=[st])
                ch = jb * 4 + c4
                store(bgT, bgT[ch * 128:(ch + 1) * 128, ti * NT:(ti + 1) * NT], st, st[:])
    for jb in range(4):
        wt = load_w([(OFF_CG + jb * 256, 256), (OFF_XA + jb * 256, 256)])
        for ti in range(NTILE):
            for c2 in range(2):
                pc = mm_fm(wt, c2 * 128, 128, ti)
                px = mm_fm(wt, 256 + c2 * 128, 128, ti)
                st = stg.get()
                S.op("act", lambda e, st=st, pc=pc: e.activation(out=st[:], in_=pc[:], func=AF.Copy), reads=[pc], writes=[st])
                S.op("dve", lambda e, st=st, px=px: e.tensor_tensor(out=st[:], in0=px[:], in1=st[:], op=ALU.mult), reads=[px, st], writes=[st])
                ch = jb * 2 + c2
                store(uT, uT[ch * 128:(ch + 1) * 128, ti * NT:(ti + 1) * NT], st, st[:])
    for jb in range(4):
        wt = load_w([(OFF_Q + jb * 512, 512)])
        for ti in range(NTILE):
            for c4 in range(4):
                ps = mm_fm(wt, c4 * 128, 128, ti)
                hd = jb * 4 + c4
                rope_store(ps, qT, qT[hd * 128:(hd + 1) * 128, ti * NT:(ti + 1) * NT], ti)
    for (kt_i, kvi, rope) in ((0, 0, True), (1, 1, False), (2, 2, True), (3, 4, True)):
        wt = load_w([(OFF_KV + kvi * 512, 512)])
        for ti in range(NTILE):
            for g in range(4):
                ps = mm_fm(wt, g * 128, 128, ti)
                rope_store(ps, kT, kT[kt_i, g, :, ti * NT:(ti + 1) * NT], ti, do_rope=rope)
    for (vt_i, kvi) in ((0, 3), (1, 5)):
        wt = load_w([(OFF_KV + kvi * 512, 512)])
        for tb in range(TC // 128):
            ps = mm_tm(wt, 512, tb)
            sb = stb.get()
            S.op("act", lambda e, sb=sb, ps=ps: e.activation(out=sb[:], in_=ps[:], func=AF.Copy), reads=[ps], writes=[sb])
            store(vtok, vtok[vt_i, tb * 128:(tb + 1) * 128, :], sb, sb[:])
    wt = load_w([(OFF_GN, 48)])
    for tb in range(TC // 128):
        ps = mm_tm(wt, 48, tb)
        st = stg.get()
        S.op("act", lambda e, st=st, ps=ps: e.activation(out=st[:, 0:48], in_=ps[:, 0:48], func=AF.Sigmoid), reads=[ps], writes=[st])
        store(gn, gn[tb * 128:(tb + 1) * 128, :], st, st[:, 0:48])
    for (dst, off) in ((gaT, OFF_GA), (gbT, OFF_GB)):
        for jb in range(4):
            wt = load_w([(off + jb * 512, 512)])
            for ti in range(NTILE):
                for c4 in range(4):
                    ps = mm_fm(wt, c4 * 128, 128, ti)
                    st = stg.get()
                    S.op("act", lambda e, st=st, ps=ps: e.activation(out=st[:], in_=ps[:], func=AF.Sigmoid), reads=[ps], writes=[st])
                    ch = jb * 4 + c4
                    store(dst, dst[ch * 128:(ch + 1) * 128, ti * NT:(ti + 1) * NT], st, st[:])
    S.final_wait("sp", [bgT, uT, qT, kT, vtok, gn, gaT, gbT])


SEQ = 16384
NKT = 128
SCALE = 128 ** -0.5
NSLOT = 16
GELU_C = 2.0 * math.sqrt(2.0 / math.pi)


def build_B2(S, io, slots=None, groups=None):
    slots = list(range(NSLOT)) if slots is None else slots
    groups = list(range(4)) if groups is None else groups
    Kfull, Vfull, qT, gn_d, attnT = io["Kfull"], io["Vfull"], io["qT"], io["gn"], io["attnT"]
    def const(name, shape, dt, src_ap, src_t, q="sp"):
        t = S.sb(name, shape, dt)
        S.dma(q, lambda e: e.dma_start(out=t[:], in_=src_ap), t, reads=[src_t], writes=[t])
        return t
    Ov = const("Ov", [128, 8, 256], BF16, io["Ov"][:], io["Ov"])
    Eall = const("Eall", [128, 64, 128], BF16, io["Eall"][:], io["Eall"])
    DB4 = const("DB4", [128, 8, 512], BF16, io["DB4"][:], io["DB4"])
    WB4 = const("WB4", [128, 12, 512], BF16, io["WB4"][:], io["WB4"])
    tidxrow = const("tidxrow", [128, TC], F32, io["tidxrow"][:], io["tidxrow"])
    curcol = const("curcol", [128, 16], F32, io["curcol"][:], io["curcol"])
    nthr = const("nthr", [128, 8], F32, io["nthr"][:], io["nthr"])
    blkrow = const("blkrow", [128, 256], F32, io["blkrow"][:], io["blkrow"])
    baserow = const("baserow", [128, 256], F32, io["baserow"][:], io["baserow"])
    gn = const("gnsb", [128, NSLOT, 48], F32, gn_d.h.rearrange("(j p) c -> p j c", p=128), gn_d)
    ident = S.sb("ident", [128, 128], BF16)
    S.op("dve", lambda e: e.memset(ident[:], 0.0), writes=[ident])
    S.op("pool", lambda e: e.affine_select(out=ident[:], in_=ident[:], pattern=[[-1, 128]], compare_op=ALU.not_equal, fill=1.0, base=0, channel_multiplier=1), reads=[ident], writes=[ident])

    bigK = S.sb("bigK", [128, SEQ + 32], BF16)
    vsa = S.sb("vsa", [128, NKT, 129], BF16)
    S.op("dve", lambda e: e.memset(bigK[:, SEQ:SEQ + 32], 0.0), writes=[bigK])
    S.op("dve", lambda e: e.memset(vsa[:, :, 128:129], 1.0), writes=[vsa])
    kcT = S.sb("kcT", [128, 4, 1024], BF16)
    vca = S.sb("vca", [128, 4, 8, 129], BF16)
    S.op("dve", lambda e: e.memset(vca[:, :, :, 128:129], 1.0), writes=[vca])

    psS = Pool(S, "psS", 3, [128, 512], F32, psum=True)
    psO = [S.ps("psO%d" % i, [128, 512], F32) for i in range(2)]
    psI = [S.ps("psI%d" % i, [128, 512], F32) for i in range(2)]
    psT = S.ps("psT", [128, 1024], BF16)

    qw = S.sb("qw", [128, 8192], BF16)
    class _V:
        def __init__(self, tt, pat, **kw):
            self.t = tt; self.pat = pat; self.kw = kw
        def __getitem__(self, k):
            return self.t[:].rearrange(self.pat, **self.kw)[k]
    w1v = _V(qw, "p (l h) -> p l h", h=256)
    QTv = _V(qw, "p (r t) -> p r t", t=TC)
    w2b = S.sb("w2b", [128, 2, 128], BF16)
    peT = S.sb("peT", [128, 32], BF16)
    b1T = S.sb("b1T", [128, 2], F32)
    b2c = S.sb("b2c", [128, 1], F32)
    b2row = S.sb("b2row", [128, 128], F32)
    cb = S.sb("cb", [128, 2], F32)
    Hg = S.sb("Hg", [128, 2, 1024], BF16)
    hs = Pool(S, "hs", 1, [128, 512], F32)
    hx = Pool(S, "hx", 1, [128, 512], F32)
    for ty in range(2):
        S.dma("pool", lambda e, ty=ty: e.dma_start(out=w1v[:, 0:16, :], in_=io["cmp_w1"].h[ty].rearrange("(l d) h -> d l h", d=128)[:, 0:16, :]), qw, reads=[io["cmp_w1"]], writes=[qw])
        S.dma("pool", lambda e, ty=ty: e.dma_start(out=w1v[:, 16:32, :], in_=io["cmp_w1"].h[ty].rearrange("(l d) h -> d l h", d=128)[:, 16:32, :]), qw, reads=[io["cmp_w1"]], writes=[qw])
        S.dma("pool", lambda e, ty=ty: e.dma_start(out=w2b[:], in_=io["cmp_w2"].h[ty].rearrange("(hc p) d -> p hc d", p=128)), w2b, reads=[io["cmp_w2"]], writes=[w2b])
        S.dma("pool", lambda e, ty=ty: e.dma_start(out=peT[:], in_=io["cmp_pe"].h[ty].rearrange("l d -> d l"), allow_slow_non_contiguous=True), peT, reads=[io["cmp_pe"]], writes=[peT])
        S.dma("sp", lambda e, ty=ty: e.dma_start(out=b1T[:], in_=io["cmp_b1"].h[ty].rearrange("(hc p) -> p hc", p=128), allow_slow_non_contiguous=True), b1T, reads=[io["cmp_b1"]], writes=[b1T])
        S.dma("sp", lambda e, ty=ty: e.dma_start(out=b2c[:], in_=io["cmp_b2"].h[ty].rearrange("(d o) -> d o", o=1), allow_slow_non_contiguous=True), b2c, reads=[io["cmp_b2"]], writes=[b2c])
        S.dma("sp", lambda e, ty=ty: e.dma_start(out=b2row[:], in_=io["cmp_b2"].h[ty:ty + 1, :].partition_broadcast(128)), b2row, reads=[io["cmp_b2"]], writes=[b2row])
        for hc in range(2):
            ps = psS.get()
            for l in range(32):
                S.op("pe", lambda e, ps=ps, l=l, hc=hc: e.matmul(ps[:, 0:1], w1v[:, l, hc * 128:(hc + 1) * 128], peT[:, l:l + 1], start=(l == 0), stop=(l == 31)), reads=[qw, peT], writes=[ps])
            S.op("dve", lambda e, ps=ps, hc=hc: e.tensor_tensor(out=cb[:, hc:hc + 1], in0=ps[:, 0:1], in1=b1T[:, hc:hc + 1], op=ALU.add), reads=[ps, b1T], writes=[cb])
        for g in range(4):
            for hf in range(2):
                S.dma("sp", lambda e, ty=ty, g=g, hf=hf: e.dma_start(out=bigK[:, hf * 8192:(hf + 1) * 8192], in_=Kfull.h[ty, g, :, hf * 8192:(hf + 1) * 8192]), bigK, reads=[Kfull], writes=[bigK])
            for nh in range(2):
                for hc in range(2):
                    ps = psS.get()
                    for l in range(32):
                        a0 = nh * 512 * 16 + l
                        S.op("pe", lambda e, ps=ps, l=l, hc=hc, a0=a0: e.matmul(ps[:], w1v[:, l, hc * 128:(hc + 1) * 128], bigK[:, a0:a0 + 512 * 16:16], start=(l == 0), stop=(l == 31)), reads=[qw, bigK], writes=[ps])
                    h = hs.get(); x2 = hx.get()
                    S.op("act", lambda e, ps=ps, h=h, hc=hc: e.activation(out=h[:], in_=ps[:], func=AF.Identity, bias=cb[:, hc:hc + 1], scale=1.0), reads=[ps, cb], writes=[h])
                    S.op("dve", lambda e, h=h, x2=x2: e.tensor_tensor(out=x2[:], in0=h[:], in1=h[:], op=ALU.mult), reads=[h], writes=[x2])
                    S.op("dve", lambda e, x2=x2: e.tensor_scalar(out=x2[:], in0=x2[:], scalar1=0.044715, scalar2=1.0, op0=ALU.mult, op1=ALU.add), reads=[x2], writes=[x2])
                    S.op("dve", lambda e, h=h, x2=x2: e.tensor_tensor(out=x2[:], in0=x2[:], in1=h[:], op=ALU.mult), reads=[h, x2], writes=[x2])
                    S.op("act", lambda e, x2=x2: e.activation(out=x2[:], in_=x2[:], func=AF.Sigmoid, scale=GELU_C), reads=[x2], writes=[x2])
                    S.op("dve", lambda e, h=h, x2=x2, hc=hc, nh=nh: e.tensor_tensor(out=Hg[:, hc, nh * 512:(nh + 1) * 512], in0=x2[:], in1=h[:], op=ALU.mult), reads=[h, x2], writes=[Hg])
            if ty == 0:
                for nh in range(2):
                    ps = psS.get()
                    for hc in range(2):
                        S.op("pe", lambda e, ps=ps, hc=hc, nh=nh: e.matmul(ps[:], w2b[:, hc, :], Hg[:, hc, nh * 512:(nh + 1) * 512], start=(hc == 0), stop=(hc == 1)), reads=[w2b, Hg], writes=[ps])
                    S.op("act", lambda e, ps=ps, g=g, nh=nh: e.activation(out=kcT[:, g, nh * 512:(nh + 1) * 512], in_=ps[:], func=AF.Identity, bias=b2c[:], scale=1.0), reads=[ps, b2c], writes=[kcT])
            else:
                for ncn in range(8):
                    ps = psS.get()
                    for hc in range(2):
                        S.op("pe", lambda e, ps=ps, hc=hc, ncn=ncn: e.matmul(ps[:, 0:128], Hg[:, hc, ncn * 128:(ncn + 1) * 128], w2b[:, hc, :], start=(hc == 0), stop=(hc == 1)), reads=[w2b, Hg], writes=[ps])
                    S.op("dve", lambda e, ps=ps, g=g, ncn=ncn: e.tensor_tensor(out=vca[:, g, ncn, 0:128], in0=ps[:, 0:128], in1=b2row[:], op=ALU.add), reads=[ps, b2row], writes=[vca])

    kws = Pool(S, "kws", 2, [128, 12, 128], BF16)
    vws = Pool(S, "vws", 2, [128, 12, 129], BF16)
    for t in vws.t:
        S.op("dve", lambda e, t=t: e.memset(t[:, :, 128:129], 1.0), writes=[t])
    PTs = Pool(S, "PT", 3, [128, 4, 128], BF16)
    Osb = Pool(S, "Osb", 2, [128, 4, 129], F32)
    oacc = Pool(S, "oacc", 2, [128, 4, 128], F32)
    obf = Pool(S, "obf", 2, [128, 4, 128], BF16)
    aT = Pool(S, "aT", 2, [128, 4, 128], BF16)
    NBT4 = Pool(S, "NBT4", 2, [128, 2, 512], BF16)
    cm = Pool(S, "cm", 2, [128, 128], BF16)
    small = Pool(S, "small", 8, [128, 8], F32)
    impp = Pool(S, "imp", 2, [128, 256], F32)
    vv = Pool(S, "vv", 2, [128, 256], F32)
    ff = Pool(S, "ff", 2, [128, 256], F32)
    wk = Pool(S, "wk", 2, [128, 256], F32)
    nbb = Pool(S, "nbb", 2, [128, 256], BF16)
    m8p = Pool(S, "m8", 4, [128, 8], F32)

    def Oview(r):
        return psO[r // 2][:, (r % 2) * 129:(r % 2) * 129 + 129]

    def finish_branch(b, g, j, oa, first):
        osb = Osb.get()
        for k in range(2):
            S.op("act", lambda e, k=k: e.activation(out=osb[:, 2 * k:2 * k + 2, :], in_=psO[k][:, 0:258].rearrange("p (r c) -> p r c", c=129), func=AF.Copy), reads=[psO[k]], writes=[osb])
        if "dbg" in io:
            S.dma("sp", lambda e: e.dma_start(out=io["dbg"].h[b, j * 128:(j + 1) * 128, :], in_=osb[:].rearrange("p r c -> p (r c)")), osb, reads=[osb], writes=[io["dbg"]])
        sm = small.get()
        S.op("dve", lambda e: e.tensor_scalar(out=sm[:, 0:4], in0=osb[:, :, 128], scalar1=1e-30, scalar2=None, op0=ALU.max), reads=[osb], writes=[sm])
        S.op("dve", lambda e: e.reciprocal(out=sm[:, 0:4], in_=sm[:, 0:4]), reads=[sm], writes=[sm])
        c0 = 4 * g * 3 + b
        S.op("dve", lambda e: e.tensor_tensor(out=sm[:, 4:8], in0=sm[:, 0:4], in1=gn[:, j, c0:c0 + 10:3], op=ALU.mult), reads=[sm, gn], writes=[sm])
        for r in range(4):
            if first:
                S.op("dve", lambda e, r=r: e.tensor_scalar(out=oa[:, r, :], in0=osb[:, r, 0:128], scalar1=sm[:, 4 + r:5 + r], scalar2=None, op0=ALU.mult), reads=[osb, sm], writes=[oa])
            else:
                S.op("dve", lambda e, r=r: e.scalar_tensor_tensor(out=oa[:, r, :], in0=osb[:, r, 0:128], scalar=sm[:, 4 + r:5 + r], in1=oa[:, r, :], op0=ALU.mult, op1=ALU.add), reads=[osb, sm, oa], writes=[oa])
        return sm

    def pv(PT, vaug_t, vaug_ap, first, last):
        for r in range(4):
            S.op("pe", lambda e, r=r: e.matmul(Oview(r), PT[:, r, :], vaug_ap, start=(first and r % 2 == 0), stop=last, skip_group_check=True), reads=[PT, vaug_t], writes=[psO[r // 2]])

    def slot(g, j):
        tsl = slice(j * 128, (j + 1) * 128)
        oa = oacc.get()
        ncn = j // 2 + 1
        pend = [None]
        for nci in range(ncn):
            ps = psS.get()
            S.op("pe", lambda e, ps=ps, nci=nci, g=g: e.matmul(ps[:].rearrange("p (r t) -> p r t", t=128), kcT[:, g, nci * 128:(nci + 1) * 128], QTv[:, :, tsl], start=True, stop=True), reads=[kcT, qw], writes=[ps])
            PT = PTs.get()
            S.op("act", lambda e, ps=ps, PT=PT: e.activation(out=PT[:], in_=ps[:].rearrange("p (r t) -> p r t", t=128), func=AF.Exp, scale=SCALE), reads=[ps], writes=[PT])
            c = cm.get()
            S.op("dve", lambda e, c=c, nci=nci: e.tensor_scalar(out=c[:], in0=tidxrow[:, tsl], scalar1=nthr[:, nci:nci + 1], scalar2=None, op0=ALU.is_ge), reads=[tidxrow, nthr], writes=[c])
            S.op("dve", lambda e, c=c, PT=PT: e.tensor_tensor(out=PT[:], in0=PT[:], in1=c[:].unsqueeze(1).broadcast_to([128, 4, 128]), op=ALU.mult), reads=[PT, c], writes=[PT])
            def tail_c(PT=PT, nci=nci):
                pv(PT, vca, vca[:, g, nci, :], nci == 0, nci == ncn - 1)
                for r in range(4):
                    S.op("pe", lambda e, r=r, PT=PT, nci=nci: e.matmul(psI[r // 2][:, (r % 2) * 256:(r % 2) * 256 + 256], PT[:, r, :], Ov[:, nci, :], start=(nci == 0 and r % 2 == 0), stop=(nci == ncn - 1), skip_group_check=True), reads=[PT, Ov], writes=[psI[r // 2]])
            if pend[0] is not None:
                pend[0]()
            pend[0] = tail_c
        pend[0](); pend[0] = None
        sm = finish_branch(0, g, j, oa, True)
        imp = impp.get()
        for r in range(4):
            src = psI[r // 2][:, (r % 2) * 256:(r % 2) * 256 + 256]
            if r == 0:
                S.op("dve", lambda e, src=src: e.tensor_scalar(out=imp[:], in0=src, scalar1=sm[:, 0:1], scalar2=None, op0=ALU.mult), reads=[psI[0], sm], writes=[imp])
            else:
                S.op("dve", lambda e, src=src, r=r: e.scalar_tensor_tensor(out=imp[:], in0=src, scalar=sm[:, r:r + 1], in1=imp[:], op0=ALU.mult, op1=ALU.add), reads=[psI[r // 2], sm, imp], writes=[imp])
        v = vv.get(); f = ff.get(); w = wk.get(); m8a = m8p.get(); m8b = m8p.get(); nb = nbb.get()
        S.op("dve", lambda e: e.tensor_scalar(out=v[:], in0=blkrow[:], scalar1=curcol[:, j:j + 1], scalar2=None, op0=ALU.is_le), reads=[blkrow, curcol], writes=[v])
        S.op("dve", lambda e: e.tensor_tensor(out=imp[:], in0=imp[:], in1=baserow[:], op=ALU.subtract), reads=[imp, baserow], writes=[imp])
        S.op("dve", lambda e: e.tensor_tensor(out=imp[:], in0=imp[:], in1=v[:], op=ALU.mult), reads=[imp, v], writes=[imp])
        S.op("dve", lambda e: e.tensor_tensor(out=imp[:], in0=imp[:], in1=baserow[:], op=ALU.add), reads=[imp, baserow], writes=[imp])
        S.op("dve", lambda e: e.tensor_scalar(out=f[:], in0=blkrow[:], scalar1=curcol[:, j:j + 1], scalar2=1e30, op0=ALU.is_equal, op1=ALU.mult), reads=[blkrow, curcol], writes=[f])
        S.op("dve", lambda e: e.tensor_tensor(out=imp[:], in0=imp[:], in1=f[:], op=ALU.max), reads=[imp, f], writes=[imp])
        S.op("dve", lambda e: e.memset(imp[:, 0:1], 2e30), reads=[imp], writes=[imp])
        S.op("dve", lambda e: e.max(out=m8a[:], in_=imp[:]), reads=[imp], writes=[m8a])
        S.op("dve", lambda e: e.match_replace(out=w[:], in_to_replace=m8a[:], in_values=imp[:], imm_value=-1e30), reads=[imp, m8a], writes=[w])
        S.op("dve", lambda e: e.max(out=m8b[:], in_=w[:]), reads=[w], writes=[m8b])
        S.op("dve", lambda e: e.tensor_scalar(out=f[:], in0=imp[:], scalar1=m8b[:, 7:8], scalar2=None, op0=ALU.is_ge), reads=[imp, m8b], writes=[f])
        S.op("dve", lambda e: e.tensor_tensor(out=f[:], in0=f[:], in1=v[:], op=ALU.mult), reads=[f, v], writes=[f])
        if "dbg2" in io:
            S.dma("sp", lambda e: e.dma_start(out=io["dbg2"].h[j * 128:(j + 1) * 128, :], in_=f[:]), f, reads=[f], writes=[io["dbg2"]])
        S.op("dve", lambda e: e.tensor_scalar(out=nb[:], in0=f[:], scalar1=-1.0, scalar2=30000.0, op0=ALU.add, op1=ALU.mult), reads=[f], writes=[nb])
        for c2 in range(2):
            S.op("pe", lambda e, c2=c2: e.transpose(psT[:, c2 * 128:(c2 + 1) * 128], nb[:, c2 * 128:(c2 + 1) * 128], ident[:]), reads=[nb, ident], writes=[psT])
        nbt = NBT4.get()
        for r in range(4):
            eng = "act" if r % 2 == 0 else "dve"
            if eng == "act":
                S.op("act", lambda e, r=r: e.activation(out=nbt[:, :, r * 128:(r + 1) * 128], in_=psT[:, 0:256].rearrange("p (c t) -> p c t", t=128), func=AF.Copy), reads=[psT], writes=[nbt])
            else:
                S.op("dve", lambda e, r=r: e.tensor_copy(out=nbt[:, :, r * 128:(r + 1) * 128], in_=psT[:, 0:256].rearrange("p (c t) -> p c t", t=128)), reads=[psT], writes=[nbt])
        nkt = 8 * j + 8
        for kt in range(nkt):
            ps = psS.get()
            diag = kt >= 8 * j
            S.op("pe", lambda e, ps=ps, kt=kt: e.matmul(ps[:].rearrange("p (r t) -> p r t", t=128), bigK[:, kt * 128:(kt + 1) * 128], QTv[:, :, tsl], start=True, stop=False), reads=[bigK, qw], writes=[ps])
            S.op("pe", lambda e, ps=ps, kt=kt, diag=diag: e.matmul(ps[:], Eall[:, kt % 64, :], nbt[:, kt // 64, :], start=False, stop=(not diag)), reads=[Eall, nbt], writes=[ps])
            if diag:
                S.op("pe", lambda e, ps=ps, kt=kt: e.matmul(ps[:], ident[:], DB4[:, kt - 8 * j, :], start=False, stop=True), reads=[ident, DB4], writes=[ps])
            PT = PTs.get()
            S.op("act", lambda e, ps=ps, PT=PT: e.activation(out=PT[:], in_=ps[:].rearrange("p (r t) -> p r t", t=128), func=AF.Exp, scale=SCALE), reads=[ps], writes=[PT])
            def tail_s(PT=PT, kt=kt):
                pv(PT, vsa, vsa[:, kt, :], kt == 0, kt == nkt - 1)
            if pend[0] is not None:
                pend[0]()
            pend[0] = tail_s
        pend[0](); pend[0] = None
        finish_branch(1, g, j, oa, False)
        kw = kws.get(); vw = vws.get()
        m0 = 4 if j == 0 else 0
        k0 = 8 * j - 4 + m0
        nm = 12 - m0
        S.dma("sp", lambda e, g=g: e.dma_start(out=kw[:, m0:12, :], in_=Kfull.h[3, g, :, k0 * 128:(k0 + nm) * 128].rearrange("d (m k) -> d m k", k=128)), kw, reads=[Kfull], writes=[kw])
        S.dma("sp", lambda e, g=g: e.dma_start(out=vw[:, m0:12, 0:128], in_=Vfull.h[1, k0 * 128:(k0 + nm) * 128, g * 128:(g + 1) * 128].rearrange("(m p) d -> p m d", p=128)), vw, reads=[Vfull], writes=[vw])
        for m in range(m0, 12):
            ps = psS.get()
            S.op("pe", lambda e, ps=ps, m=m: e.matmul(ps[:].rearrange("p (r t) -> p r t", t=128), kw[:, m, :], QTv[:, :, tsl], start=True, stop=False), reads=[kw, qw], writes=[ps])
            S.op("pe", lambda e, ps=ps, m=m: e.matmul(ps[:], ident[:], WB4[:, m, :], start=False, stop=True), reads=[ident, WB4], writes=[ps])
            PT = PTs.get()
            S.op("act", lambda e, ps=ps, PT=PT: e.activation(out=PT[:], in_=ps[:].rearrange("p (r t) -> p r t", t=128), func=AF.Exp, scale=SCALE), reads=[ps], writes=[PT])
            def tail_w(PT=PT, m=m):
                pv(PT, vw, vw[:, m, :], m == m0, m == 11)
            if pend[0] is not None:
                pend[0]()
            pend[0] = tail_w
        pend[0](); pend[0] = None
        finish_branch(2, g, j, oa, False)
        ob = obf.get()
        S.op("act", lambda e: e.activation(out=ob[:], in_=oa[:], func=AF.Copy), reads=[oa], writes=[ob])
        for r in range(4):
            S.op("pe", lambda e, r=r: e.transpose(psT[:, 256 + r * 128:256 + (r + 1) * 128], ob[:, r, :], ident[:]), reads=[ob, ident], writes=[psT])
        at = aT.get()
        S.op("dve", lambda e: e.tensor_copy(out=at[:], in_=psT[:, 256:768].rearrange("p (r t) -> p r t", t=128)), reads=[psT], writes=[at])
        S.dma("sp", lambda e, g=g: e.dma_start(out=attnT.h[g * 512:(g + 1) * 512, tsl].rearrange("(r d) t -> d r t", d=128), in_=at[:]), at, reads=[at], writes=[attnT])


    for g in groups:
        for hf in range(2):
            S.dma("sp", lambda e, g=g, hf=hf: e.dma_start(out=bigK[:, hf * 8192:(hf + 1) * 8192], in_=Kfull.h[2, g, :, hf * 8192:(hf + 1) * 8192]), bigK, reads=[Kfull], writes=[bigK])
        for q8 in range(8):
            S.dma("sp", lambda e, g=g, q8=q8: e.dma_start(out=vsa[:, q8 * 16:(q8 + 1) * 16, 0:128], in_=Vfull.h[0, q8 * 2048:(q8 + 1) * 2048, g * 128:(g + 1) * 128].rearrange("(kt p) d -> p kt d", p=128)), vsa, reads=[Vfull], writes=[vsa])
        S.dma("sp", lambda e, g=g: e.dma_start(out=QTv[:], in_=qT.h[g * 512:(g + 1) * 512, :].rearrange("(r d) t -> d r t", d=128)), qw, reads=[qT], writes=[qw])
        for j in slots:
            slot(g, j)
    S.final_wait("sp", [attnT])


DFF = 8192


def build_B3(S, io):
    xT_d, bgT, uT, utail, attnT, gaT, gbT, xo = (io[k] for k in ("xT", "bgT", "uT", "utail", "attnT", "gaT", "gbT", "xT_out"))
    ones_b = S.sb("ones_b", [128, 128], BF16)
    S.op("dve", lambda e: e.memset(ones_b[:], 1.0), writes=[ones_b])
    modT = S.sb("modT", [128, 96], F32)
    gT = S.sb("gT", [128, 64], F32)
    S.dma("sp", lambda e: e.dma_start(out=modT[:], in_=io["modT"][:]), modT, reads=[io["modT"]], writes=[modT])
    S.dma("sp", lambda e: e.dma_start(out=gT[:], in_=io["gainsT"][:]), gT, reads=[io["gainsT"]], writes=[gT])
    cw = S.sb("cw", [128, 3, 8], F32)
    for k in range(3):
        S.dma("sp", lambda e, k=k: e.dma_start(out=cw[:, k, :], in_=io["conv_w"].h[k].rearrange("(cc p) -> p cc", p=128), allow_slow_non_contiguous=True), cw, reads=[io["conv_w"]], writes=[cw])
    ag1 = S.sb("ag1", [128, 16], F32); a2 = S.sb("a2", [128, 16], F32); b2 = S.sb("b2", [128, 16], F32); ag3 = S.sb("ag3", [128, 16], F32)
    S.op("dve", lambda e: e.tensor_tensor(out=ag1[:], in0=modT[:, 32:48], in1=gT[:, 16:32], op=ALU.mult), reads=[modT, gT], writes=[ag1])
    S.op("dve", lambda e: e.scalar_tensor_tensor(out=a2[:], in0=modT[:, 64:80], scalar=1.0, in1=gT[:, 32:48], op0=ALU.add, op1=ALU.mult), reads=[modT, gT], writes=[a2])
    S.op("dve", lambda e: e.tensor_copy(out=b2[:], in_=modT[:, 48:64]), reads=[modT], writes=[b2])
    S.op("dve", lambda e: e.tensor_tensor(out=ag3[:], in0=modT[:, 80:96], in1=gT[:, 48:64], op=ALU.mult), reads=[modT, gT], writes=[ag3])

    xt = S.sb("xt", [128, KC, NT], F32)
    ysb = S.sb("ysb", [128, KC, NT], F32)
    hid = S.sb("hid", [128, 64, NT], BF16)
    mg = S.sb("mg", [128, KC, NT], BF16)
    wts = Pool(S, "wt", 2, [128, KC, 512], BF16)
    P = {"ps": Pool(S, "ps", 8, [128, 512], F32, psum=True), "sq": Pool(S, "sq", 2, [128, NT], BF16),
         "rstd": Pool(S, "rstd", 1, [128, NT], F32), "tmp": Pool(S, "tmp", 3, [128, NT], F32)}
    P["eps"] = S.sb("epsc", [128, 1], F32)
    S.op("dve", lambda e: e.memset(P["eps"][:], EPS), writes=[P["eps"]])
    uext = Pool(S, "uext", 2, [128, 4, 130], F32)
    bgc = Pool(S, "bgc", 2, [128, NT], F32)
    zt = Pool(S, "zt", 2, [128, 4, 128], F32)
    gch = Pool(S, "gch", 4, [128, NT], F32)

    def load_w(wd, r0, nk, c0):
        wt = wts.get()
        src = wd.h[r0:r0 + nk * 128, c0:c0 + 512].rearrange("(kc p) n -> p kc n", p=128)
        hk = nk // 2
        for half in range(2):
            S.dma("pool", lambda e, half=half: e.dma_start(out=wt[:, half * hk:(half + 1) * hk, :], in_=src[:, half * hk:(half + 1) * hk, :]), wt, reads=[wd], writes=[wt])
        return wt

    def post_norm_residual(coef):
        ss = P["ps"].get()
        for kc in range(KC):
            s = P["sq"].get()
            S.op("act", lambda e, s=s, kc=kc: e.activation(out=s[:], in_=ysb[:, kc, :], func=AF.Square), reads=[ysb], writes=[s])
            S.op("pe", lambda e, s=s, kc=kc: e.matmul(ss[:], ones_b[:], s[:], start=(kc == 0), stop=(kc == KC - 1)), reads=[s, ones_b], writes=[ss])
        r = P["rstd"].get()
        S.op("act", lambda e: e.activation(out=r[:], in_=ss[:], func=AF.Sqrt, bias=P["eps"][:], scale=1.0 / D), reads=[ss, P["eps"]], writes=[r])
        S.op("dve", lambda e: e.reciprocal(out=r[:], in_=r[:]), reads=[r], writes=[r])
        for kc in range(KC):
            t = P["tmp"].get()
            S.op("dve", lambda e, t=t, kc=kc: e.scalar_tensor_tensor(out=t[:], in0=ysb[:, kc, :], scalar=coef[:, kc:kc + 1], in1=r[:], op0=ALU.mult, op1=ALU.mult), reads=[ysb, coef, r], writes=[t])
            S.op("pool", lambda e, t=t, kc=kc: e.tensor_tensor(out=xt[:, kc, :], in0=xt[:, kc, :], in1=t[:], op=ALU.add), reads=[xt, t], writes=[xt])

    def tile(ti):
        tsl = slice(ti * NT, (ti + 1) * NT)
        load_x_tile(S, xT_d, xt, ti)
        for q4 in range(4):
            S.dma("sp", lambda e, q4=q4: e.dma_start(out=hid[:, q4 * 4:(q4 + 1) * 4, :], in_=attnT.h[q4 * 512:(q4 + 1) * 512, tsl].rearrange("(hc p) t -> p hc t", p=128)), hid, reads=[attnT], writes=[hid])
        for cc in range(8):
            ue = uext.get(); bg = bgc.get(); z = zt.get()
            S.dma("sp", lambda e, ue=ue, cc=cc: e.dma_start(out=ue[:, :, 2:130], in_=uT.h[cc * 128:(cc + 1) * 128, tsl].rearrange("p (b t) -> p b t", t=128)), ue, reads=[uT], writes=[ue])
            S.dma("sp", lambda e, ue=ue, cc=cc: e.dma_start(out=ue[:, :, 0:2], in_=utail.h[cc * 128:(cc + 1) * 128, ti * 4:(ti + 1) * 4, :]), ue, reads=[utail], writes=[ue])
            S.dma("sp", lambda e, bg=bg, cc=cc: e.dma_start(out=bg[:], in_=bgT.h[cc * 128:(cc + 1) * 128, tsl]), bg, reads=[bgT], writes=[bg])
            S.op("dve", lambda e, ue=ue, z=z, cc=cc: e.tensor_scalar(out=z[:], in0=ue[:, :, 2:130], scalar1=cw[:, 2, cc:cc + 1], scalar2=None, op0=ALU.mult), reads=[ue, cw], writes=[z])
            S.op("dve", lambda e, ue=ue, z=z, cc=cc: e.scalar_tensor_tensor(out=z[:], in0=ue[:, :, 1:129], scalar=cw[:, 1, cc:cc + 1], in1=z[:], op0=ALU.mult, op1=ALU.add), reads=[ue, cw, z], writes=[z])
            S.op("dve", lambda e, ue=ue, z=z, cc=cc: e.scalar_tensor_tensor(out=z[:], in0=ue[:, :, 0:128], scalar=cw[:, 0, cc:cc + 1], in1=z[:], op0=ALU.mult, op1=ALU.add), reads=[ue, cw, z], writes=[z])
            S.op("dve", lambda e, bg=bg, z=z, cc=cc: e.tensor_tensor(out=hid[:, 16 + cc, :], in0=z[:].rearrange("p b t -> p (b t)"), in1=bg[:], op=ALU.mult), reads=[z, bg], writes=[hid])
        for og in range(4):
            wa = load_w(io["w_conv_out"], 0, 8, og * 512)
            wb = load_w(io["w_nsa_out"], 0, 16, og * 512)
            for c4 in range(4):
                oc = og * 4 + c4
                pa = P["ps"].get(); pb = P["ps"].get()
                for cc in range(8):
                    S.op("pe", lambda e, cc=cc, pa=pa, c4=c4, wa=wa: e.matmul(pa[:], wa[:, cc, c4 * 128:(c4 + 1) * 128], hid[:, 16 + cc, :], start=(cc == 0), stop=(cc == 7)), reads=[wa, hid], writes=[pa])
                for hc in range(16):
                    S.op("pe", lambda e, hc=hc, pb=pb, c4=c4, wb=wb: e.matmul(pb[:], wb[:, hc, c4 * 128:(c4 + 1) * 128], hid[:, hc, :], start=(hc == 0), stop=(hc == 15)), reads=[wb, hid], writes=[pb])
                ga = gch.get(); gb = gch.get()
                S.dma("sp", lambda e, ga=ga, oc=oc: e.dma_start(out=ga[:], in_=gaT.h[oc * 128:(oc + 1) * 128, tsl]), ga, reads=[gaT], writes=[ga])
                S.dma("sp", lambda e, gb=gb, oc=oc: e.dma_start(out=gb[:], in_=gbT.h[oc * 128:(oc + 1) * 128, tsl]), gb, reads=[gbT], writes=[gb])
                S.op("dve", lambda e, ga=ga, pa=pa: e.tensor_tensor(out=ga[:], in0=pa[:], in1=ga[:], op=ALU.mult), reads=[pa, ga], writes=[ga])
                S.op("dve", lambda e, gb=gb, pb=pb: e.tensor_tensor(out=gb[:], in0=pb[:], in1=gb[:], op=ALU.mult), reads=[pb, gb], writes=[gb])
                S.op("pool", lambda e, ga=ga, gb=gb, oc=oc: e.tensor_tensor(out=mg[:, oc, :], in0=ga[:], in1=gb[:], op=ALU.add), reads=[ga, gb], writes=[mg])
        for og in range(4):
            wo = load_w(io["w_out"], 0, 16, og * 512)
            for c4 in range(4):
                ps = P["ps"].get()
                for kc in range(KC):
                    S.op("pe", lambda e, kc=kc, ps=ps, c4=c4, wo=wo: e.matmul(ps[:], wo[:, kc, c4 * 128:(c4 + 1) * 128], mg[:, kc, :], start=(kc == 0), stop=(kc == KC - 1)), reads=[wo, mg], writes=[ps])
                S.op("act", lambda e, ps=ps, og=og, c4=c4: e.activation(out=ysb[:, og * 4 + c4, :], in_=ps[:], func=AF.Copy), reads=[ps], writes=[ysb])
        post_norm_residual(ag1)
        if "xmid" in io:
            for q4 in range(4):
                S.dma("sp", lambda e, q4=q4: e.dma_start(out=io["xmid"].h.rearrange("(kc p) t -> p kc t", p=128)[:, q4 * 4:(q4 + 1) * 4, tsl], in_=xt[:, q4 * 4:(q4 + 1) * 4, :]), xt, reads=[xt], writes=[io["xmid"]])
        rms_affine(S, xt, mg, 0, a2, b2, ones_b, P)
        for fg in range(16):
            wu = load_w(io["w_mlp_up"], 0, 16, fg * 512)
            for c4 in range(4):
                ps = P["ps"].get()
                for kc in range(KC):
                    S.op("pe", lambda e, kc=kc, ps=ps, c4=c4, wu=wu: e.matmul(ps[:], wu[:, kc, c4 * 128:(c4 + 1) * 128], mg[:, kc, :], start=(kc == 0), stop=(kc == KC - 1)), reads=[wu, mg], writes=[ps])
                t = P["tmp"].get()
                S.op("act", lambda e, ps=ps, t=t: e.activation(out=t[:], in_=ps[:], func=AF.Relu), reads=[ps], writes=[t])
                S.op("dve", lambda e, t=t, fg=fg, c4=c4: e.tensor_tensor(out=hid[:, fg * 4 + c4, :], in0=t[:], in1=t[:], op=ALU.mult), reads=[t], writes=[hid])
        for og in range(4):
            pss = [P["ps"].get() for _ in range(4)]
            for slab in range(4):
                wd = load_w(io["w_mlp_down"], slab * 2048, 16, og * 512)
                for c4 in range(4):
                    for fcl in range(16):
                        S.op("pe", lambda e, fcl=fcl, c4=c4, wd=wd, slab=slab, pss=pss: e.matmul(pss[c4][:], wd[:, fcl, c4 * 128:(c4 + 1) * 128], hid[:, slab * 16 + fcl, :], start=(slab == 0 and fcl == 0), stop=(slab == 3 and fcl == 15)), reads=[wd, hid], writes=[pss[c4]])
            for c4 in range(4):
                S.op("act", lambda e, c4=c4, og=og, pss=pss: e.activation(out=ysb[:, og * 4 + c4, :], in_=pss[c4][:], func=AF.Copy), reads=[pss[c4]], writes=[ysb])
        post_norm_residual(ag3)
        for q4 in range(4):
            S.dma("sp", lambda e, q4=q4: e.dma_start(out=xo.h.rearrange("(kc p) t -> p kc t", p=128)[:, q4 * 4:(q4 + 1) * 4, tsl], in_=xt[:, q4 * 4:(q4 + 1) * 4, :]), xt, reads=[xt], writes=[xo])

    for ti in range(NTILE):
        tile(ti)
    S.final_wait("sp", [xo] + ([io["xmid"]] if "xmid" in io else []))


MCOLS = 6144


def build_M(S, io):
    cT = S.sb("cT", [128, 16], F32)
    S.dma("sp", lambda e: e.dma_start(out=cT[:], in_=io["cT"][:]), cT, reads=[io["cT"]], writes=[cT])
    S.op("act", lambda e: e.activation(out=cT[:], in_=cT[:], func=AF.Silu), reads=[cT], writes=[cT])
    brow = S.sb("brow", [1, MCOLS], F32)
    S.dma("sp", lambda e: e.dma_start(out=brow[:], in_=io["ada_b"][:]), brow, reads=[io["ada_b"]], writes=[brow])
    orow = S.sb("orow", [1, MCOLS], F32)
    wts = Pool(S, "wm", 2, [128, 16, 512], F32)
    pss = Pool(S, "psm", 2, [128, 512], F32, psum=True)
    wsrc = io["ada_w"].h.rearrange("(kc p) n -> p kc n", p=128)
    for gi in range(MCOLS // 512):
        wt = wts.get()
        for half in range(2):
            S.dma("sp", lambda e, wt=wt, half=half, gi=gi: e.dma_start(out=wt[:, half * 8:(half + 1) * 8, :], in_=wsrc[:, half * 8:(half + 1) * 8, gi * 512:(gi + 1) * 512]), wt, reads=[io["ada_w"]], writes=[wt])
        ps = pss.get()
        for kc in range(16):
            S.op("pe", lambda e, wt=wt, ps=ps, kc=kc: e.matmul(ps[0:1, :], cT[:, kc:kc + 1], wt[:, kc, :], start=(kc == 0), stop=(kc == 15)), reads=[cT, wt], writes=[ps])
        S.op("dve", lambda e, ps=ps, gi=gi: e.tensor_tensor(out=orow[:, gi * 512:(gi + 1) * 512], in0=ps[0:1, :], in1=brow[:, gi * 512:(gi + 1) * 512], op=ALU.add), reads=[ps, brow], writes=[orow])
    S.dma("sp", lambda e: e.dma_start(out=io["mod"][:], in_=orow[:]), orow, reads=[orow], writes=[io["mod"]])
    S.final_wait("sp", [io["mod"]])

bf = ml_dtypes.bfloat16
SEQ = 16384; TC = 2048

def tok_idx(c):
    return np.concatenate([np.arange((8 * j + c) * 128, (8 * j + c + 1) * 128) for j in range(16)])

def consts_common():
    n = np.arange(1024)[:, None]; jb = np.arange(256)[None, :]
    ov = np.clip(np.minimum(16 * n + 32, 64 * jb + 64) - np.maximum(16 * n, 64 * jb), 0, None).astype(np.float32) / 32
    ov[1023:] = 0
    Ov = np.ascontiguousarray(ov.reshape(8, 128, 256).transpose(1, 0, 2)).astype(bf)
    Eall = np.zeros((128, 64, 128), np.float32)
    for i in range(64):
        Eall[2 * i, i, 0:64] = 1; Eall[2 * i + 1, i, 64:128] = 1
    p = np.arange(128)[:, None]
    nthr = (16 * (np.arange(8)[None, :] * 128 + p) + 31).astype(np.float32)
    blkrow = np.broadcast_to(np.arange(256, dtype=np.float32)[None, :], (128, 256)).copy()
    baserow = -(blkrow + 2)
    invf = (500000.0 ** (-np.arange(0, 32, 2, dtype=np.float32) / 32)).astype(np.float32)
    invf2 = np.zeros((32, 2), np.float32); invf2[:, 0] = np.tile(invf, 2); invf2[:16, 1] = -1; invf2[16:, 1] = 1
    return {"Ov": Ov, "Eall": Eall.astype(bf), "nthr": nthr, "blkrow": blkrow, "baserow": baserow, "invf": invf2}

def consts_core(c):
    i = np.arange(128)[:, None]; ip = np.arange(128)[None, :]
    NEG = -30000.0
    causal = np.where(i <= ip, 0.0, NEG).astype(np.float32)
    band = np.where(i > ip, 0.0, NEG).astype(np.float32)
    full = np.zeros((128, 128), np.float32); none = np.full((128, 128), NEG, np.float32)
    DB = np.zeros((128, 8, 4, 128), np.float32)
    for kk in range(8):
        DB[:, kk] = (causal if kk == c else full)[:, None, :]
    WB = np.zeros((128, 12, 4, 128), np.float32)
    for m in range(12):
        d = m - 4 - c
        t = none if d < -4 else band if d == -4 else full if d < 0 else causal if d == 0 else none
        WB[:, m] = t[:, None, :]
    ti = tok_idx(c)
    tidxrow = np.broadcast_to(ti.astype(np.float32)[None, :], (128, TC)).copy()
    curcol = (ti.reshape(16, 128).T // 64).astype(np.float32)
    return {"DB4": DB.reshape(128, 8, 512).astype(bf), "WB4": WB.reshape(128, 12, 512).astype(bf), "tidxrow": tidxrow, "curcol": np.ascontiguousarray(curcol)}

def gelu_tanh(x):
    return 0.5 * x * (1 + np.tanh(np.sqrt(2 / np.pi) * (x + 0.044715 * x ** 3)))

def ref_compress(rawT, pe, w1, b1, w2, b2):
    kv = rawT.T
    idx = np.arange(1023)[:, None] * 16 + np.arange(32)[None, :]
    blocks = (kv[idx] + pe[None]).reshape(1023, 4096)
    h = gelu_tanh(blocks @ w1 + b1)
    return h @ w2 + b2

def ref_attn_block(gb, q, kc, vc, ks, vs, kw, vw, gate):
    scale = 128 ** -0.5
    t = gb * 128 + np.arange(128)
    sc = np.einsum('trd,nd->rtn', q, kc) * scale
    cmp_end = np.arange(1023) * 16 + 31
    m_c = cmp_end[None, :] <= t[:, None]
    scm = np.where(m_c[None], sc, -1e30)
    e = np.exp(scm - scm.max(-1, keepdims=True)); p = e / e.sum(-1, keepdims=True)
    p_c = np.where(m_c[None], p, 0.0)
    o_c = np.einsum('rtn,nd->trd', p_c, vc)
    n = np.arange(1023)[:, None]; jb = np.arange(256)[None, :]
    ov = np.clip(np.minimum(16 * n + 32, 64 * jb + 64) - np.maximum(16 * n, 64 * jb), 0, None) / 32.0
    imp = np.einsum('rtn,nj->tj', p_c, ov)
    cur = t // 64
    blk = np.arange(256)
    forced = (blk[None, :] == cur[:, None]) | (blk[None, :] == 0)
    valid = blk[None, :] <= cur[:, None]
    imp = np.where(forced, 1e30, np.where(valid, imp, -1e30))
    sel = np.argsort(-imp, axis=-1, kind='stable')[:, :16]
    o_s = np.zeros((128, 4, 128));
    for i in range(128):
        kpos = (sel[i][:, None] * 64 + np.arange(64)[None, :]).reshape(-1)
        msk = kpos <= t[i]
        s = (q[i] @ ks[kpos].T) * scale
        s = np.where(msk[None], s, -1e30)
        e = np.exp(s - s.max(-1, keepdims=True)); pp = e / e.sum(-1, keepdims=True)
        o_s[i] = pp @ vs[kpos]
    o_w = np.zeros((128, 4, 128))
    for i in range(128):
        lo = max(0, t[i] - 511)
        s = (q[i] @ kw[lo:t[i] + 1].T) * scale
        e = np.exp(s - s.max(-1, keepdims=True)); pp = e / e.sum(-1, keepdims=True)
        o_w[i] = pp @ vw[lo:t[i] + 1]
    return gate[..., 0:1] * o_c + gate[..., 1:2] * o_s + gate[..., 2:3] * o_w, (o_c, o_s, o_w, sel)

_PROGS = {}
NCORES = 8


def _prog(name):
    if name in _PROGS:
        return _PROGS[name]
    nc = bass.Bass("TRN2", target_bir_lowering=False)
    with ExitStack() as es:
        S = Sched(nc, es)
        io = {}

        def din(n, shape, dt):
            io[n] = S.dram(n, shape, dt, kind="ExternalInput")

        def dout(n, shape, dt):
            io[n] = S.dram(n, shape, dt, kind="ExternalOutput")
        if name == "M":
            din("cT", [128, 16], F32); din("ada_w", [2048, MCOLS], F32); din("ada_b", [1, MCOLS], F32)
            dout("mod", [1, MCOLS], F32)
            build_M(S, io)
        elif name == "A":
            din("xT", [D, TC], F32); din("modT", [128, 96], F32); din("gainsT", [128, 64], F32)
            din("pos", [32, TC], I32); din("invf", [32, 2], F32); din("w_in", [1, D, INW], F32)
            dout("bgT", [1024, TC], F32); dout("uT", [1024, TC], F32); dout("qT", [2048, TC], BF16)
            dout("kT", [4, 4, 128, TC], BF16); dout("vtok", [2, TC, 512], BF16); dout("gn", [TC, 48], F32)
            dout("gaT", [2048, TC], F32); dout("gbT", [2048, TC], F32)
            build_A(S, io, 0)
        elif name == "B2":
            din("Kfull", [4, 4, 128, SEQ], BF16); din("Vfull", [2, SEQ, 512], BF16); din("qT", [2048, TC], BF16); din("gn", [TC, 48], F32)
            din("cmp_pe", [2, 32, 128], F32); din("cmp_w1", [2, 4096, 256], F32); din("cmp_b1", [2, 256], F32)
            din("cmp_w2", [2, 256, 128], F32); din("cmp_b2", [2, 128], F32)
            din("Ov", [128, 8, 256], BF16); din("Eall", [128, 64, 128], BF16); din("DB4", [128, 8, 512], BF16); din("WB4", [128, 12, 512], BF16)
            din("tidxrow", [128, TC], F32); din("curcol", [128, 16], F32); din("nthr", [128, 8], F32)
            din("blkrow", [128, 256], F32); din("baserow", [128, 256], F32)
            dout("attnT", [2048, TC], BF16)
            build_B2(S, io)
        elif name == "B3":
            din("xT", [D, TC], F32); din("bgT", [1024, TC], F32); din("uT", [1024, TC], F32); din("utail", [1024, 16, 2], F32)
            din("attnT", [2048, TC], BF16); din("gaT", [2048, TC], F32); din("gbT", [2048, TC], F32)
            din("modT", [128, 96], F32); din("gainsT", [128, 64], F32); din("conv_w", [3, 1024], F32)
            din("w_conv_out", [1024, 2048], F32); din("w_nsa_out", [2048, 2048], F32); din("w_out", [2048, 2048], F32)
            din("w_mlp_up", [2048, DFF], F32); din("w_mlp_down", [DFF, 2048], F32)
            dout("xT_out", [D, TC], F32)
            build_B3(S, io)
        S.emit()
    _PROGS[name] = nc
    return nc


def _run(name, in_maps):
    nc = _prog(name)
    res = run_bass_kernel_spmd(nc, in_maps, core_ids=list(range(NCORES)))
    return res.results


def kernel(x, c, positions, ada_w, ada_b, norm_gains, w_in, conv_w, w_conv_out, cmp_pe, cmp_w1, cmp_b1,
           cmp_w2, cmp_b2, w_nsa_out, w_out, w_mlp_up, w_mlp_down):
    f32 = lambda a: np.ascontiguousarray(np.asarray(a), dtype=np.float32)
    x = f32(x); c = f32(c); positions = np.asarray(positions).astype(np.int32)
    ada_w = np.asarray(ada_w); ada_b = np.asarray(ada_b); norm_gains = f32(norm_gains)
    DEPTH = ada_w.shape[0]
    toks = [tok_idx(cc) for cc in range(NCORES)]
    cc_ = consts_common()
    ccore = [consts_core(cc) for cc in range(NCORES)]
    cT = np.ascontiguousarray(c[0].reshape(16, 128).T)
    in_maps = []
    per_layer = 12288 // MCOLS
    for cc in range(NCORES):
        l, h = cc // per_layer, cc % per_layer
        in_maps.append({"cT": cT, "ada_w": f32(ada_w[l][:, h * MCOLS:(h + 1) * MCOLS]), "ada_b": f32(ada_b[l][None, h * MCOLS:(h + 1) * MCOLS])})
    resM = _run("M", in_maps)
    mods = [np.concatenate([resM[l * per_layer + h]["mod"][0] for h in range(per_layer)]) for l in range(DEPTH)]
    xT = [np.ascontiguousarray(x[0][toks[cc]].T) for cc in range(NCORES)]
    pos = [np.ascontiguousarray(np.broadcast_to(positions[0][toks[cc]][None, :], (32, TC))).astype(np.int32) for cc in range(NCORES)]
    for l in range(DEPTH):
        modT = np.ascontiguousarray(mods[l].reshape(96, 128).T)
        gainsT = np.ascontiguousarray(norm_gains[l].reshape(64, 128).T)
        w_in_l = f32(np.asarray(w_in)[l:l + 1])
        resA = _run("A", [{"xT": xT[cc], "modT": modT, "gainsT": gainsT, "pos": pos[cc], "invf": cc_["invf"], "w_in": w_in_l} for cc in range(NCORES)])
        Kfull = np.zeros((4, 4, 128, SEQ), bf); Vfull = np.zeros((2, SEQ, 512), bf)
        for cc in range(NCORES):
            Kfull[:, :, :, toks[cc]] = resA[cc]["kT"]
            Vfull[:, toks[cc], :] = resA[cc]["vtok"]
        utails = []
        for cc in range(NCORES):
            ut = np.zeros((1024, 16, 2), np.float32)
            for j in range(16):
                gb = 8 * j + cc
                if gb == 0:
                    continue
                pc, pj = (gb - 1) % 8, (gb - 1) // 8
                ut[:, j, :] = resA[pc]["uT"][:, pj * 128 + 126:pj * 128 + 128]
            utails.append(ut)
        cmpw = {k: f32(np.asarray(v)[l]) for k, v in (("cmp_pe", cmp_pe), ("cmp_w1", cmp_w1), ("cmp_b1", cmp_b1), ("cmp_w2", cmp_w2), ("cmp_b2", cmp_b2))}
        in_maps = []
        for cc in range(NCORES):
            m = {"Kfull": Kfull, "Vfull": Vfull, "qT": resA[cc]["qT"], "gn": resA[cc]["gn"]}
            m.update(cmpw)
            for k in ("Ov", "Eall", "nthr", "blkrow", "baserow"):
                m[k] = cc_[k]
            m.update(ccore[cc])
            in_maps.append(m)
        resB2 = _run("B2", in_maps)
        wl = {k: f32(np.asarray(v)[l]) for k, v in (("conv_w", conv_w), ("w_conv_out", w_conv_out), ("w_nsa_out", w_nsa_out), ("w_out", w_out), ("w_mlp_up", w_mlp_up), ("w_mlp_down", w_mlp_down))}
        in_maps = []
        for cc in range(NCORES):
            m = {"xT": xT[cc], "bgT": resA[cc]["bgT"], "uT": resA[cc]["uT"], "utail": utails[cc], "attnT": resB2[cc]["attnT"],
                 "gaT": resA[cc]["gaT"], "gbT": resA[cc]["gbT"], "modT": modT, "gainsT": gainsT}
            m.update(wl)
            in_maps.append(m)
        resB3 = _run("B3", in_maps)
        xT = [resB3[cc]["xT_out"] for cc in range(NCORES)]
    out = np.zeros((SEQ, D), np.float32)
    for cc in range(NCORES):
        out[toks[cc]] = xT[cc].T
    return out.reshape(1, SEQ, D)
```

```python
import math
from contextlib import ExitStack
import numpy as np
import ml_dtypes
import concourse.bass as bass
import concourse.mybir as mybir
from concourse.bass_utils import run_bass_kernel_spmd


F32 = mybir.dt.float32
BF16 = mybir.dt.bfloat16
I32 = mybir.dt.int32
AF = mybir.ActivationFunctionType
ALU = mybir.AluOpType
AX = mybir.AxisListType


class T:
    def __init__(self, name, h):
        self.name = name
        self.h = h
        self.w = None
        self.r = []
        self.sem = None
        self.dcnt = 0
        self.multi = False
        self.ws = []

    def __getitem__(self, k):
        return self.h[k]


class Sched:
    ENG = ("pe", "act", "dve", "pool", "sp")

    def __init__(self, nc, es, same_engine_sync=True):
        self.nc = nc
        self.es = es
        self.ops = {e: [] for e in self.ENG}
        self.cnt = {e: 0 for e in self.ENG}
        self.esem = {e: es.enter_context(nc.semaphore("sem_" + e)) for e in self.ENG}
        self.waited = {e: {} for e in self.ENG}
        self.same = same_engine_sync
        self.nsem = 5
        self.ninst = 0

    def sb(self, name, shape, dt):
        return T(name, self.es.enter_context(self.nc.sbuf_tensor("sb_" + name, list(shape), dt)))

    def ps(self, name, shape, dt=F32):
        return T(name, self.es.enter_context(self.nc.psum_tensor("ps_" + name, list(shape), dt)))

    def dram(self, name, shape, dt, kind="Internal"):
        t = T(name, self.nc.dram_tensor(name, list(shape), dt, kind=kind).ap())
        t.multi = True
        return t

    def _deps(self, eng, reads, writes):
        deps = []
        for t in reads:
            if t.w is not None:
                deps.append(t.w)
            deps.extend(t.ws)
        for t in writes:
            if not t.multi:
                if t.w is not None:
                    deps.append(t.w)
            deps.extend(t.r)
        waits = []
        wd = self.waited[eng]
        own = self.esem[eng]
        best = {}
        for (sem, val) in deps:
            if sem is own and (not self.same or eng == "pe"):
                continue
            if wd.get(id(sem), 0) >= val:
                continue
            if id(sem) not in best or best[id(sem)][1] < val:
                best[id(sem)] = (sem, val)
        for k, (sem, val) in best.items():
            wd[k] = val
            waits.append((sem, val))
        return waits

    def _stamp(self, stamp, reads, writes):
        for t in writes:
            if t.multi:
                t.ws.append(stamp)
                if len(t.ws) > 12:
                    t.ws = self._compact(t.ws)
                continue
            t.w = stamp
            t.r = []
        for t in reads:
            if t in writes:
                continue
            t.r.append(stamp)
            if len(t.r) > 12:
                t.r = self._compact(t.r)

    @staticmethod
    def _compact(lst):
        m = {}
        for (s, v) in lst:
            if id(s) not in m or m[id(s)][1] < v:
                m[id(s)] = (s, v)
        return list(m.values())

    def op(self, eng, fn, reads=(), writes=()):
        waits = self._deps(eng, reads, writes)
        self.cnt[eng] += 1
        stamp = (self.esem[eng], self.cnt[eng])
        self.ops[eng].append((fn, waits, (self.esem[eng], 1)))
        self._stamp(stamp, reads, writes)
        self.ninst += 1

    def dma(self, q, fn, semt, reads=(), writes=()):
        waits = self._deps(q, reads, writes)
        if semt.sem is None:
            semt.sem = self.es.enter_context(self.nc.semaphore("dsem_" + semt.name))
            self.nsem += 1
        semt.dcnt += 16
        stamp = (semt.sem, semt.dcnt)
        self.ops[q].append((fn, waits, (semt.sem, 16)))
        self._stamp(stamp, reads, writes)
        self.ninst += 1

    def final_wait(self, eng, tiles):
        deps = []
        for t in tiles:
            if t.w is not None:
                deps.append(t.w)
            deps.extend(t.ws)
        self.ops[eng].append((None, self._compact(deps), None))

    def emit(self):
        nc = self.nc
        emap = {"pe": "tensor", "act": "scalar", "dve": "vector", "pool": "gpsimd", "sp": "sync"}
        with nc.Block() as block:
            for e in self.ENG:
                ops = self.ops[e]

                def body(engine, ops=ops):
                    for (fn, waits, inc) in ops:
                        for (sem, val) in waits:
                            engine.wait_ge(sem, val)
                        if fn is not None:
                            ins = fn(engine)
                            ins.then_inc(inc[0], inc[1])
                getattr(block, emap[e])(body)


D = 2048; TC = 2048; NT = 512; NTILE = 4; KC = 16
OFF_BG, OFF_CG, OFF_XA, OFF_Q, OFF_KV, OFF_GN, OFF_GA, OFF_GB = 0, 1024, 2048, 3072, 5120, 8192, 8240, 10288
INW = 12336
EPS = 1e-6
TWO_PI = 2.0 * math.pi


class Pool:
    def __init__(self, S, name, n, shape, dt, psum=False):
        self.t = [(S.ps if psum else S.sb)("%s%d" % (name, i), shape, dt) for i in range(n)]
        self.i = 0

    def get(self):
        t = self.t[self.i % len(self.t)]
        self.i += 1
        return t


def load_x_tile(S, xT_d, xt, ti):
    src = xT_d.h.rearrange("(kc p) t -> p kc t", p=128)
    for q4 in range(4):
        S.dma("sp", lambda e, q4=q4: e.dma_start(out=xt[:, q4 * 4:(q4 + 1) * 4, :], in_=src[:, q4 * 4:(q4 + 1) * 4, ti * NT:(ti + 1) * NT]),
              xt, reads=[xT_d], writes=[xt])


def rms_affine(S, xt, hT, tcol0, a_t, b_t, ones_b, P):
    ss = P["ps"].get()
    for kc in range(KC):
        s = P["sq"].get()
        S.op("act", lambda e, s=s, kc=kc: e.activation(out=s[:], in_=xt[:, kc, :], func=AF.Square), reads=[xt], writes=[s])
        S.op("pe", lambda e, s=s, kc=kc: e.matmul(ss[:], ones_b[:], s[:], start=(kc == 0), stop=(kc == KC - 1)), reads=[s, ones_b], writes=[ss])
    r = P["rstd"].get()
    S.op("act", lambda e: e.activation(out=r[:], in_=ss[:], func=AF.Sqrt, bias=P["eps"][:], scale=1.0 / D), reads=[ss, P["eps"]], writes=[r])
    S.op("dve", lambda e: e.reciprocal(out=r[:], in_=r[:]), reads=[r], writes=[r])
    for kc in range(KC):
        t = P["tmp"].get()
        S.op("dve", lambda e, t=t, kc=kc: e.scalar_tensor_tensor(out=t[:], in0=xt[:, kc, :], scalar=a_t[:, kc:kc + 1], in1=r[:], op0=ALU.mult, op1=ALU.mult), reads=[xt, a_t, r], writes=[t])
        S.op("act", lambda e, t=t, kc=kc: e.activation(out=hT[:, kc, tcol0:tcol0 + NT], in_=t[:], func=AF.Identity, bias=b_t[:, kc:kc + 1], scale=1.0), reads=[t, b_t], writes=[hT])


def build_A(S, io, L):
    nc = S.nc
    xT_d = io["xT"]
    ones_b = S.sb("ones_b", [128, 128], BF16)
    S.op("dve", lambda e: e.memset(ones_b[:], 1.0), writes=[ones_b])
    modT = S.sb("modT", [128, 96], F32)
    gT = S.sb("gT", [128, 64], F32)
    S.dma("sp", lambda e: e.dma_start(out=modT[:], in_=io["modT"][:]), modT, reads=[io["modT"]], writes=[modT])
    S.dma("sp", lambda e: e.dma_start(out=gT[:], in_=io["gainsT"][:]), gT, reads=[io["gainsT"]], writes=[gT])
    a1 = S.sb("a1", [128, 16], F32)
    S.op("dve", lambda e: e.scalar_tensor_tensor(out=a1[:], in0=modT[:, 16:32], scalar=1.0, in1=gT[:, 0:16], op0=ALU.add, op1=ALU.mult), reads=[modT, gT], writes=[a1])
    b1 = S.sb("b1", [128, 16], F32)
    S.op("dve", lambda e: e.tensor_copy(out=b1[:], in_=modT[:, 0:16]), reads=[modT], writes=[b1])
    posi = S.sb("posi", [32, TC], I32)
    S.dma("sp", lambda e: e.dma_start(out=posi[:], in_=io["pos"][:]), posi, reads=[io["pos"]], writes=[posi])
    invf = S.sb("invf", [32, 2], F32)
    S.dma("sp", lambda e: e.dma_start(out=invf[:], in_=io["invf"][:]), invf, reads=[io["invf"]], writes=[invf])
    posf = S.sb("posf", [32, TC], F32)
    S.op("dve", lambda e: e.tensor_copy(out=posf[:], in_=posi[:]), reads=[posi], writes=[posf])
    ang = S.sb("ang", [32, TC], F32)
    S.op("dve", lambda e: e.tensor_scalar(out=ang[:], in0=posf[:], scalar1=invf[:, 0:1], scalar2=None, op0=ALU.mult), reads=[posf, invf], writes=[ang])
    Ct = S.sb("Ct", [32, TC], F32)
    Sn = S.sb("Snt", [32, TC], F32)
    C1 = 6.28125
    C2 = TWO_PI - C1
    yk = posf
    ki = posi
    S.op("dve", lambda e: e.tensor_scalar(out=yk[:], in0=ang[:], scalar1=1.0 / TWO_PI, scalar2=None, op0=ALU.mult), reads=[ang], writes=[yk])
    S.op("dve", lambda e: e.tensor_copy(out=ki[:], in_=yk[:]), reads=[yk], writes=[ki])
    S.op("dve", lambda e: e.tensor_copy(out=yk[:], in_=ki[:]), reads=[ki], writes=[yk])
    S.op("dve", lambda e: e.scalar_tensor_tensor(out=ang[:], in0=yk[:], scalar=-C1, in1=ang[:], op0=ALU.mult, op1=ALU.add), reads=[yk, ang], writes=[ang])
    S.op("dve", lambda e: e.scalar_tensor_tensor(out=ang[:], in0=yk[:], scalar=-C2, in1=ang[:], op0=ALU.mult, op1=ALU.add), reads=[yk, ang], writes=[ang])
    S.op("dve", lambda e: e.tensor_scalar(out=Sn[:], in0=ang[:], scalar1=-math.pi, scalar2=math.pi, op0=ALU.max, op1=ALU.min), reads=[ang], writes=[Sn])
    S.op("act", lambda e: e.activation(out=Sn[:], in_=Sn[:], func=AF.Sin), reads=[Sn], writes=[Sn])
    S.op("dve", lambda e: e.tensor_scalar(out=Sn[:], in0=Sn[:], scalar1=invf[:, 1:2], scalar2=None, op0=ALU.mult), reads=[Sn, invf], writes=[Sn])
    S.op("dve", lambda e: e.tensor_single_scalar(out=yk[:], in_=ang[:], scalar=math.pi / 2, op=ALU.is_gt), reads=[ang], writes=[yk])
    S.op("dve", lambda e: e.scalar_tensor_tensor(out=Ct[:], in0=yk[:], scalar=-TWO_PI, in1=ang[:], op0=ALU.mult, op1=ALU.add), reads=[yk, ang], writes=[Ct])
    S.op("dve", lambda e: e.tensor_scalar(out=Ct[:], in0=Ct[:], scalar1=math.pi / 2, scalar2=math.pi, op0=ALU.add, op1=ALU.min), reads=[Ct], writes=[Ct])
    S.op("dve", lambda e: e.tensor_scalar(out=Ct[:], in0=Ct[:], scalar1=-math.pi, scalar2=None, op0=ALU.max), reads=[Ct], writes=[Ct])
    S.op("act", lambda e: e.activation(out=Ct[:], in_=Ct[:], func=AF.Sin), reads=[Ct], writes=[Ct])

    hT = S.sb("hT", [128, KC, TC], BF16)
    xts = Pool(S, "xt", 1, [128, KC, NT], F32)
    P = {"ps": Pool(S, "ps", 8, [128, 512], F32, psum=True), "sq": Pool(S, "sq", 3, [128, NT], BF16),
         "rstd": Pool(S, "rstd", 2, [128, NT], F32), "tmp": Pool(S, "tmp", 3, [128, NT], F32)}
    P["eps"] = S.sb("epsc", [128, 1], F32)
    S.op("dve", lambda e: e.memset(P["eps"][:], EPS), writes=[P["eps"]])
    for ti in range(NTILE):
        xt = xts.get()
        load_x_tile(S, xT_d, xt, ti)
        rms_affine(S, xt, hT, ti * NT, a1, b1, ones_b, P)

    wts = Pool(S, "wt", 2, [128, KC, 512], BF16)
    stg = Pool(S, "stg", 4, [128, NT], F32)
    stb = Pool(S, "stb", 3, [128, NT], BF16)
    swp = Pool(S, "swp", 2, [32, NT], F32)
    rt2 = Pool(S, "rt2", 2, [32, NT], F32)
    w_in = io["w_in"]
    wsrc = w_in.h[L].rearrange("(kc p) n -> p kc n", p=128)

    def load_w(blocks):
        wt = wts.get()
        o = 0
        for (c0, w) in blocks:
            for half in range(2):
                S.dma("pool", lambda e, wt=wt, o=o, c0=c0, w=w, half=half: e.dma_start(out=wt[:, half * 8:(half + 1) * 8, o:o + w], in_=wsrc[:, half * 8:(half + 1) * 8, c0:c0 + w]),
                      wt, reads=[w_in], writes=[wt])
            o += w
        return wt

    def mm_fm(wt, off, M, ti):
        ps = P["ps"].get()
        for kc in range(KC):
            S.op("pe", lambda e, kc=kc: e.matmul(ps[:M, :], wt[:, kc, off:off + M], hT[:, kc, ti * NT:(ti + 1) * NT], start=(kc == 0), stop=(kc == KC - 1)), reads=[wt, hT], writes=[ps])
        return ps

    def mm_tm(wt, W, tb):
        ps = P["ps"].get()
        for kc in range(KC):
            S.op("pe", lambda e, kc=kc: e.matmul(ps[:, :W], hT[:, kc, tb * 128:(tb + 1) * 128], wt[:, kc, 0:W], start=(kc == 0), stop=(kc == KC - 1)), reads=[wt, hT], writes=[ps])
        return ps

    def store(dst_d, dst_ap, src_t, src_ap):
        S.dma("sp", lambda e: e.dma_start(out=dst_ap, in_=src_ap), src_t, reads=[src_t], writes=[dst_d])

    def rope_store(ps, dst_d, dst_ap, ti, do_rope=True):
        st = stg.get()
        S.op("act", lambda e: e.activation(out=st[:], in_=ps[:], func=AF.Copy), reads=[ps], writes=[st])
        if do_rope:
            sw = swp.get()
            S.dma("sp", lambda e: e.dma_start(out=sw[0:16, :], in_=st[16:32, :]), sw, reads=[st], writes=[sw])
            S.dma("sp", lambda e: e.dma_start(out=sw[16:32, :], in_=st[0:16, :]), sw, reads=[st], writes=[sw])
            t2 = rt2.get()
            tsl = slice(ti * NT, (ti + 1) * NT)
            S.op("dve", lambda e: e.tensor_tensor(out=t2[:], in0=sw[:], in1=Sn[:, tsl], op=ALU.mult), reads=[sw, Sn], writes=[t2])
            S.op("dve", lambda e: e.tensor_tensor(out=st[0:32, :], in0=st[0:32, :], in1=Ct[:, tsl], op=ALU.mult), reads=[st, Ct], writes=[st])
            S.op("dve", lambda e: e.tensor_tensor(out=st[0:32, :], in0=st[0:32, :], in1=t2[:], op=ALU.add), reads=[st, t2], writes=[st])
        sb = stb.get()
        S.op("act", lambda e: e.activation(out=sb[:], in_=st[:], func=AF.Copy), reads=[st], writes=[sb])
        store(dst_d, dst_ap, sb, sb[:])

    bgT, uT, qT, kT, vtok, gn, gaT, gbT = (io[k] for k in ("bgT", "uT", "qT", "kT", "vtok", "gn", "gaT", "gbT"))
    for jb in range(2):
        wt = load_w([(OFF_BG + jb * 512, 512)])
        for ti in range(NTILE):
            for c4 in range(4):
                ps = mm_fm(wt, c4 * 128, 128, ti)
                st = stg.get()
                S.op("act", lambda e, st=st, ps=ps: e.activation(out=st[:], in_=ps[:], func=AF.Copy), reads=[ps], writes=[st])
                ch = jb * 4 + c4
                store(bgT, bgT[ch * 128:(ch + 1) * 128, ti * NT:(ti + 1) * NT], st, st[:])
    for jb in range(4):
        wt = load_w([(OFF_CG + jb * 256, 256), (OFF_XA + jb * 256, 256)])
        for ti in range(NTILE):
            for c2 in range(2):
                pc = mm_fm(wt, c2 * 128, 128, ti)
                px = mm_fm(wt, 256 + c2 * 128, 128, ti)
                st = stg.get()
                S.op("act", lambda e, st=st, pc=pc: e.activation(out=st[:], in_=pc[:], func=AF.Copy), reads=[pc], writes=[st])
                S.op("dve", lambda e, st=st, px=px: e.tensor_tensor(out=st[:], in0=px[:], in1=st[:], op=ALU.mult), reads=[px, st], writes=[st])
                ch = jb * 2 + c2
                store(uT, uT[ch * 128:(ch + 1) * 128, ti * NT:(ti + 1) * NT], st, st[:])
    for jb in range(4):
        wt = load_w([(OFF_Q + jb * 512, 512)])
        for ti in range(NTILE):
            for c4 in range(4):
                ps = mm_fm(wt, c4 * 128, 128, ti)
                hd = jb * 4 + c4
                rope_store(ps, qT, qT[hd * 128:(hd + 1) * 128, ti * NT:(ti + 1) * NT], ti)
    for (kt_i, kvi, rope) in ((0, 0, True), (1, 1, False), (2, 2, True), (3, 4, True)):
        wt = load_w([(OFF_KV + kvi * 512, 512)])
        for ti in range(NTILE):
            for g in range(4):
                ps = mm_fm(wt, g * 128, 128, ti)
                rope_store(ps, kT, kT[kt_i, g, :, ti * NT:(ti + 1) * NT], ti, do_rope=rope)
    for (vt_i, kvi) in ((0, 3), (1, 5)):
        wt = load_w([(OFF_KV + kvi * 512, 512)])
        for tb in range(TC // 128):
            ps = mm_tm(wt, 512, tb)
            sb = stb.get()
            S.op("act", lambda e, sb=sb, ps=ps: e.activation(out=sb[:], in_=ps[:], func=AF.Copy), reads=[ps], writes=[sb])
            store(vtok, vtok[vt_i, tb * 128:(tb + 1) * 128, :], sb, sb[:])
    wt = load_w([(OFF_GN, 48)])
    for tb in range(TC // 128):
        ps = mm_tm(wt, 48, tb)
        st = stg.get()
        S.op("act", lambda e, st=st, ps=ps: e.activation(out=st[:, 0:48], in_=ps[:, 0:48], func=AF.Sigmoid), reads=[ps], writes=[st])
        store(gn, gn[tb * 128:(tb + 1) * 128, :], st, st[:, 0:48])
    for (dst, off) in ((gaT, OFF_GA), (gbT, OFF_GB)):
        for jb in range(4):
            wt = load_w([(off + jb * 512, 512)])
            for ti in range(NTILE):
                for c4 in range(4):
                    ps = mm_fm(wt, c4 * 128, 128, ti)
                    st = stg.get()
                    S.op("act", lambda e, st=st, ps=ps: e.activation(out=st[:], in_=ps[:], func=AF.Sigmoid), reads=[ps], writes=[st])
                    ch = jb * 4 + c4
                    store(dst, dst[ch * 128:(ch + 1) * 128, ti * NT:(ti + 1) * NT], st, st[:])
    S.final_wait("sp", [bgT, uT, qT, kT, vtok, gn, gaT, gbT])


SEQ = 16384
NKT = 128
SCALE = 128 ** -0.5
NSLOT = 16
LAG = 2
GELU_C = 2.0 * math.sqrt(2.0 / math.pi)


def build_B2(S, io, slots=None, groups=None):
    slots = list(range(NSLOT)) if slots is None else slots
    groups = list(range(4)) if groups is None else groups
    Kfull, Vfull, qT, gn_d, attnT = io["Kfull"], io["Vfull"], io["qT"], io["gn"], io["attnT"]
    def const(name, shape, dt, src_ap, src_t, q="sp"):
        t = S.sb(name, shape, dt)
        S.dma(q, lambda e: e.dma_start(out=t[:], in_=src_ap), t, reads=[src_t], writes=[t])
        return t
    Ov = const("Ov", [128, 8, 256], BF16, io["Ov"][:], io["Ov"])
    Eall = const("Eall", [128, 64, 128], BF16, io["Eall"][:], io["Eall"])
    DB4 = const("DB4", [128, 8, 512], BF16, io["DB4"][:], io["DB4"])
    WB4 = const("WB4", [128, 12, 512], BF16, io["WB4"][:], io["WB4"])
    tidxrow = const("tidxrow", [128, TC], F32, io["tidxrow"][:], io["tidxrow"])
    curcol = const("curcol", [128, 16], F32, io["curcol"][:], io["curcol"])
    nthr = const("nthr", [128, 8], F32, io["nthr"][:], io["nthr"])
    blkrow = const("blkrow", [128, 256], F32, io["blkrow"][:], io["blkrow"])
    baserow = const("baserow", [128, 256], F32, io["baserow"][:], io["baserow"])
    gn = const("gnsb", [128, NSLOT, 48], F32, gn_d.h.rearrange("(j p) c -> p j c", p=128), gn_d)
    ident = S.sb("ident", [128, 128], BF16)
    S.op("dve", lambda e: e.memset(ident[:], 0.0), writes=[ident])
    S.op("pool", lambda e: e.affine_select(out=ident[:], in_=ident[:], pattern=[[-1, 128]], compare_op=ALU.not_equal, fill=1.0, base=0, channel_multiplier=1), reads=[ident], writes=[ident])

    bigK = S.sb("bigK", [128, SEQ + 32], BF16)
    vsa = S.sb("vsa", [128, NKT, 129], BF16)
    S.op("dve", lambda e: e.memset(bigK[:, SEQ:SEQ + 32], 0.0), writes=[bigK])
    S.op("dve", lambda e: e.memset(vsa[:, :, 128:129], 1.0), writes=[vsa])
    kcT = S.sb("kcT", [128, 4, 1024], BF16)
    vca = S.sb("vca", [128, 4, 8, 129], BF16)
    S.op("dve", lambda e: e.memset(vca[:, :, :, 128:129], 1.0), writes=[vca])

    psS = Pool(S, "psS", 3, [128, 512], F32, psum=True)
    psO = [S.ps("psO%d" % i, [128, 512], F32) for i in range(2)]
    psI = [S.ps("psI%d" % i, [128, 512], F32) for i in range(2)]
    psT = S.ps("psT", [128, 1024], BF16)

    qw = S.sb("qw", [128, 8192], BF16)
    class _V:
        def __init__(self, tt, pat, **kw):
            self.t = tt; self.pat = pat; self.kw = kw
        def __getitem__(self, k):
            return self.t[:].rearrange(self.pat, **self.kw)[k]
    w1v = _V(qw, "p (l h) -> p l h", h=256)
    QTv = _V(qw, "p (r t) -> p r t", t=TC)
    w2b = S.sb("w2b", [128, 2, 128], BF16)
    peT = S.sb("peT", [128, 32], BF16)
    b1T = S.sb("b1T", [128, 2], F32)
    b2c = S.sb("b2c", [128, 1], F32)
    b2row = S.sb("b2row", [128, 128], F32)
    cb = S.sb("cb", [128, 2], F32)
    Hg = S.sb("Hg", [128, 2, 1024], BF16)
    hs = Pool(S, "hs", 1, [128, 512], F32)
    hx = Pool(S, "hx", 1, [128, 512], F32)
    for ty in range(2):
        S.dma("pool", lambda e, ty=ty: e.dma_start(out=w1v[:, 0:16, :], in_=io["cmp_w1"].h[ty].rearrange("(l d) h -> d l h", d=128)[:, 0:16, :]), qw, reads=[io["cmp_w1"]], writes=[qw])
        S.dma("pool", lambda e, ty=ty: e.dma_start(out=w1v[:, 16:32, :], in_=io["cmp_w1"].h[ty].rearrange("(l d) h -> d l h", d=128)[:, 16:32, :]), qw, reads=[io["cmp_w1"]], writes=[qw])
        S.dma("pool", lambda e, ty=ty: e.dma_start(out=w2b[:], in_=io["cmp_w2"].h[ty].rearrange("(hc p) d -> p hc d", p=128)), w2b, reads=[io["cmp_w2"]], writes=[w2b])
        S.dma("pool", lambda e, ty=ty: e.dma_start(out=peT[:], in_=io["cmp_pe"].h[ty].rearrange("l d -> d l"), allow_slow_non_contiguous=True), peT, reads=[io["cmp_pe"]], writes=[peT])
        S.dma("sp", lambda e, ty=ty: e.dma_start(out=b1T[:], in_=io["cmp_b1"].h[ty].rearrange("(hc p) -> p hc", p=128), allow_slow_non_contiguous=True), b1T, reads=[io["cmp_b1"]], writes=[b1T])
        S.dma("sp", lambda e, ty=ty: e.dma_start(out=b2c[:], in_=io["cmp_b2"].h[ty].rearrange("(d o) -> d o", o=1), allow_slow_non_contiguous=True), b2c, reads=[io["cmp_b2"]], writes=[b2c])
        S.dma("sp", lambda e, ty=ty: e.dma_start(out=b2row[:], in_=io["cmp_b2"].h[ty:ty + 1, :].partition_broadcast(128)), b2row, reads=[io["cmp_b2"]], writes=[b2row])
        for hc in range(2):
            ps = psS.get()
            for l in range(32):
                S.op("pe", lambda e, ps=ps, l=l, hc=hc: e.matmul(ps[:, 0:1], w1v[:, l, hc * 128:(hc + 1) * 128], peT[:, l:l + 1], start=(l == 0), stop=(l == 31)), reads=[qw, peT], writes=[ps])
            S.op("dve", lambda e, ps=ps, hc=hc: e.tensor_tensor(out=cb[:, hc:hc + 1], in0=ps[:, 0:1], in1=b1T[:, hc:hc + 1], op=ALU.add), reads=[ps, b1T], writes=[cb])
        for g in range(4):
            for hf in range(2):
                S.dma("sp", lambda e, ty=ty, g=g, hf=hf: e.dma_start(out=bigK[:, hf * 8192:(hf + 1) * 8192], in_=Kfull.h[ty, g, :, hf * 8192:(hf + 1) * 8192]), bigK, reads=[Kfull], writes=[bigK])
            for nh in range(2):
                for hc in range(2):
                    ps = psS.get()
                    for l in range(32):
                        a0 = nh * 512 * 16 + l
                        S.op("pe", lambda e, ps=ps, l=l, hc=hc, a0=a0: e.matmul(ps[:], w1v[:, l, hc * 128:(hc + 1) * 128], bigK[:, a0:a0 + 512 * 16:16], start=(l == 0), stop=(l == 31)), reads=[qw, bigK], writes=[ps])
                    h = hs.get(); x2 = hx.get()
                    S.op("act", lambda e, ps=ps, h=h, hc=hc: e.activation(out=h[:], in_=ps[:], func=AF.Identity, bias=cb[:, hc:hc + 1], scale=1.0), reads=[ps, cb], writes=[h])
                    S.op("dve", lambda e, h=h, x2=x2: e.tensor_tensor(out=x2[:], in0=h[:], in1=h[:], op=ALU.mult), reads=[h], writes=[x2])
                    S.op("dve", lambda e, x2=x2: e.tensor_scalar(out=x2[:], in0=x2[:], scalar1=0.044715, scalar2=1.0, op0=ALU.mult, op1=ALU.add), reads=[x2], writes=[x2])
                    S.op("dve", lambda e, h=h, x2=x2: e.tensor_tensor(out=x2[:], in0=x2[:], in1=h[:], op=ALU.mult), reads=[h, x2], writes=[x2])
                    S.op("act", lambda e, x2=x2: e.activation(out=x2[:], in_=x2[:], func=AF.Sigmoid, scale=GELU_C), reads=[x2], writes=[x2])
                    S.op("dve", lambda e, h=h, x2=x2, hc=hc, nh=nh: e.tensor_tensor(out=Hg[:, hc, nh * 512:(nh + 1) * 512], in0=x2[:], in1=h[:], op=ALU.mult), reads=[h, x2], writes=[Hg])
            if ty == 0:
                for nh in range(2):
                    ps = psS.get()
                    for hc in range(2):
                        S.op("pe", lambda e, ps=ps, hc=hc, nh=nh: e.matmul(ps[:], w2b[:, hc, :], Hg[:, hc, nh * 512:(nh + 1) * 512], start=(hc == 0), stop=(hc == 1)), reads=[w2b, Hg], writes=[ps])
                    S.op("act", lambda e, ps=ps, g=g, nh=nh: e.activation(out=kcT[:, g, nh * 512:(nh + 1) * 512], in_=ps[:], func=AF.Identity, bias=b2c[:], scale=1.0), reads=[ps, b2c], writes=[kcT])
            else:
                for ncn in range(8):
                    ps = psS.get()
                    for hc in range(2):
                        S.op("pe", lambda e, ps=ps, hc=hc, ncn=ncn: e.matmul(ps[:, 0:128], Hg[:, hc, ncn * 128:(ncn + 1) * 128], w2b[:, hc, :], start=(hc == 0), stop=(hc == 1)), reads=[w2b, Hg], writes=[ps])
                    S.op("dve", lambda e, ps=ps, g=g, ncn=ncn: e.tensor_tensor(out=vca[:, g, ncn, 0:128], in0=ps[:, 0:128], in1=b2row[:], op=ALU.add), reads=[ps, b2row], writes=[vca])

    kws = Pool(S, "kws", 2, [128, 12, 128], BF16)
    vws = Pool(S, "vws", 2, [128, 12, 129], BF16)
    for t in vws.t:
        S.op("dve", lambda e, t=t: e.memset(t[:, :, 128:129], 1.0), writes=[t])
    PTs = Pool(S, "PT", 4, [128, 4, 128], BF16)
    Osb = Pool(S, "Osb", 2, [128, 4, 129], F32)
    oacc = Pool(S, "oacc", 2, [128, 4, 128], F32)
    obf = Pool(S, "obf", 2, [128, 4, 128], BF16)
    aT = Pool(S, "aT", 2, [128, 4, 128], BF16)
    NBT4 = Pool(S, "NBT4", 2, [128, 2, 512], BF16)
    cm = Pool(S, "cm", 2, [128, 128], BF16)
    small = Pool(S, "small", 8, [128, 8], F32)
    impp = Pool(S, "imp", 2, [128, 256], F32)
    vv = Pool(S, "vv", 2, [128, 256], F32)
    ff = Pool(S, "ff", 2, [128, 256], F32)
    wk = Pool(S, "wk", 2, [128, 256], F32)
    nbb = Pool(S, "nbb", 2, [128, 256], BF16)
    m8p = Pool(S, "m8", 4, [128, 8], F32)

    def Oview(r):
        return psO[r // 2][:, (r % 2) * 129:(r % 2) * 129 + 129]

    def finish_branch(b, g, j, oa, first):
        osb = Osb.get()
        for k in range(2):
            S.op("act", lambda e, k=k: e.activation(out=osb[:, 2 * k:2 * k + 2, :], in_=psO[k][:, 0:258].rearrange("p (r c) -> p r c", c=129), func=AF.Copy), reads=[psO[k]], writes=[osb])
        if "dbg" in io:
            S.dma("sp", lambda e: e.dma_start(out=io["dbg"].h[b, j * 128:(j + 1) * 128, :], in_=osb[:].rearrange("p r c -> p (r c)")), osb, reads=[osb], writes=[io["dbg"]])
        sm = small.get()
        S.op("dve", lambda e: e.tensor_scalar(out=sm[:, 0:4], in0=osb[:, :, 128], scalar1=1e-30, scalar2=None, op0=ALU.max), reads=[osb], writes=[sm])
        S.op("dve", lambda e: e.reciprocal(out=sm[:, 0:4], in_=sm[:, 0:4]), reads=[sm], writes=[sm])
        c0 = 4 * g * 3 + b
        S.op("dve", lambda e: e.tensor_tensor(out=sm[:, 4:8], in0=sm[:, 0:4], in1=gn[:, j, c0:c0 + 10:3], op=ALU.mult), reads=[sm, gn], writes=[sm])
        for r in range(4):
            if first:
                S.op("dve", lambda e, r=r: e.tensor_scalar(out=oa[:, r, :], in0=osb[:, r, 0:128], scalar1=sm[:, 4 + r:5 + r], scalar2=None, op0=ALU.mult), reads=[osb, sm], writes=[oa])
            else:
                S.op("dve", lambda e, r=r: e.scalar_tensor_tensor(out=oa[:, r, :], in0=osb[:, r, 0:128], scalar=sm[:, 4 + r:5 + r], in1=oa[:, r, :], op0=ALU.mult, op1=ALU.add), reads=[osb, sm, oa], writes=[oa])
        return sm

    def pv(PT, vaug_t, vaug_ap, first, last):
        for r in range(4):
            S.op("pe", lambda e, r=r: e.matmul(Oview(r), PT[:, r, :], vaug_ap, start=(first and r % 2 == 0), stop=last, skip_group_check=True), reads=[PT, vaug_t], writes=[psO[r // 2]])

    def part1(g, j):
        tsl = slice(j * 128, (j + 1) * 128)
        oa = oacc.get()
        ncn = j // 2 + 1
        pend = []
        for nci in range(ncn):
            ps = psS.get()
            S.op("pe", lambda e, ps=ps, nci=nci, g=g: e.matmul(ps[:].rearrange("p (r t) -> p r t", t=128), kcT[:, g, nci * 128:(nci + 1) * 128], QTv[:, :, tsl], start=True, stop=True), reads=[kcT, qw], writes=[ps])
            PT = PTs.get()
            S.op("act", lambda e, ps=ps, PT=PT: e.activation(out=PT[:], in_=ps[:].rearrange("p (r t) -> p r t", t=128), func=AF.Exp, scale=SCALE), reads=[ps], writes=[PT])
            c = cm.get()
            S.op("dve", lambda e, c=c, nci=nci: e.tensor_scalar(out=c[:], in0=tidxrow[:, tsl], scalar1=nthr[:, nci:nci + 1], scalar2=None, op0=ALU.is_ge), reads=[tidxrow, nthr], writes=[c])
            S.op("dve", lambda e, c=c, PT=PT: e.tensor_tensor(out=PT[:], in0=PT[:], in1=c[:].unsqueeze(1).broadcast_to([128, 4, 128]), op=ALU.mult), reads=[PT, c], writes=[PT])
            def tail_c(PT=PT, nci=nci):
                pv(PT, vca, vca[:, g, nci, :], nci == 0, nci == ncn - 1)
                for r in range(4):
                    S.op("pe", lambda e, r=r, PT=PT, nci=nci: e.matmul(psI[r // 2][:, (r % 2) * 256:(r % 2) * 256 + 256], PT[:, r, :], Ov[:, nci, :], start=(nci == 0 and r % 2 == 0), stop=(nci == ncn - 1), skip_group_check=True), reads=[PT, Ov], writes=[psI[r // 2]])
            pend.append(tail_c)
            if len(pend) > LAG:
                pend.pop(0)()
        while pend:
            pend.pop(0)()
        sm = finish_branch(0, g, j, oa, True)
        imp = impp.get()
        for r in range(4):
            src = psI[r // 2][:, (r % 2) * 256:(r % 2) * 256 + 256]
            if r == 0:
                S.op("dve", lambda e, src=src: e.tensor_scalar(out=imp[:], in0=src, scalar1=sm[:, 0:1], scalar2=None, op0=ALU.mult), reads=[psI[0], sm], writes=[imp])
            else:
                S.op("dve", lambda e, src=src, r=r: e.scalar_tensor_tensor(out=imp[:], in0=src, scalar=sm[:, r:r + 1], in1=imp[:], op0=ALU.mult, op1=ALU.add), reads=[psI[r // 2], sm, imp], writes=[imp])
        v = vv.get(); f = ff.get(); w = wk.get(); m8a = m8p.get(); m8b = m8p.get(); nb = nbb.get()
        S.op("dve", lambda e: e.tensor_scalar(out=v[:], in0=blkrow[:], scalar1=curcol[:, j:j + 1], scalar2=None, op0=ALU.is_le), reads=[blkrow, curcol], writes=[v])
        S.op("dve", lambda e: e.tensor_tensor(out=imp[:], in0=imp[:], in1=baserow[:], op=ALU.subtract), reads=[imp, baserow], writes=[imp])
        S.op("dve", lambda e: e.tensor_tensor(out=imp[:], in0=imp[:], in1=v[:], op=ALU.mult), reads=[imp, v], writes=[imp])
        S.op("dve", lambda e: e.tensor_tensor(out=imp[:], in0=imp[:], in1=baserow[:], op=ALU.add), reads=[imp, baserow], writes=[imp])
        S.op("dve", lambda e: e.tensor_scalar(out=f[:], in0=blkrow[:], scalar1=curcol[:, j:j + 1], scalar2=1e30, op0=ALU.is_equal, op1=ALU.mult), reads=[blkrow, curcol], writes=[f])
        S.op("dve", lambda e: e.tensor_tensor(out=imp[:], in0=imp[:], in1=f[:], op=ALU.max), reads=[imp, f], writes=[imp])
        S.op("dve", lambda e: e.memset(imp[:, 0:1], 2e30), reads=[imp], writes=[imp])
        S.op("dve", lambda e: e.max(out=m8a[:], in_=imp[:]), reads=[imp], writes=[m8a])
        S.op("dve", lambda e: e.match_replace(out=w[:], in_to_replace=m8a[:], in_values=imp[:], imm_value=-1e30), reads=[imp, m8a], writes=[w])
        S.op("dve", lambda e: e.max(out=m8b[:], in_=w[:]), reads=[w], writes=[m8b])
        S.op("dve", lambda e: e.tensor_scalar(out=f[:], in0=imp[:], scalar1=m8b[:, 7:8], scalar2=None, op0=ALU.is_ge), reads=[imp, m8b], writes=[f])
        S.op("dve", lambda e: e.tensor_tensor(out=f[:], in0=f[:], in1=v[:], op=ALU.mult), reads=[f, v], writes=[f])
        if "dbg2" in io:
            S.dma("sp", lambda e: e.dma_start(out=io["dbg2"].h[j * 128:(j + 1) * 128, :], in_=f[:]), f, reads=[f], writes=[io["dbg2"]])
        S.op("dve", lambda e: e.tensor_scalar(out=nb[:], in0=f[:], scalar1=-1.0, scalar2=30000.0, op0=ALU.add, op1=ALU.mult), reads=[f], writes=[nb])
        for c2 in range(2):
            S.op("pe", lambda e, c2=c2: e.transpose(psT[:, c2 * 128:(c2 + 1) * 128], nb[:, c2 * 128:(c2 + 1) * 128], ident[:]), reads=[nb, ident], writes=[psT])
        nbt = NBT4.get()
        for r in range(4):
            eng = "act" if r % 2 == 0 else "dve"
            if eng == "act":
                S.op("act", lambda e, r=r: e.activation(out=nbt[:, :, r * 128:(r + 1) * 128], in_=psT[:, 0:256].rearrange("p (c t) -> p c t", t=128), func=AF.Copy), reads=[psT], writes=[nbt])
            else:
                S.op("dve", lambda e, r=r: e.tensor_copy(out=nbt[:, :, r * 128:(r + 1) * 128], in_=psT[:, 0:256].rearrange("p (c t) -> p c t", t=128)), reads=[psT], writes=[nbt])
        return (oa, nbt)

    def part2(g, j, st):
        oa, nbt = st
        tsl = slice(j * 128, (j + 1) * 128)
        pend = []
        nkt = 8 * j + 8
        for kt in range(nkt):
            ps = psS.get()
            diag = kt >= 8 * j
            S.op("pe", lambda e, ps=ps, kt=kt: e.matmul(ps[:].rearrange("p (r t) -> p r t", t=128), bigK[:, kt * 128:(kt + 1) * 128], QTv[:, :, tsl], start=True, stop=False), reads=[bigK, qw], writes=[ps])
            S.op("pe", lambda e, ps=ps, kt=kt, diag=diag: e.matmul(ps[:], Eall[:, kt % 64, :], nbt[:, kt // 64, :], start=False, stop=(not diag)), reads=[Eall, nbt], writes=[ps])
            if diag:
                S.op("pe", lambda e, ps=ps, kt=kt: e.matmul(ps[:], ident[:], DB4[:, kt - 8 * j, :], start=False, stop=True), reads=[ident, DB4], writes=[ps])
            PT = PTs.get()
            S.op("act", lambda e, ps=ps, PT=PT: e.activation(out=PT[:], in_=ps[:].rearrange("p (r t) -> p r t", t=128), func=AF.Exp, scale=SCALE), reads=[ps], writes=[PT])
            def tail_s(PT=PT, kt=kt):
                pv(PT, vsa, vsa[:, kt, :], kt == 0, kt == nkt - 1)
            pend.append(tail_s)
            if len(pend) > LAG:
                pend.pop(0)()
        while pend:
            pend.pop(0)()
        finish_branch(1, g, j, oa, False)
        kw = kws.get(); vw = vws.get()
        m0 = 4 if j == 0 else 0
        k0 = 8 * j - 4 + m0
        nm = 12 - m0
        S.dma("sp", lambda e, g=g: e.dma_start(out=kw[:, m0:12, :], in_=Kfull.h[3, g, :, k0 * 128:(k0 + nm) * 128].rearrange("d (m k) -> d m k", k=128)), kw, reads=[Kfull], writes=[kw])
        S.dma("sp", lambda e, g=g: e.dma_start(out=vw[:, m0:12, 0:128], in_=Vfull.h[1, k0 * 128:(k0 + nm) * 128, g * 128:(g + 1) * 128].rearrange("(m p) d -> p m d", p=128)), vw, reads=[Vfull], writes=[vw])
        for m in range(m0, 12):
            ps = psS.get()
            S.op("pe", lambda e, ps=ps, m=m: e.matmul(ps[:].rearrange("p (r t) -> p r t", t=128), kw[:, m, :], QTv[:, :, tsl], start=True, stop=False), reads=[kw, qw], writes=[ps])
            S.op("pe", lambda e, ps=ps, m=m: e.matmul(ps[:], ident[:], WB4[:, m, :], start=False, stop=True), reads=[ident, WB4], writes=[ps])
            PT = PTs.get()
            S.op("act", lambda e, ps=ps, PT=PT: e.activation(out=PT[:], in_=ps[:].rearrange("p (r t) -> p r t", t=128), func=AF.Exp, scale=SCALE), reads=[ps], writes=[PT])
            def tail_w(PT=PT, m=m):
                pv(PT, vw, vw[:, m, :], m == m0, m == 11)
            pend.append(tail_w)
            if len(pend) > LAG:
                pend.pop(0)()
        while pend:
            pend.pop(0)()
        finish_branch(2, g, j, oa, False)
        ob = obf.get()
        S.op("act", lambda e: e.activation(out=ob[:], in_=oa[:], func=AF.Copy), reads=[oa], writes=[ob])
        for r in range(4):
            S.op("pe", lambda e, r=r: e.transpose(psT[:, 256 + r * 128:256 + (r + 1) * 128], ob[:, r, :], ident[:]), reads=[ob, ident], writes=[psT])
        at = aT.get()
        S.op("dve", lambda e: e.tensor_copy(out=at[:], in_=psT[:, 256:768].rearrange("p (r t) -> p r t", t=128)), reads=[psT], writes=[at])
        S.dma("sp", lambda e, g=g: e.dma_start(out=attnT.h[g * 512:(g + 1) * 512, tsl].rearrange("(r d) t -> d r t", d=128), in_=at[:]), at, reads=[at], writes=[attnT])


    for g in groups:
        for hf in range(2):
            S.dma("sp", lambda e, g=g, hf=hf: e.dma_start(out=bigK[:, hf * 8192:(hf + 1) * 8192], in_=Kfull.h[2, g, :, hf * 8192:(hf + 1) * 8192]), bigK, reads=[Kfull], writes=[bigK])
        for q8 in range(8):
            S.dma("sp", lambda e, g=g, q8=q8: e.dma_start(out=vsa[:, q8 * 16:(q8 + 1) * 16, 0:128], in_=Vfull.h[0, q8 * 2048:(q8 + 1) * 2048, g * 128:(g + 1) * 128].rearrange("(kt p) d -> p kt d", p=128)), vsa, reads=[Vfull], writes=[vsa])
        S.dma("sp", lambda e, g=g: e.dma_start(out=QTv[:], in_=qT.h[g * 512:(g + 1) * 512, :].rearrange("(r d) t -> d r t", d=128)), qw, reads=[qT], writes=[qw])
        st = part1(g, slots[0])
        for i, j in enumerate(slots):
            nxt = part1(g, slots[i + 1]) if i + 1 < len(slots) else None
            part2(g, j, st)
            st = nxt
    S.final_wait("sp", [attnT])


DFF = 8192


def build_B3(S, io):
    xT_d, bgT, uT, utail, attnT, gaT, gbT, xo = (io[k] for k in ("xT", "bgT", "uT", "utail", "attnT", "gaT", "gbT", "xT_out"))
    ones_b = S.sb("ones_b", [128, 128], BF16)
    S.op("dve", lambda e: e.memset(ones_b[:], 1.0), writes=[ones_b])
    modT = S.sb("modT", [128, 96], F32)
    gT = S.sb("gT", [128, 64], F32)
    S.dma("sp", lambda e: e.dma_start(out=modT[:], in_=io["modT"][:]), modT, reads=[io["modT"]], writes=[modT])
    S.dma("sp", lambda e: e.dma_start(out=gT[:], in_=io["gainsT"][:]), gT, reads=[io["gainsT"]], writes=[gT])
    cw = S.sb("cw", [128, 3, 8], F32)
    for k in range(3):
        S.dma("sp", lambda e, k=k: e.dma_start(out=cw[:, k, :], in_=io["conv_w"].h[k].rearrange("(cc p) -> p cc", p=128), allow_slow_non_contiguous=True), cw, reads=[io["conv_w"]], writes=[cw])
    ag1 = S.sb("ag1", [128, 16], F32); a2 = S.sb("a2", [128, 16], F32); b2 = S.sb("b2", [128, 16], F32); ag3 = S.sb("ag3", [128, 16], F32)
    S.op("dve", lambda e: e.tensor_tensor(out=ag1[:], in0=modT[:, 32:48], in1=gT[:, 16:32], op=ALU.mult), reads=[modT, gT], writes=[ag1])
    S.op("dve", lambda e: e.scalar_tensor_tensor(out=a2[:], in0=modT[:, 64:80], scalar=1.0, in1=gT[:, 32:48], op0=ALU.add, op1=ALU.mult), reads=[modT, gT], writes=[a2])
    S.op("dve", lambda e: e.tensor_copy(out=b2[:], in_=modT[:, 48:64]), reads=[modT], writes=[b2])
    S.op("dve", lambda e: e.tensor_tensor(out=ag3[:], in0=modT[:, 80:96], in1=gT[:, 48:64], op=ALU.mult), reads=[modT, gT], writes=[ag3])

    xt = S.sb("xt", [128, KC, NT], F32)
    ysb = S.sb("ysb", [128, KC, NT], F32)
    hid = S.sb("hid", [128, 64, NT], BF16)
    mg = S.sb("mg", [128, KC, NT], BF16)
    wts = Pool(S, "wt", 2, [128, KC, 512], BF16)
    P = {"ps": Pool(S, "ps", 8, [128, 512], F32, psum=True), "sq": Pool(S, "sq", 2, [128, NT], BF16),
         "rstd": Pool(S, "rstd", 1, [128, NT], F32), "tmp": Pool(S, "tmp", 3, [128, NT], F32)}
    P["eps"] = S.sb("epsc", [128, 1], F32)
    S.op("dve", lambda e: e.memset(P["eps"][:], EPS), writes=[P["eps"]])
    uext = Pool(S, "uext", 2, [128, 4, 130], F32)
    bgc = Pool(S, "bgc", 2, [128, NT], F32)
    zt = Pool(S, "zt", 2, [128, 4, 128], F32)
    gch = Pool(S, "gch", 4, [128, NT], F32)

    def load_w(wd, r0, nk, c0):
        wt = wts.get()
        src = wd.h[r0:r0 + nk * 128, c0:c0 + 512].rearrange("(kc p) n -> p kc n", p=128)
        hk = nk // 2
        for half in range(2):
            S.dma("pool", lambda e, half=half: e.dma_start(out=wt[:, half * hk:(half + 1) * hk, :], in_=src[:, half * hk:(half + 1) * hk, :]), wt, reads=[wd], writes=[wt])
        return wt

    def post_norm_residual(coef):
        ss = P["ps"].get()
        for kc in range(KC):
            s = P["sq"].get()
            S.op("act", lambda e, s=s, kc=kc: e.activation(out=s[:], in_=ysb[:, kc, :], func=AF.Square), reads=[ysb], writes=[s])
            S.op("pe", lambda e, s=s, kc=kc: e.matmul(ss[:], ones_b[:], s[:], start=(kc == 0), stop=(kc == KC - 1)), reads=[s, ones_b], writes=[ss])
        r = P["rstd"].get()
        S.op("act", lambda e: e.activation(out=r[:], in_=ss[:], func=AF.Sqrt, bias=P["eps"][:], scale=1.0 / D), reads=[ss, P["eps"]], writes=[r])
        S.op("dve", lambda e: e.reciprocal(out=r[:], in_=r[:]), reads=[r], writes=[r])
        for kc in range(KC):
            t = P["tmp"].get()
            S.op("dve", lambda e, t=t, kc=kc: e.scalar_tensor_tensor(out=t[:], in0=ysb[:, kc, :], scalar=coef[:, kc:kc + 1], in1=r[:], op0=ALU.mult, op1=ALU.mult), reads=[ysb, coef, r], writes=[t])
            S.op("pool", lambda e, t=t, kc=kc: e.tensor_tensor(out=xt[:, kc, :], in0=xt[:, kc, :], in1=t[:], op=ALU.add), reads=[xt, t], writes=[xt])

    def tile(ti):
        tsl = slice(ti * NT, (ti + 1) * NT)
        load_x_tile(S, xT_d, xt, ti)
        for q4 in range(4):
            S.dma("sp", lambda e, q4=q4: e.dma_start(out=hid[:, q4 * 4:(q4 + 1) * 4, :], in_=attnT.h[q4 * 512:(q4 + 1) * 512, tsl].rearrange("(hc p) t -> p hc t", p=128)), hid, reads=[attnT], writes=[hid])
        for cc in range(8):
            ue = uext.get(); bg = bgc.get(); z = zt.get()
            S.dma("sp", lambda e, ue=ue, cc=cc: e.dma_start(out=ue[:, :, 2:130], in_=uT.h[cc * 128:(cc + 1) * 128, tsl].rearrange("p (b t) -> p b t", t=128)), ue, reads=[uT], writes=[ue])
            S.dma("sp", lambda e, ue=ue, cc=cc: e.dma_start(out=ue[:, :, 0:2], in_=utail.h[cc * 128:(cc + 1) * 128, ti * 4:(ti + 1) * 4, :]), ue, reads=[utail], writes=[ue])
            S.dma("sp", lambda e, bg=bg, cc=cc: e.dma_start(out=bg[:], in_=bgT.h[cc * 128:(cc + 1) * 128, tsl]), bg, reads=[bgT], writes=[bg])
            S.op("dve", lambda e, ue=ue, z=z, cc=cc: e.tensor_scalar(out=z[:], in0=ue[:, :, 2:130], scalar1=cw[:, 2, cc:cc + 1], scalar2=None, op0=ALU.mult), reads=[ue, cw], writes=[z])
            S.op("dve", lambda e, ue=ue, z=z, cc=cc: e.scalar_tensor_tensor(out=z[:], in0=ue[:, :, 1:129], scalar=cw[:, 1, cc:cc + 1], in1=z[:], op0=ALU.mult, op1=ALU.add), reads=[ue, cw, z], writes=[z])
            S.op("dve", lambda e, ue=ue, z=z, cc=cc: e.scalar_tensor_tensor(out=z[:], in0=ue[:, :, 0:128], scalar=cw[:, 0, cc:cc + 1], in1=z[:], op0=ALU.mult, op1=ALU.add), reads=[ue, cw, z], writes=[z])
            S.op("dve", lambda e, bg=bg, z=z, cc=cc: e.tensor_tensor(out=hid[:, 16 + cc, :], in0=z[:].rearrange("p b t -> p (b t)"), in1=bg[:], op=ALU.mult), reads=[z, bg], writes=[hid])
        for og in range(4):
            wa = load_w(io["w_conv_out"], 0, 8, og * 512)
            wb = load_w(io["w_nsa_out"], 0, 16, og * 512)
            for c4 in range(4):
                oc = og * 4 + c4
                pa = P["ps"].get(); pb = P["ps"].get()
                for cc in range(8):
                    S.op("pe", lambda e, cc=cc, pa=pa, c4=c4, wa=wa: e.matmul(pa[:], wa[:, cc, c4 * 128:(c4 + 1) * 128], hid[:, 16 + cc, :], start=(cc == 0), stop=(cc == 7)), reads=[wa, hid], writes=[pa])
                for hc in range(16):
                    S.op("pe", lambda e, hc=hc, pb=pb, c4=c4, wb=wb: e.matmul(pb[:], wb[:, hc, c4 * 128:(c4 + 1) * 128], hid[:, hc, :], start=(hc == 0), stop=(hc == 15)), reads=[wb, hid], writes=[pb])
                ga = gch.get(); gb = gch.get()
                S.dma("sp", lambda e, ga=ga, oc=oc: e.dma_start(out=ga[:], in_=gaT.h[oc * 128:(oc + 1) * 128, tsl]), ga, reads=[gaT], writes=[ga])
                S.dma("sp", lambda e, gb=gb, oc=oc: e.dma_start(out=gb[:], in_=gbT.h[oc * 128:(oc + 1) * 128, tsl]), gb, reads=[gbT], writes=[gb])
                S.op("dve", lambda e, ga=ga, pa=pa: e.tensor_tensor(out=ga[:], in0=pa[:], in1=ga[:], op=ALU.mult), reads=[pa, ga], writes=[ga])
                S.op("dve", lambda e, gb=gb, pb=pb: e.tensor_tensor(out=gb[:], in0=pb[:], in1=gb[:], op=ALU.mult), reads=[pb, gb], writes=[gb])
                S.op("pool", lambda e, ga=ga, gb=gb, oc=oc: e.tensor_tensor(out=mg[:, oc, :], in0=ga[:], in1=gb[:], op=ALU.add), reads=[ga, gb], writes=[mg])
        for og in range(4):
            wo = load_w(io["w_out"], 0, 16, og * 512)
            for c4 in range(4):
                ps = P["ps"].get()
                for kc in range(KC):
                    S.op("pe", lambda e, kc=kc, ps=ps, c4=c4, wo=wo: e.matmul(ps[:], wo[:, kc, c4 * 128:(c4 + 1) * 128], mg[:, kc, :], start=(kc == 0), stop=(kc == KC - 1)), reads=[wo, mg], writes=[ps])
                S.op("act", lambda e, ps=ps, og=og, c4=c4: e.activation(out=ysb[:, og * 4 + c4, :], in_=ps[:], func=AF.Copy), reads=[ps], writes=[ysb])
        post_norm_residual(ag1)
        if "xmid" in io:
            for q4 in range(4):
                S.dma("sp", lambda e, q4=q4: e.dma_start(out=io["xmid"].h.rearrange("(kc p) t -> p kc t", p=128)[:, q4 * 4:(q4 + 1) * 4, tsl], in_=xt[:, q4 * 4:(q4 + 1) * 4, :]), xt, reads=[xt], writes=[io["xmid"]])
        rms_affine(S, xt, mg, 0, a2, b2, ones_b, P)
        for fg in range(16):
            wu = load_w(io["w_mlp_up"], 0, 16, fg * 512)
            for c4 in range(4):
                ps = P["ps"].get()
                for kc in range(KC):
                    S.op("pe", lambda e, kc=kc, ps=ps, c4=c4, wu=wu: e.matmul(ps[:], wu[:, kc, c4 * 128:(c4 + 1) * 128], mg[:, kc, :], start=(kc == 0), stop=(kc == KC - 1)), reads=[wu, mg], writes=[ps])
                t = P["tmp"].get()
                S.op("act", lambda e, ps=ps, t=t: e.activation(out=t[:], in_=ps[:], func=AF.Relu), reads=[ps], writes=[t])
                S.op("dve", lambda e, t=t, fg=fg, c4=c4: e.tensor_tensor(out=hid[:, fg * 4 + c4, :], in0=t[:], in1=t[:], op=ALU.mult), reads=[t], writes=[hid])
        for og in range(4):
            pss = [P["ps"].get() for _ in range(4)]
            for slab in range(4):
                wd = load_w(io["w_mlp_down"], slab * 2048, 16, og * 512)
                for c4 in range(4):
                    for fcl in range(16):
                        S.op("pe", lambda e, fcl=fcl, c4=c4, wd=wd, slab=slab, pss=pss: e.matmul(pss[c4][:], wd[:, fcl, c4 * 128:(c4 + 1) * 128], hid[:, slab * 16 + fcl, :], start=(slab == 0 and fcl == 0), stop=(slab == 3 and fcl == 15)), reads=[wd, hid], writes=[pss[c4]])
            for c4 in range(4):
                S.op("act", lambda e, c4=c4, og=og, pss=pss: e.activation(out=ysb[:, og * 4 + c4, :], in_=pss[c4][:], func=AF.Copy), reads=[pss[c4]], writes=[ysb])
        post_norm_residual(ag3)
        for q4 in range(4):
            S.dma("sp", lambda e, q4=q4: e.dma_start(out=xo.h.rearrange("(kc p) t -> p kc t", p=128)[:, q4 * 4:(q4 + 1) * 4, tsl], in_=xt[:, q4 * 4:(q4 + 1) * 4, :]), xt, reads=[xt], writes=[xo])

    for ti in range(NTILE):
        tile(ti)
    S.final_wait("sp", [xo] + ([io["xmid"]] if "xmid" in io else []))


MCOLS = 6144


def build_M(S, io):
    cT = S.sb("cT", [128, 16], F32)
    S.dma("sp", lambda e: e.dma_start(out=cT[:], in_=io["cT"][:]), cT, reads=[io["cT"]], writes=[cT])
    S.op("act", lambda e: e.activation(out=cT[:], in_=cT[:], func=AF.Silu), reads=[cT], writes=[cT])
    brow = S.sb("brow", [1, MCOLS], F32)
    S.dma("sp", lambda e: e.dma_start(out=brow[:], in_=io["ada_b"][:]), brow, reads=[io["ada_b"]], writes=[brow])
    orow = S.sb("orow", [1, MCOLS], F32)
    wts = Pool(S, "wm", 2, [128, 16, 512], F32)
    pss = Pool(S, "psm", 2, [128, 512], F32, psum=True)
    wsrc = io["ada_w"].h.rearrange("(kc p) n -> p kc n", p=128)
    for gi in range(MCOLS // 512):
        wt = wts.get()
        for half in range(2):
            S.dma("sp", lambda e, wt=wt, half=half, gi=gi: e.dma_start(out=wt[:, half * 8:(half + 1) * 8, :], in_=wsrc[:, half * 8:(half + 1) * 8, gi * 512:(gi + 1) * 512]), wt, reads=[io["ada_w"]], writes=[wt])
        ps = pss.get()
        for kc in range(16):
            S.op("pe", lambda e, wt=wt, ps=ps, kc=kc: e.matmul(ps[0:1, :], cT[:, kc:kc + 1], wt[:, kc, :], start=(kc == 0), stop=(kc == 15)), reads=[cT, wt], writes=[ps])
        S.op("dve", lambda e, ps=ps, gi=gi: e.tensor_tensor(out=orow[:, gi * 512:(gi + 1) * 512], in0=ps[0:1, :], in1=brow[:, gi * 512:(gi + 1) * 512], op=ALU.add), reads=[ps, brow], writes=[orow])
    S.dma("sp", lambda e: e.dma_start(out=io["mod"][:], in_=orow[:]), orow, reads=[orow], writes=[io["mod"]])
    S.final_wait("sp", [io["mod"]])

bf = ml_dtypes.bfloat16
SEQ = 16384; TC = 2048

def tok_idx(c):
    return np.concatenate([np.arange((8 * j + c) * 128, (8 * j + c + 1) * 128) for j in range(16)])

def consts_common():
    n = np.arange(1024)[:, None]; jb = np.arange(256)[None, :]
    ov = np.clip(np.minimum(16 * n + 32, 64 * jb + 64) - np.maximum(16 * n, 64 * jb), 0, None).astype(np.float32) / 32
    ov[1023:] = 0
    Ov = np.ascontiguousarray(ov.reshape(8, 128, 256).transpose(1, 0, 2)).astype(bf)
    Eall = np.zeros((128, 64, 128), np.float32)
    for i in range(64):
        Eall[2 * i, i, 0:64] = 1; Eall[2 * i + 1, i, 64:128] = 1
    p = np.arange(128)[:, None]
    nthr = (16 * (np.arange(8)[None, :] * 128 + p) + 31).astype(np.float32)
    blkrow = np.broadcast_to(np.arange(256, dtype=np.float32)[None, :], (128, 256)).copy()
    baserow = -(blkrow + 2)
    invf = (500000.0 ** (-np.arange(0, 32, 2, dtype=np.float32) / 32)).astype(np.float32)
    invf2 = np.zeros((32, 2), np.float32); invf2[:, 0] = np.tile(invf, 2); invf2[:16, 1] = -1; invf2[16:, 1] = 1
    return {"Ov": Ov, "Eall": Eall.astype(bf), "nthr": nthr, "blkrow": blkrow, "baserow": baserow, "invf": invf2}

def consts_core(c):
    i = np.arange(128)[:, None]; ip = np.arange(128)[None, :]
    NEG = -30000.0
    causal = np.where(i <= ip, 0.0, NEG).astype(np.float32)
    band = np.where(i > ip, 0.0, NEG).astype(np.float32)
    full = np.zeros((128, 128), np.float32); none = np.full((128, 128), NEG, np.float32)
    DB = np.zeros((128, 8, 4, 128), np.float32)
    for kk in range(8):
        DB[:, kk] = (causal if kk == c else full)[:, None, :]
    WB = np.zeros((128, 12, 4, 128), np.float32)
    for m in range(12):
        d = m - 4 - c
        t = none if d < -4 else band if d == -4 else full if d < 0 else causal if d == 0 else none
        WB[:, m] = t[:, None, :]
    ti = tok_idx(c)
    tidxrow = np.broadcast_to(ti.astype(np.float32)[None, :], (128, TC)).copy()
    curcol = (ti.reshape(16, 128).T // 64).astype(np.float32)
    return {"DB4": DB.reshape(128, 8, 512).astype(bf), "WB4": WB.reshape(128, 12, 512).astype(bf), "tidxrow": tidxrow, "curcol": np.ascontiguousarray(curcol)}

def gelu_tanh(x):
    return 0.5 * x * (1 + np.tanh(np.sqrt(2 / np.pi) * (x + 0.044715 * x ** 3)))

def ref_compress(rawT, pe, w1, b1, w2, b2):
    kv = rawT.T
    idx = np.arange(1023)[:, None] * 16 + np.arange(32)[None, :]
    blocks = (kv[idx] + pe[None]).reshape(1023, 4096)
    h = gelu_tanh(blocks @ w1 + b1)
    return h @ w2 + b2

def ref_attn_block(gb, q, kc, vc, ks, vs, kw, vw, gate):
    scale = 128 ** -0.5
    t = gb * 128 + np.arange(128)
    sc = np.einsum('trd,nd->rtn', q, kc) * scale
    cmp_end = np.arange(1023) * 16 + 31
    m_c = cmp_end[None, :] <= t[:, None]
    scm = np.where(m_c[None], sc, -1e30)
    e = np.exp(scm - scm.max(-1, keepdims=True)); p = e / e.sum(-1, keepdims=True)
    p_c = np.where(m_c[None], p, 0.0)
    o_c = np.einsum('rtn,nd->trd', p_c, vc)
    n = np.arange(1023)[:, None]; jb = np.arange(256)[None, :]
    ov = np.clip(np.minimum(16 * n + 32, 64 * jb + 64) - np.maximum(16 * n, 64 * jb), 0, None) / 32.0
    imp = np.einsum('rtn,nj->tj', p_c, ov)
    cur = t // 64
    blk = np.arange(256)
    forced = (blk[None, :] == cur[:, None]) | (blk[None, :] == 0)
    valid = blk[None, :] <= cur[:, None]
    imp = np.where(forced, 1e30, np.where(valid, imp, -1e30))
    sel = np.argsort(-imp, axis=-1, kind='stable')[:, :16]
    o_s = np.zeros((128, 4, 128));
    for i in range(128):
        kpos = (sel[i][:, None] * 64 + np.arange(64)[None, :]).reshape(-1)
        msk = kpos <= t[i]
        s = (q[i] @ ks[kpos].T) * scale
        s = np.where(msk[None], s, -1e30)
        e = np.exp(s - s.max(-1, keepdims=True)); pp = e / e.sum(-1, keepdims=True)
        o_s[i] = pp @ vs[kpos]
    o_w = np.zeros((128, 4, 128))
    for i in range(128):
        lo = max(0, t[i] - 511)
        s = (q[i] @ kw[lo:t[i] + 1].T) * scale
        e = np.exp(s - s.max(-1, keepdims=True)); pp = e / e.sum(-1, keepdims=True)
        o_w[i] = pp @ vw[lo:t[i] + 1]
    return gate[..., 0:1] * o_c + gate[..., 1:2] * o_s + gate[..., 2:3] * o_w, (o_c, o_s, o_w, sel)

_PROGS = {}
NCORES = 8


def _prog(name):
    if name in _PROGS:
        return _PROGS[name]
    nc = bass.Bass("TRN2", target_bir_lowering=False)
    with ExitStack() as es:
        S = Sched(nc, es)
        io = {}

        def din(n, shape, dt):
            io[n] = S.dram(n, shape, dt, kind="ExternalInput")

        def dout(n, shape, dt):
            io[n] = S.dram(n, shape, dt, kind="ExternalOutput")
        if name == "M":
            din("cT", [128, 16], F32); din("ada_w", [2048, MCOLS], F32); din("ada_b", [1, MCOLS], F32)
            dout("mod", [1, MCOLS], F32)
            build_M(S, io)
        elif name == "A":
            din("xT", [D, TC], F32); din("modT", [128, 96], F32); din("gainsT", [128, 64], F32)
            din("pos", [32, TC], I32); din("invf", [32, 2], F32); din("w_in", [1, D, INW], F32)
            dout("bgT", [1024, TC], F32); dout("uT", [1024, TC], F32); dout("qT", [2048, TC], BF16)
            dout("kT", [4, 4, 128, TC], BF16); dout("vtok", [2, TC, 512], BF16); dout("gn", [TC, 48], F32)
            dout("gaT", [2048, TC], F32); dout("gbT", [2048, TC], F32)
            build_A(S, io, 0)
        elif name == "B2":
            din("Kfull", [4, 4, 128, SEQ], BF16); din("Vfull", [2, SEQ, 512], BF16); din("qT", [2048, TC], BF16); din("gn", [TC, 48], F32)
            din("cmp_pe", [2, 32, 128], F32); din("cmp_w1", [2, 4096, 256], F32); din("cmp_b1", [2, 256], F32)
            din("cmp_w2", [2, 256, 128], F32); din("cmp_b2", [2, 128], F32)
            din("Ov", [128, 8, 256], BF16); din("Eall", [128, 64, 128], BF16); din("DB4", [128, 8, 512], BF16); din("WB4", [128, 12, 512], BF16)
            din("tidxrow", [128, TC], F32); din("curcol", [128, 16], F32); din("nthr", [128, 8], F32)
            din("blkrow", [128, 256], F32); din("baserow", [128, 256], F32)
            dout("attnT", [2048, TC], BF16)
            build_B2(S, io)
        elif name == "B3":
            din("xT", [D, TC], F32); din("bgT", [1024, TC], F32); din("uT", [1024, TC], F32); din("utail", [1024, 16, 2], F32)
            din("attnT", [2048, TC], BF16); din("gaT", [2048, TC], F32); din("gbT", [2048, TC], F32)
            din("modT", [128, 96], F32); din("gainsT", [128, 64], F32); din("conv_w", [3, 1024], F32)
            din("w_conv_out", [1024, 2048], F32); din("w_nsa_out", [2048, 2048], F32); din("w_out", [2048, 2048], F32)
            din("w_mlp_up", [2048, DFF], F32); din("w_mlp_down", [DFF, 2048], F32)
            dout("xT_out", [D, TC], F32)
            build_B3(S, io)
        S.emit()
    _PROGS[name] = nc
    return nc


def _run(name, in_maps):
    nc = _prog(name)
    res = run_bass_kernel_spmd(nc, in_maps, core_ids=list(range(NCORES)))
    return res.results


def kernel(x, c, positions, ada_w, ada_b, norm_gains, w_in, conv_w, w_conv_out, cmp_pe, cmp_w1, cmp_b1,
           cmp_w2, cmp_b2, w_nsa_out, w_out, w_mlp_up, w_mlp_down):
    f32 = lambda a: np.ascontiguousarray(np.asarray(a), dtype=np.float32)
    x = f32(x); c = f32(c); positions = np.asarray(positions).astype(np.int32)
    ada_w = np.asarray(ada_w); ada_b = np.asarray(ada_b); norm_gains = f32(norm_gains)
    DEPTH = ada_w.shape[0]
    toks = [tok_idx(cc) for cc in range(NCORES)]
    cc_ = consts_common()
    ccore = [consts_core(cc) for cc in range(NCORES)]
    cT = np.ascontiguousarray(c[0].reshape(16, 128).T)
    in_maps = []
    per_layer = 12288 // MCOLS
    for cc in range(NCORES):
        l, h = cc // per_layer, cc % per_layer
        in_maps.append({"cT": cT, "ada_w": f32(ada_w[l][:, h * MCOLS:(h + 1) * MCOLS]), "ada_b": f32(ada_b[l][None, h * MCOLS:(h + 1) * MCOLS])})
    resM = _run("M", in_maps)
    mods = [np.concatenate([resM[l * per_layer + h]["mod"][0] for h in range(per_layer)]) for l in range(DEPTH)]
    xT = [np.ascontiguousarray(x[0][toks[cc]].T) for cc in range(NCORES)]
    pos = [np.ascontiguousarray(np.broadcast_to(positions[0][toks[cc]][None, :], (32, TC))).astype(np.int32) for cc in range(NCORES)]
    for l in range(DEPTH):
        modT = np.ascontiguousarray(mods[l].reshape(96, 128).T)
        gainsT = np.ascontiguousarray(norm_gains[l].reshape(64, 128).T)
        w_in_l = f32(np.asarray(w_in)[l:l + 1])
        resA = _run("A", [{"xT": xT[cc], "modT": modT, "gainsT": gainsT, "pos": pos[cc], "invf": cc_["invf"], "w_in": w_in_l} for cc in range(NCORES)])
        Kfull = np.zeros((4, 4, 128, SEQ), bf); Vfull = np.zeros((2, SEQ, 512), bf)
        for cc in range(NCORES):
            Kfull[:, :, :, toks[cc]] = resA[cc]["kT"]
            Vfull[:, toks[cc], :] = resA[cc]["vtok"]
        utails = []
        for cc in range(NCORES):
            ut = np.zeros((1024, 16, 2), np.float32)
            for j in range(16):
                gb = 8 * j + cc
                if gb == 0:
                    continue
                pc, pj = (gb - 1) % 8, (gb - 1) // 8
                ut[:, j, :] = resA[pc]["uT"][:, pj * 128 + 126:pj * 128 + 128]
            utails.append(ut)
        cmpw = {k: f32(np.asarray(v)[l]) for k, v in (("cmp_pe", cmp_pe), ("cmp_w1", cmp_w1), ("cmp_b1", cmp_b1), ("cmp_w2", cmp_w2), ("cmp_b2", cmp_b2))}
        in_maps = []
        for cc in range(NCORES):
            m = {"Kfull": Kfull, "Vfull": Vfull, "qT": resA[cc]["qT"], "gn": resA[cc]["gn"]}
            m.update(cmpw)
            for k in ("Ov", "Eall", "nthr", "blkrow", "baserow"):
                m[k] = cc_[k]
            m.update(ccore[cc])
            in_maps.append(m)
        resB2 = _run("B2", in_maps)
        wl = {k: f32(np.asarray(v)[l]) for k, v in (("conv_w", conv_w), ("w_conv_out", w_conv_out), ("w_nsa_out", w_nsa_out), ("w_out", w_out), ("w_mlp_up", w_mlp_up), ("w_mlp_down", w_mlp_down))}
        in_maps = []
        for cc in range(NCORES):
            m = {"xT": xT[cc], "bgT": resA[cc]["bgT"], "uT": resA[cc]["uT"], "utail": utails[cc], "attnT": resB2[cc]["attnT"],
                 "gaT": resA[cc]["gaT"], "gbT": resA[cc]["gbT"], "modT": modT, "gainsT": gainsT}
            m.update(wl)
            in_maps.append(m)
        resB3 = _run("B3", in_maps)
        xT = [resB3[cc]["xT_out"] for cc in range(NCORES)]
    out = np.zeros((SEQ, D), np.float32)
    for cc in range(NCORES):
        out[toks[cc]] = xT[cc].T
    return out.reshape(1, SEQ, D)
```

```python
import math
from contextlib import ExitStack
import numpy as np
import ml_dtypes
import concourse.bass as bass
import concourse.mybir as mybir
from concourse.bass_utils import run_bass_kernel_spmd


F32 = mybir.dt.float32
BF16 = mybir.dt.bfloat16
I32 = mybir.dt.int32
AF = mybir.ActivationFunctionType
ALU = mybir.AluOpType
AX = mybir.AxisListType


class T:
    def __init__(self, name, h):
        self.name = name
        self.h = h
        self.w = None
        self.r = []
        self.sem = None
        self.dcnt = 0
        self.multi = False
        self.ws = []

    def __getitem__(self, k):
        return self.h[k]


class Sched:
    ENG = ("pe", "act", "dve", "pool", "sp")

    def __init__(self, nc, es, same_engine_sync=True):
        self.nc = nc
        self.es = es
        self.ops = {e: [] for e in self.ENG}
        self.cnt = {e: 0 for e in self.ENG}
        self.esem = {e: es.enter_context(nc.semaphore("sem_" + e)) for e in self.ENG}
        self.waited = {e: {} for e in self.ENG}
        self.same = same_engine_sync
        self.nsem = 5
        self.ninst = 0

    def sb(self, name, shape, dt):
        return T(name, self.es.enter_context(self.nc.sbuf_tensor("sb_" + name, list(shape), dt)))

    def ps(self, name, shape, dt=F32):
        return T(name, self.es.enter_context(self.nc.psum_tensor("ps_" + name, list(shape), dt)))

    def dram(self, name, shape, dt, kind="Internal"):
        t = T(name, self.nc.dram_tensor(name, list(shape), dt, kind=kind).ap())
        t.multi = True
        return t

    def _deps(self, eng, reads, writes):
        deps = []
        for t in reads:
            if t.w is not None:
                deps.append(t.w)
            deps.extend(t.ws)
        for t in writes:
            if not t.multi:
                if t.w is not None:
                    deps.append(t.w)
            deps.extend(t.r)
        waits = []
        wd = self.waited[eng]
        own = self.esem[eng]
        best = {}
        for (sem, val) in deps:
            if sem is own and (not self.same or eng == "pe"):
                continue
            if wd.get(id(sem), 0) >= val:
                continue
            if id(sem) not in best or best[id(sem)][1] < val:
                best[id(sem)] = (sem, val)
        for k, (sem, val) in best.items():
            wd[k] = val
            waits.append((sem, val))
        return waits

    def _stamp(self, stamp, reads, writes):
        for t in writes:
            if t.multi:
                t.ws.append(stamp)
                if len(t.ws) > 12:
                    t.ws = self._compact(t.ws)
                continue
            t.w = stamp
            t.r = []
        for t in reads:
            if t in writes:
                continue
            t.r.append(stamp)
            if len(t.r) > 12:
                t.r = self._compact(t.r)

    @staticmethod
    def _compact(lst):
        m = {}
        for (s, v) in lst:
            if id(s) not in m or m[id(s)][1] < v:
                m[id(s)] = (s, v)
        return list(m.values())

    def op(self, eng, fn, reads=(), writes=()):
        waits = self._deps(eng, reads, writes)
        self.cnt[eng] += 1
        stamp = (self.esem[eng], self.cnt[eng])
        self.ops[eng].append((fn, waits, (self.esem[eng], 1)))
        self._stamp(stamp, reads, writes)
        self.ninst += 1

    def dma(self, q, fn, semt, reads=(), writes=()):
        waits = self._deps(q, reads, writes)
        if semt.sem is None:
            semt.sem = self.es.enter_context(self.nc.semaphore("dsem_" + semt.name))
            self.nsem += 1
        semt.dcnt += 16
        stamp = (semt.sem, semt.dcnt)
        self.ops[q].append((fn, waits, (semt.sem, 16)))
        self._stamp(stamp, reads, writes)
        self.ninst += 1

    def final_wait(self, eng, tiles):
        deps = []
        for t in tiles:
            if t.w is not None:
                deps.append(t.w)
            deps.extend(t.ws)
        self.ops[eng].append((None, self._compact(deps), None))

    def emit(self):
        nc = self.nc
        emap = {"pe": "tensor", "act": "scalar", "dve": "vector", "pool": "gpsimd", "sp": "sync"}
        with nc.Block() as block:
            for e in self.ENG:
                ops = self.ops[e]

                def body(engine, ops=ops):
                    for (fn, waits, inc) in ops:
                        for (sem, val) in waits:
                            engine.wait_ge(sem, val)
                        if fn is not None:
                            ins = fn(engine)
                            ins.then_inc(inc[0], inc[1])
                getattr(block, emap[e])(body)


D = 2048; TC = 2048; NT = 512; NTILE = 4; KC = 16
OFF_BG, OFF_CG, OFF_XA, OFF_Q, OFF_KV, OFF_GN, OFF_GA, OFF_GB = 0, 1024, 2048, 3072, 5120, 8192, 8240, 10288
INW = 12336
EPS = 1e-6
TWO_PI = 2.0 * math.pi


class Pool:
    def __init__(self, S, name, n, shape, dt, psum=False):
        self.t = [(S.ps if psum else S.sb)("%s%d" % (name, i), shape, dt) for i in range(n)]
        self.i = 0

    def get(self):
        t = self.t[self.i % len(self.t)]
        self.i += 1
        return t


def load_x_tile(S, xT_d, xt, ti):
    src = xT_d.h.rearrange("(kc p) t -> p kc t", p=128)
    for q4 in range(4):
        S.dma("sp", lambda e, q4=q4: e.dma_start(out=xt[:, q4 * 4:(q4 + 1) * 4, :], in_=src[:, q4 * 4:(q4 + 1) * 4, ti * NT:(ti + 1) * NT]),
              xt, reads=[xT_d], writes=[xt])


def rms_affine(S, xt, hT, tcol0, a_t, b_t, ones_b, P):
    ss = P["ps"].get()
    for kc in range(KC):
        s = P["sq"].get()
        S.op("act", lambda e, s=s, kc=kc: e.activation(out=s[:], in_=xt[:, kc, :], func=AF.Square), reads=[xt], writes=[s])
        S.op("pe", lambda e, s=s, kc=kc: e.matmul(ss[:], ones_b[:], s[:], start=(kc == 0), stop=(kc == KC - 1)), reads=[s, ones_b], writes=[ss])
    r = P["rstd"].get()
    S.op("act", lambda e: e.activation(out=r[:], in_=ss[:], func=AF.Sqrt, bias=P["eps"][:], scale=1.0 / D), reads=[ss, P["eps"]], writes=[r])
    S.op("dve", lambda e: e.reciprocal(out=r[:], in_=r[:]), reads=[r], writes=[r])
    for kc in range(KC):
        t = P["tmp"].get()
        S.op("dve", lambda e, t=t, kc=kc: e.scalar_tensor_tensor(out=t[:], in0=xt[:, kc, :], scalar=a_t[:, kc:kc + 1], in1=r[:], op0=ALU.mult, op1=ALU.mult), reads=[xt, a_t, r], writes=[t])
        S.op("act", lambda e, t=t, kc=kc: e.activation(out=hT[:, kc, tcol0:tcol0 + NT], in_=t[:], func=AF.Identity, bias=b_t[:, kc:kc + 1], scale=1.0), reads=[t, b_t], writes=[hT])


def build_A(S, io, L):
    nc = S.nc
    xT_d = io["xT"]
    ones_b = S.sb("ones_b", [128, 128], BF16)
    S.op("dve", lambda e: e.memset(ones_b[:], 1.0), writes=[ones_b])
    modT = S.sb("modT", [128, 96], F32)
    gT = S.sb("gT", [128, 64], F32)
    S.dma("sp", lambda e: e.dma_start(out=modT[:], in_=io["modT"][:]), modT, reads=[io["modT"]], writes=[modT])
    S.dma("sp", lambda e: e.dma_start(out=gT[:], in_=io["gainsT"][:]), gT, reads=[io["gainsT"]], writes=[gT])
    a1 = S.sb("a1", [128, 16], F32)
    S.op("dve", lambda e: e.scalar_tensor_tensor(out=a1[:], in0=modT[:, 16:32], scalar=1.0, in1=gT[:, 0:16], op0=ALU.add, op1=ALU.mult), reads=[modT, gT], writes=[a1])
    b1 = S.sb("b1", [128, 16], F32)
    S.op("dve", lambda e: e.tensor_copy(out=b1[:], in_=modT[:, 0:16]), reads=[modT], writes=[b1])
    posi = S.sb("posi", [32, TC], I32)
    S.dma("sp", lambda e: e.dma_start(out=posi[:], in_=io["pos"][:]), posi, reads=[io["pos"]], writes=[posi])
    invf = S.sb("invf", [32, 2], F32)
    S.dma("sp", lambda e: e.dma_start(out=invf[:], in_=io["invf"][:]), invf, reads=[io["invf"]], writes=[invf])
    posf = S.sb("posf", [32, TC], F32)
    S.op("dve", lambda e: e.tensor_copy(out=posf[:], in_=posi[:]), reads=[posi], writes=[posf])
    ang = S.sb("ang", [32, TC], F32)
    S.op("dve", lambda e: e.tensor_scalar(out=ang[:], in0=posf[:], scalar1=invf[:, 0:1], scalar2=None, op0=ALU.mult), reads=[posf, invf], writes=[ang])
    Ct = S.sb("Ct", [32, TC], F32)
    Sn = S.sb("Snt", [32, TC], F32)
    C1 = 6.28125
    C2 = TWO_PI - C1
    yk = posf
    ki = posi
    S.op("dve", lambda e: e.tensor_scalar(out=yk[:], in0=ang[:], scalar1=1.0 / TWO_PI, scalar2=None, op0=ALU.mult), reads=[ang], writes=[yk])
    S.op("dve", lambda e: e.tensor_copy(out=ki[:], in_=yk[:]), reads=[yk], writes=[ki])
    S.op("dve", lambda e: e.tensor_copy(out=yk[:], in_=ki[:]), reads=[ki], writes=[yk])
    S.op("dve", lambda e: e.scalar_tensor_tensor(out=ang[:], in0=yk[:], scalar=-C1, in1=ang[:], op0=ALU.mult, op1=ALU.add), reads=[yk, ang], writes=[ang])
    S.op("dve", lambda e: e.scalar_tensor_tensor(out=ang[:], in0=yk[:], scalar=-C2, in1=ang[:], op0=ALU.mult, op1=ALU.add), reads=[yk, ang], writes=[ang])
    S.op("dve", lambda e: e.tensor_scalar(out=Sn[:], in0=ang[:], scalar1=-math.pi, scalar2=math.pi, op0=ALU.max, op1=ALU.min), reads=[ang], writes=[Sn])
    S.op("act", lambda e: e.activation(out=Sn[:], in_=Sn[:], func=AF.Sin), reads=[Sn], writes=[Sn])
    S.op("dve", lambda e: e.tensor_scalar(out=Sn[:], in0=Sn[:], scalar1=invf[:, 1:2], scalar2=None, op0=ALU.mult), reads=[Sn, invf], writes=[Sn])
    S.op("dve", lambda e: e.tensor_single_scalar(out=yk[:], in_=ang[:], scalar=math.pi / 2, op=ALU.is_gt), reads=[ang], writes=[yk])
    S.op("dve", lambda e: e.scalar_tensor_tensor(out=Ct[:], in0=yk[:], scalar=-TWO_PI, in1=ang[:], op0=ALU.mult, op1=ALU.add), reads=[yk, ang], writes=[Ct])
    S.op("dve", lambda e: e.tensor_scalar(out=Ct[:], in0=Ct[:], scalar1=math.pi / 2, scalar2=math.pi, op0=ALU.add, op1=ALU.min), reads=[Ct], writes=[Ct])
    S.op("dve", lambda e: e.tensor_scalar(out=Ct[:], in0=Ct[:], scalar1=-math.pi, scalar2=None, op0=ALU.max), reads=[Ct], writes=[Ct])
    S.op("act", lambda e: e.activation(out=Ct[:], in_=Ct[:], func=AF.Sin), reads=[Ct], writes=[Ct])

    hT = S.sb("hT", [128, KC, TC], BF16)
    xts = Pool(S, "xt", 1, [128, KC, NT], F32)
    P = {"ps": Pool(S, "ps", 8, [128, 512], F32, psum=True), "sq": Pool(S, "sq", 3, [128, NT], BF16),
         "rstd": Pool(S, "rstd", 2, [128, NT], F32), "tmp": Pool(S, "tmp", 3, [128, NT], F32)}
    P["eps"] = S.sb("epsc", [128, 1], F32)
    S.op("dve", lambda e: e.memset(P["eps"][:], EPS), writes=[P["eps"]])
    for ti in range(NTILE):
        xt = xts.get()
        load_x_tile(S, xT_d, xt, ti)
        rms_affine(S, xt, hT, ti * NT, a1, b1, ones_b, P)

    wts = Pool(S, "wt", 2, [128, KC, 512], BF16)
    stg = Pool(S, "stg", 4, [128, NT], F32)
    stb = Pool(S, "stb", 3, [128, NT], BF16)
    swp = Pool(S, "swp", 2, [32, NT], F32)
    rt2 = Pool(S, "rt2", 2, [32, NT], F32)
    w_in = io["w_in"]
    wsrc = w_in.h[L].rearrange("(kc p) n -> p kc n", p=128)

    def load_w(blocks):
        wt = wts.get()
        o = 0
        for (c0, w) in blocks:
            for half in range(2):
                S.dma("pool", lambda e, wt=wt, o=o, c0=c0, w=w, half=half: e.dma_start(out=wt[:, half * 8:(half + 1) * 8, o:o + w], in_=wsrc[:, half * 8:(half + 1) * 8, c0:c0 + w]),
                      wt, reads=[w_in], writes=[wt])
            o += w
        return wt

    def mm_fm(wt, off, M, ti):
        ps = P["ps"].get()
        for kc in range(KC):
            S.op("pe", lambda e, kc=kc: e.matmul(ps[:M, :], wt[:, kc, off:off + M], hT[:, kc, ti * NT:(ti + 1) * NT], start=(kc == 0), stop=(kc == KC - 1)), reads=[wt, hT], writes=[ps])
        return ps

    def mm_tm(wt, W, tb):
        ps = P["ps"].get()
        for kc in range(KC):
            S.op("pe", lambda e, kc=kc: e.matmul(ps[:, :W], hT[:, kc, tb * 128:(tb + 1) * 128], wt[:, kc, 0:W], start=(kc == 0), stop=(kc == KC - 1)), reads=[wt, hT], writes=[ps])
        return ps

    def store(dst_d, dst_ap, src_t, src_ap):
        S.dma("sp", lambda e: e.dma_start(out=dst_ap, in_=src_ap), src_t, reads=[src_t], writes=[dst_d])

    def rope_store(ps, dst_d, dst_ap, ti, do_rope=True):
        st = stg.get()
        S.op("act", lambda e: e.activation(out=st[:], in_=ps[:], func=AF.Copy), reads=[ps], writes=[st])
        if do_rope:
            sw = swp.get()
            S.dma("sp", lambda e: e.dma_start(out=sw[0:16, :], in_=st[16:32, :]), sw, reads=[st], writes=[sw])
            S.dma("sp", lambda e: e.dma_start(out=sw[16:32, :], in_=st[0:16, :]), sw, reads=[st], writes=[sw])
            t2 = rt2.get()
            tsl = slice(ti * NT, (ti + 1) * NT)
            S.op("dve", lambda e: e.tensor_tensor(out=t2[:], in0=sw[:], in1=Sn[:, tsl], op=ALU.mult), reads=[sw, Sn], writes=[t2])
            S.op("dve", lambda e: e.tensor_tensor(out=st[0:32, :], in0=st[0:32, :], in1=Ct[:, tsl], op=ALU.mult), reads=[st, Ct], writes=[st])
            S.op("dve", lambda e: e.tensor_tensor(out=st[0:32, :], in0=st[0:32, :], in1=t2[:], op=ALU.add), reads=[st, t2], writes=[st])
        sb = stb.get()
        S.op("act", lambda e: e.activation(out=sb[:], in_=st[:], func=AF.Copy), reads=[st], writes=[sb])
        store(dst_d, dst_ap, sb, sb[:])

    bgT, uT, qT, kT, vtok, gn, gaT, gbT = (io[k] for k in ("bgT", "uT", "qT", "kT", "vtok", "gn", "gaT", "gbT"))
    for jb in range(2):
        wt = load_w([(OFF_BG + jb * 512, 512)])
        for ti in range(NTILE):
            for c4 in range(4):
                ps = mm_fm(wt, c4 * 128, 128, ti)
                st = stg.get()
                S.op("act", lambda e, st=st, ps=ps: e.activation(out=st[:], in_=ps[:], func=AF.Copy), reads=[ps], writes=[st])
                ch = jb * 4 + c4
                store(bgT, bgT[ch * 128:(ch + 1) * 128, ti * NT:(ti + 1) * NT], st, st[:])
    for jb in range(4):
        wt = load_w([(OFF_CG + jb * 256, 256), (OFF_XA + jb * 256, 256)])
        for ti in range(NTILE):
            for c2 in range(2):
                pc = mm_fm(wt, c2 * 128, 128, ti)
                px = mm_fm(wt, 256 + c2 * 128, 128, ti)
                st = stg.get()
                S.op("act", lambda e, st=st, pc=pc: e.activation(out=st[:], in_=pc[:], func=AF.Copy), reads=[pc], writes=[st])
                S.op("dve", lambda e, st=st, px=px: e.tensor_tensor(out=st[:], in0=px[:], in1=st[:], op=ALU.mult), reads=[px, st], writes=[st])
                ch = jb * 2 + c2
                store(uT, uT[ch * 128:(ch + 1) * 128, ti * NT:(ti + 1) * NT], st, st[:])
    for jb in range(4):
        wt = load_w([(OFF_Q + jb * 512, 512)])
        for ti in range(NTILE):
            for c4 in range(4):
                ps = mm_fm(wt, c4 * 128, 128, ti)
                hd = jb * 4 + c4
                rope_store(ps, qT, qT[hd * 128:(hd + 1) * 128, ti * NT:(ti + 1) * NT], ti)
    for (kt_i, kvi, rope) in ((0, 0, True), (1, 1, False), (2, 2, True), (3, 4, True)):
        wt = load_w([(OFF_KV + kvi * 512, 512)])
        for ti in range(NTILE):
            for g in range(4):
                ps = mm_fm(wt, g * 128, 128, ti)
                rope_store(ps, kT, kT[kt_i, g, :, ti * NT:(ti + 1) * NT], ti, do_rope=rope)
    for (vt_i, kvi) in ((0, 3), (1, 5)):
        wt = load_w([(OFF_KV + kvi * 512, 512)])
        for tb in range(TC // 128):
            ps = mm_tm(wt, 512, tb)
            sb = stb.get()
            S.op("act", lambda e, sb=sb, ps=ps: e.activation(out=sb[:], in_=ps[:], func=AF.Copy), reads=[ps], writes=[sb])
            store(vtok, vtok[vt_i, tb * 128:(tb + 1) * 128, :], sb, sb[:])
    wt = load_w([(OFF_GN, 48)])
    for tb in range(TC // 128):
        ps = mm_tm(wt, 48, tb)
        st = stg.get()
        S.op("act", lambda e, st=st, ps=ps: e.activation(out=st[:, 0:48], in_=ps[:, 0:48], func=AF.Sigmoid), reads=[ps], writes=[st])
        store(gn, gn[tb * 128:(tb + 1) * 128, :], st, st[:, 0:48])
    for (dst, off) in ((gaT, OFF_GA), (gbT, OFF_GB)):
        for jb in range(4):
            wt = load_w([(off + jb * 512, 512)])
            for ti in range(NTILE):
                for c4 in range(4):
                    ps = mm_fm(wt, c4 * 128, 128, ti)
                    st = stg.get()
                    S.op("act", lambda e, st=st, ps=ps: e.activation(out=st[:], in_=ps[:], func=AF.Sigmoid), reads=[ps], writes=[st])
                    ch = jb * 4 + c4
                    store(dst, dst[ch * 128:(ch + 1) * 128, ti * NT:(ti + 1) * NT], st, st[:])
    S.final_wait("sp", [bgT, uT, qT, kT, vtok, gn, gaT, gbT])


SEQ = 16384
NKT = 128
SCALE = 128 ** -0.5
NSLOT = 16
LAG = 2
GELU_C = 2.0 * math.sqrt(2.0 / math.pi)


def build_B2(S, io, slots=None, groups=None):
    slots = list(range(NSLOT)) if slots is None else slots
    groups = list(range(4)) if groups is None else groups
    Kfull, Vfull, qT, gn_d, attnT = io["Kfull"], io["Vfull"], io["qT"], io["gn"], io["attnT"]
    def const(name, shape, dt, src_ap, src_t, q="sp"):
        t = S.sb(name, shape, dt)
        S.dma(q, lambda e: e.dma_start(out=t[:], in_=src_ap), t, reads=[src_t], writes=[t])
        return t
    Ov = const("Ov", [128, 8, 256], BF16, io["Ov"][:], io["Ov"])
    Eall = const("Eall", [128, 64, 128], BF16, io["Eall"][:], io["Eall"])
    DB4 = const("DB4", [128, 8, 512], BF16, io["DB4"][:], io["DB4"])
    WB4 = const("WB4", [128, 12, 512], BF16, io["WB4"][:], io["WB4"])
    tidxrow = const("tidxrow", [128, TC], F32, io["tidxrow"][:], io["tidxrow"])
    curcol = const("curcol", [128, 16], F32, io["curcol"][:], io["curcol"])
    nthr = const("nthr", [128, 8], F32, io["nthr"][:], io["nthr"])
    blkrow = const("blkrow", [128, 256], F32, io["blkrow"][:], io["blkrow"])
    baserow = const("baserow", [128, 256], F32, io["baserow"][:], io["baserow"])
    gn = const("gnsb", [128, NSLOT, 48], F32, gn_d.h.rearrange("(j p) c -> p j c", p=128), gn_d)
    ident = S.sb("ident", [128, 128], BF16)
    S.op("dve", lambda e: e.memset(ident[:], 0.0), writes=[ident])
    S.op("pool", lambda e: e.affine_select(out=ident[:], in_=ident[:], pattern=[[-1, 128]], compare_op=ALU.not_equal, fill=1.0, base=0, channel_multiplier=1), reads=[ident], writes=[ident])

    bigK = S.sb("bigK", [128, SEQ + 32], BF16)
    vsa = S.sb("vsa", [128, NKT, 129], BF16)
    S.op("dve", lambda e: e.memset(bigK[:, SEQ:SEQ + 32], 0.0), writes=[bigK])
    S.op("dve", lambda e: e.memset(vsa[:, :, 128:129], 1.0), writes=[vsa])
    kcT = S.sb("kcT", [128, 4, 1024], BF16)
    vca = S.sb("vca", [128, 4, 8, 129], BF16)
    S.op("dve", lambda e: e.memset(vca[:, :, :, 128:129], 1.0), writes=[vca])

    psS = Pool(S, "psS", 3, [128, 512], F32, psum=True)
    psO = [S.ps("psO%d" % i, [128, 512], F32) for i in range(2)]
    psI = [S.ps("psI%d" % i, [128, 512], F32) for i in range(2)]
    psT = S.ps("psT", [128, 1024], BF16)

    qw = S.sb("qw", [128, 8192], BF16)
    class _V:
        def __init__(self, tt, pat, **kw):
            self.t = tt; self.pat = pat; self.kw = kw
        def __getitem__(self, k):
            return self.t[:].rearrange(self.pat, **self.kw)[k]
    w1v = _V(qw, "p (l h) -> p l h", h=256)
    QTv = _V(qw, "p (r t) -> p r t", t=TC)
    w2b = S.sb("w2b", [128, 2, 128], BF16)
    peT = S.sb("peT", [128, 32], BF16)
    b1T = S.sb("b1T", [128, 2], F32)
    b2c = S.sb("b2c", [128, 1], F32)
    b2row = S.sb("b2row", [128, 128], F32)
    cb = S.sb("cb", [128, 2], F32)
    Hg = S.sb("Hg", [128, 2, 1024], BF16)
    hs = Pool(S, "hs", 1, [128, 512], F32)
    hx = Pool(S, "hx", 1, [128, 512], F32)
    for ty in range(2):
        S.dma("pool", lambda e, ty=ty: e.dma_start(out=w1v[:, 0:16, :], in_=io["cmp_w1"].h[ty].rearrange("(l d) h -> d l h", d=128)[:, 0:16, :]), qw, reads=[io["cmp_w1"]], writes=[qw])
        S.dma("pool", lambda e, ty=ty: e.dma_start(out=w1v[:, 16:32, :], in_=io["cmp_w1"].h[ty].rearrange("(l d) h -> d l h", d=128)[:, 16:32, :]), qw, reads=[io["cmp_w1"]], writes=[qw])
        S.dma("pool", lambda e, ty=ty: e.dma_start(out=w2b[:], in_=io["cmp_w2"].h[ty].rearrange("(hc p) d -> p hc d", p=128)), w2b, reads=[io["cmp_w2"]], writes=[w2b])
        S.dma("pool", lambda e, ty=ty: e.dma_start(out=peT[:], in_=io["cmp_pe"].h[ty].rearrange("l d -> d l"), allow_slow_non_contiguous=True), peT, reads=[io["cmp_pe"]], writes=[peT])
        S.dma("sp", lambda e, ty=ty: e.dma_start(out=b1T[:], in_=io["cmp_b1"].h[ty].rearrange("(hc p) -> p hc", p=128), allow_slow_non_contiguous=True), b1T, reads=[io["cmp_b1"]], writes=[b1T])
        S.dma("sp", lambda e, ty=ty: e.dma_start(out=b2c[:], in_=io["cmp_b2"].h[ty].rearrange("(d o) -> d o", o=1), allow_slow_non_contiguous=True), b2c, reads=[io["cmp_b2"]], writes=[b2c])
        S.dma("sp", lambda e, ty=ty: e.dma_start(out=b2row[:], in_=io["cmp_b2"].h[ty:ty + 1, :].partition_broadcast(128)), b2row, reads=[io["cmp_b2"]], writes=[b2row])
        for hc in range(2):
            ps = psS.get()
            for l in range(32):
                S.op("pe", lambda e, ps=ps, l=l, hc=hc: e.matmul(ps[:, 0:1], w1v[:, l, hc * 128:(hc + 1) * 128], peT[:, l:l + 1], start=(l == 0), stop=(l == 31)), reads=[qw, peT], writes=[ps])
            S.op("dve", lambda e, ps=ps, hc=hc: e.tensor_tensor(out=cb[:, hc:hc + 1], in0=ps[:, 0:1], in1=b1T[:, hc:hc + 1], op=ALU.add), reads=[ps, b1T], writes=[cb])
        for g in range(4):
            for hf in range(2):
                S.dma("sp", lambda e, ty=ty, g=g, hf=hf: e.dma_start(out=bigK[:, hf * 8192:(hf + 1) * 8192], in_=Kfull.h[ty, g, :, hf * 8192:(hf + 1) * 8192]), bigK, reads=[Kfull], writes=[bigK])
            for nh in range(2):
                for hc in range(2):
                    ps = psS.get()
                    for l in range(32):
                        a0 = nh * 512 * 16 + l
                        S.op("pe", lambda e, ps=ps, l=l, hc=hc, a0=a0: e.matmul(ps[:], w1v[:, l, hc * 128:(hc + 1) * 128], bigK[:, a0:a0 + 512 * 16:16], start=(l == 0), stop=(l == 31)), reads=[qw, bigK], writes=[ps])
                    h = hs.get(); x2 = hx.get()
                    S.op("act", lambda e, ps=ps, h=h, hc=hc: e.activation(out=h[:], in_=ps[:], func=AF.Identity, bias=cb[:, hc:hc + 1], scale=1.0), reads=[ps, cb], writes=[h])
                    S.op("dve", lambda e, h=h, x2=x2: e.tensor_tensor(out=x2[:], in0=h[:], in1=h[:], op=ALU.mult), reads=[h], writes=[x2])
                    S.op("dve", lambda e, x2=x2: e.tensor_scalar(out=x2[:], in0=x2[:], scalar1=0.044715, scalar2=1.0, op0=ALU.mult, op1=ALU.add), reads=[x2], writes=[x2])
                    S.op("dve", lambda e, h=h, x2=x2: e.tensor_tensor(out=x2[:], in0=x2[:], in1=h[:], op=ALU.mult), reads=[h, x2], writes=[x2])
                    S.op("act", lambda e, x2=x2: e.activation(out=x2[:], in_=x2[:], func=AF.Sigmoid, scale=GELU_C), reads=[x2], writes=[x2])
                    S.op("dve", lambda e, h=h, x2=x2, hc=hc, nh=nh: e.tensor_tensor(out=Hg[:, hc, nh * 512:(nh + 1) * 512], in0=x2[:], in1=h[:], op=ALU.mult), reads=[h, x2], writes=[Hg])
            if ty == 0:
                for nh in range(2):
                    ps = psS.get()
                    for hc in range(2):
                        S.op("pe", lambda e, ps=ps, hc=hc, nh=nh: e.matmul(ps[:], w2b[:, hc, :], Hg[:, hc, nh * 512:(nh + 1) * 512], start=(hc == 0), stop=(hc == 1)), reads=[w2b, Hg], writes=[ps])
                    S.op("act", lambda e, ps=ps, g=g, nh=nh: e.activation(out=kcT[:, g, nh * 512:(nh + 1) * 512], in_=ps[:], func=AF.Identity, bias=b2c[:], scale=1.0), reads=[ps, b2c], writes=[kcT])
            else:
                for ncn in range(8):
                    ps = psS.get()
                    for hc in range(2):
                        S.op("pe", lambda e, ps=ps, hc=hc, ncn=ncn: e.matmul(ps[:, 0:128], Hg[:, hc, ncn * 128:(ncn + 1) * 128], w2b[:, hc, :], start=(hc == 0), stop=(hc == 1)), reads=[w2b, Hg], writes=[ps])
                    S.op("dve", lambda e, ps=ps, g=g, ncn=ncn: e.tensor_tensor(out=vca[:, g, ncn, 0:128], in0=ps[:, 0:128], in1=b2row[:], op=ALU.add), reads=[ps, b2row], writes=[vca])

    kws = Pool(S, "kws", 2, [128, 12, 128], BF16)
    vws = Pool(S, "vws", 2, [128, 12, 129], BF16)
    for t in vws.t:
        S.op("dve", lambda e, t=t: e.memset(t[:, :, 128:129], 1.0), writes=[t])
    PTs = Pool(S, "PT", 4, [128, 4, 128], BF16)
    Osb = Pool(S, "Osb", 2, [128, 4, 129], F32)
    oacc = Pool(S, "oacc", 2, [128, 4, 128], F32)
    obf = Pool(S, "obf", 2, [128, 4, 128], BF16)
    aT = Pool(S, "aT", 2, [128, 4, 128], BF16)
    NBT4 = Pool(S, "NBT4", 2, [128, 2, 512], BF16)
    cm = Pool(S, "cm", 2, [128, 128], BF16)
    small = Pool(S, "small", 8, [128, 8], F32)
    impp = Pool(S, "imp", 2, [128, 256], F32)
    vv = Pool(S, "vv", 2, [128, 256], F32)
    ff = Pool(S, "ff", 2, [128, 256], F32)
    wk = Pool(S, "wk", 2, [128, 256], F32)
    nbb = Pool(S, "nbb", 2, [128, 256], BF16)
    m8p = Pool(S, "m8", 4, [128, 8], F32)

    def Oview(r):
        return psO[r // 2][:, (r % 2) * 129:(r % 2) * 129 + 129]

    def finish_branch(b, g, j, oa, first):
        osb = Osb.get()
        for k in range(2):
            S.op("act", lambda e, k=k: e.activation(out=osb[:, 2 * k:2 * k + 2, :], in_=psO[k][:, 0:258].rearrange("p (r c) -> p r c", c=129), func=AF.Copy), reads=[psO[k]], writes=[osb])
        if "dbg" in io:
            S.dma("sp", lambda e: e.dma_start(out=io["dbg"].h[b, j * 128:(j + 1) * 128, :], in_=osb[:].rearrange("p r c -> p (r c)")), osb, reads=[osb], writes=[io["dbg"]])
        sm = small.get()
        S.op("dve", lambda e: e.tensor_scalar(out=sm[:, 0:4], in0=osb[:, :, 128], scalar1=1e-30, scalar2=None, op0=ALU.max), reads=[osb], writes=[sm])
        S.op("dve", lambda e: e.reciprocal(out=sm[:, 0:4], in_=sm[:, 0:4]), reads=[sm], writes=[sm])
        c0 = 4 * g * 3 + b
        S.op("dve", lambda e: e.tensor_tensor(out=sm[:, 4:8], in0=sm[:, 0:4], in1=gn[:, j, c0:c0 + 10:3], op=ALU.mult), reads=[sm, gn], writes=[sm])
        for r in range(4):
            if first:
                S.op("dve", lambda e, r=r: e.tensor_scalar(out=oa[:, r, :], in0=osb[:, r, 0:128], scalar1=sm[:, 4 + r:5 + r], scalar2=None, op0=ALU.mult), reads=[osb, sm], writes=[oa])
            else:
                S.op("dve", lambda e, r=r: e.scalar_tensor_tensor(out=oa[:, r, :], in0=osb[:, r, 0:128], scalar=sm[:, 4 + r:5 + r], in1=oa[:, r, :], op0=ALU.mult, op1=ALU.add), reads=[osb, sm, oa], writes=[oa])
        return sm

    def pv(PT, vaug_t, vaug_ap, first, last):
        for r in range(4):
            S.op("pe", lambda e, r=r: e.matmul(Oview(r), PT[:, r, :], vaug_ap, start=(first and r % 2 == 0), stop=last, skip_group_check=True), reads=[PT, vaug_t], writes=[psO[r // 2]])

    def part1(g, j):
        tsl = slice(j * 128, (j + 1) * 128)
        oa = oacc.get()
        ncn = j // 2 + 1
        pend = []
        for nci in range(ncn):
            ps = psS.get()
            S.op("pe", lambda e, ps=ps, nci=nci, g=g: e.matmul(ps[:].rearrange("p (r t) -> p r t", t=128), kcT[:, g, nci * 128:(nci + 1) * 128], QTv[:, :, tsl], start=True, stop=True), reads=[kcT, qw], writes=[ps])
            PT = PTs.get()
            S.op("act", lambda e, ps=ps, PT=PT: e.activation(out=PT[:], in_=ps[:].rearrange("p (r t) -> p r t", t=128), func=AF.Exp, scale=SCALE), reads=[ps], writes=[PT])
            c = cm.get()
            S.op("dve", lambda e, c=c, nci=nci: e.tensor_scalar(out=c[:], in0=tidxrow[:, tsl], scalar1=nthr[:, nci:nci + 1], scalar2=None, op0=ALU.is_ge), reads=[tidxrow, nthr], writes=[c])
            S.op("dve", lambda e, c=c, PT=PT: e.tensor_tensor(out=PT[:], in0=PT[:], in1=c[:].unsqueeze(1).broadcast_to([128, 4, 128]), op=ALU.mult), reads=[PT, c], writes=[PT])
            def tail_c(PT=PT, nci=nci):
                pv(PT, vca, vca[:, g, nci, :], nci == 0, nci == ncn - 1)
                for r in range(4):
                    S.op("pe", lambda e, r=r, PT=PT, nci=nci: e.matmul(psI[r // 2][:, (r % 2) * 256:(r % 2) * 256 + 256], PT[:, r, :], Ov[:, nci, :], start=(nci == 0 and r % 2 == 0), stop=(nci == ncn - 1), skip_group_check=True), reads=[PT, Ov], writes=[psI[r // 2]])
            pend.append(tail_c)
            if len(pend) > LAG:
                pend.pop(0)()
        while pend:
            pend.pop(0)()
        sm = finish_branch(0, g, j, oa, True)
        imp = impp.get()
        for r in range(4):
            src = psI[r // 2][:, (r % 2) * 256:(r % 2) * 256 + 256]
            if r == 0:
                S.op("dve", lambda e, src=src: e.tensor_scalar(out=imp[:], in0=src, scalar1=sm[:, 0:1], scalar2=None, op0=ALU.mult), reads=[psI[0], sm], writes=[imp])
            else:
                S.op("dve", lambda e, src=src, r=r: e.scalar_tensor_tensor(out=imp[:], in0=src, scalar=sm[:, r:r + 1], in1=imp[:], op0=ALU.mult, op1=ALU.add), reads=[psI[r // 2], sm, imp], writes=[imp])
        v = vv.get(); f = ff.get(); w = wk.get(); m8a = m8p.get(); m8b = m8p.get(); nb = nbb.get()
        S.op("dve", lambda e: e.tensor_scalar(out=v[:], in0=blkrow[:], scalar1=curcol[:, j:j + 1], scalar2=None, op0=ALU.is_le), reads=[blkrow, curcol], writes=[v])
        S.op("dve", lambda e: e.tensor_tensor(out=imp[:], in0=imp[:], in1=baserow[:], op=ALU.subtract), reads=[imp, baserow], writes=[imp])
        S.op("dve", lambda e: e.tensor_tensor(out=imp[:], in0=imp[:], in1=v[:], op=ALU.mult), reads=[imp, v], writes=[imp])
        S.op("dve", lambda e: e.tensor_tensor(out=imp[:], in0=imp[:], in1=baserow[:], op=ALU.add), reads=[imp, baserow], writes=[imp])
        S.op("dve", lambda e: e.tensor_scalar(out=f[:], in0=blkrow[:], scalar1=curcol[:, j:j + 1], scalar2=1e30, op0=ALU.is_equal, op1=ALU.mult), reads=[blkrow, curcol], writes=[f])
        S.op("dve", lambda e: e.tensor_tensor(out=imp[:], in0=imp[:], in1=f[:], op=ALU.max), reads=[imp, f], writes=[imp])
        S.op("dve", lambda e: e.memset(imp[:, 0:1], 2e30), reads=[imp], writes=[imp])
        S.op("dve", lambda e: e.max(out=m8a[:], in_=imp[:]), reads=[imp], writes=[m8a])
        S.op("dve", lambda e: e.match_replace(out=w[:], in_to_replace=m8a[:], in_values=imp[:], imm_value=-1e30), reads=[imp, m8a], writes=[w])
        S.op("dve", lambda e: e.max(out=m8b[:], in_=w[:]), reads=[w], writes=[m8b])
        S.op("dve", lambda e: e.tensor_scalar(out=f[:], in0=imp[:], scalar1=m8b[:, 7:8], scalar2=None, op0=ALU.is_ge), reads=[imp, m8b], writes=[f])
        S.op("dve", lambda e: e.tensor_tensor(out=f[:], in0=f[:], in1=v[:], op=ALU.mult), reads=[f, v], writes=[f])
        if "dbg2" in io:
            S.dma("sp", lambda e: e.dma_start(out=io["dbg2"].h[j * 128:(j + 1) * 128, :], in_=f[:]), f, reads=[f], writes=[io["dbg2"]])
        S.op("dve", lambda e: e.tensor_scalar(out=nb[:], in0=f[:], scalar1=-1.0, scalar2=30000.0, op0=ALU.add, op1=ALU.mult), reads=[f], writes=[nb])
        for c2 in range(2):
            S.op("pe", lambda e, c2=c2: e.transpose(psT[:, c2 * 128:(c2 + 1) * 128], nb[:, c2 * 128:(c2 + 1) * 128], ident[:]), reads=[nb, ident], writes=[psT])
        nbt = NBT4.get()
        for r in range(4):
            eng = "act" if r % 2 == 0 else "dve"
            if eng == "act":
                S.op("act", lambda e, r=r: e.activation(out=nbt[:, :, r * 128:(r + 1) * 128], in_=psT[:, 0:256].rearrange("p (c t) -> p c t", t=128), func=AF.Copy), reads=[psT], writes=[nbt])
            else:
                S.op("dve", lambda e, r=r: e.tensor_copy(out=nbt[:, :, r * 128:(r + 1) * 128], in_=psT[:, 0:256].rearrange("p (c t) -> p c t", t=128)), reads=[psT], writes=[nbt])
        return (oa, nbt)

    def part2(g, j, st):
        oa, nbt = st
        tsl = slice(j * 128, (j + 1) * 128)
        pend = []
        nkt = 8 * j + 8
        for kt in range(nkt):
            ps = psS.get()
            diag = kt >= 8 * j
            S.op("pe", lambda e, ps=ps, kt=kt: e.matmul(ps[:].rearrange("p (r t) -> p r t", t=128), bigK[:, kt * 128:(kt + 1) * 128], QTv[:, :, tsl], start=True, stop=False), reads=[bigK, qw], writes=[ps])
            S.op("pe", lambda e, ps=ps, kt=kt, diag=diag: e.matmul(ps[:], Eall[:, kt % 64, :], nbt[:, kt // 64, :], start=False, stop=(not diag)), reads=[Eall, nbt], writes=[ps])
            if diag:
                S.op("pe", lambda e, ps=ps, kt=kt: e.matmul(ps[:], ident[:], DB4[:, kt - 8 * j, :], start=False, stop=True), reads=[ident, DB4], writes=[ps])
            PT = PTs.get()
            S.op("act", lambda e, ps=ps, PT=PT: e.activation(out=PT[:], in_=ps[:].rearrange("p (r t) -> p r t", t=128), func=AF.Exp, scale=SCALE), reads=[ps], writes=[PT])
            def tail_s(PT=PT, kt=kt):
                pv(PT, vsa, vsa[:, kt, :], kt == 0, kt == nkt - 1)
            pend.append(tail_s)
            if len(pend) > LAG:
                pend.pop(0)()
        while pend:
            pend.pop(0)()
        finish_branch(1, g, j, oa, False)
        kw = kws.get(); vw = vws.get()
        m0 = 4 if j == 0 else 0
        k0 = 8 * j - 4 + m0
        nm = 12 - m0
        S.dma("sp", lambda e, g=g: e.dma_start(out=kw[:, m0:12, :], in_=Kfull.h[3, g, :, k0 * 128:(k0 + nm) * 128].rearrange("d (m k) -> d m k", k=128)), kw, reads=[Kfull], writes=[kw])
        S.dma("sp", lambda e, g=g: e.dma_start(out=vw[:, m0:12, 0:128], in_=Vfull.h[1, k0 * 128:(k0 + nm) * 128, g * 128:(g + 1) * 128].rearrange("(m p) d -> p m d", p=128)), vw, reads=[Vfull], writes=[vw])
        for m in range(m0, 12):
            ps = psS.get()
            S.op("pe", lambda e, ps=ps, m=m: e.matmul(ps[:].rearrange("p (r t) -> p r t", t=128), kw[:, m, :], QTv[:, :, tsl], start=True, stop=False), reads=[kw, qw], writes=[ps])
            S.op("pe", lambda e, ps=ps, m=m: e.matmul(ps[:], ident[:], WB4[:, m, :], start=False, stop=True), reads=[ident, WB4], writes=[ps])
            PT = PTs.get()
            S.op("act", lambda e, ps=ps, PT=PT: e.activation(out=PT[:], in_=ps[:].rearrange("p (r t) -> p r t", t=128), func=AF.Exp, scale=SCALE), reads=[ps], writes=[PT])
            def tail_w(PT=PT, m=m):
                pv(PT, vw, vw[:, m, :], m == m0, m == 11)
            pend.append(tail_w)
            if len(pend) > LAG:
                pend.pop(0)()
        while pend:
            pend.pop(0)()
        finish_branch(2, g, j, oa, False)
        ob = obf.get()
        S.op("act", lambda e: e.activation(out=ob[:], in_=oa[:], func=AF.Copy), reads=[oa], writes=[ob])
        for r in range(4):
            S.op("pe", lambda e, r=r: e.transpose(psT[:, 256 + r * 128:256 + (r + 1) * 128], ob[:, r, :], ident[:]), reads=[ob, ident], writes=[psT])
        at = aT.get()
        S.op("dve", lambda e: e.tensor_copy(out=at[:], in_=psT[:, 256:768].rearrange("p (r t) -> p r t", t=128)), reads=[psT], writes=[at])
        S.dma("sp", lambda e, g=g: e.dma_start(out=attnT.h[g * 512:(g + 1) * 512, tsl].rearrange("(r d) t -> d r t", d=128), in_=at[:]), at, reads=[at], writes=[attnT])


    for g in groups:
        for hf in range(2):
            S.dma("sp", lambda e, g=g, hf=hf: e.dma_start(out=bigK[:, hf * 8192:(hf + 1) * 8192], in_=Kfull.h[2, g, :, hf * 8192:(hf + 1) * 8192]), bigK, reads=[Kfull], writes=[bigK])
        for q8 in range(8):
            S.dma("sp", lambda e, g=g, q8=q8: e.dma_start(out=vsa[:, q8 * 16:(q8 + 1) * 16, 0:128], in_=Vfull.h[0, q8 * 2048:(q8 + 1) * 2048, g * 128:(g + 1) * 128].rearrange("(kt p) d -> p kt d", p=128)), vsa, reads=[Vfull], writes=[vsa])
        S.dma("sp", lambda e, g=g: e.dma_start(out=QTv[:], in_=qT.h[g * 512:(g + 1) * 512, :].rearrange("(r d) t -> d r t", d=128)), qw, reads=[qT], writes=[qw])
        st = part1(g, slots[0])
        for i, j in enumerate(slots):
            nxt = part1(g, slots[i + 1]) if i + 1 < len(slots) else None
            part2(g, j, st)
            st = nxt
    S.final_wait("sp", [attnT])


DFF = 8192


def build_B3(S, io):
    xT_d, bgT, uT, utail, attnT, gaT, gbT, xo = (io[k] for k in ("xT", "bgT", "uT", "utail", "attnT", "gaT", "gbT", "xT_out"))
    ones_b = S.sb("ones_b", [128, 128], BF16)
    S.op("dve", lambda e: e.memset(ones_b[:], 1.0), writes=[ones_b])
    modT = S.sb("modT", [128, 96], F32)
    gT = S.sb("gT", [128, 64], F32)
    S.dma("sp", lambda e: e.dma_start(out=modT[:], in_=io["modT"][:]), modT, reads=[io["modT"]], writes=[modT])
    S.dma("sp", lambda e: e.dma_start(out=gT[:], in_=io["gainsT"][:]), gT, reads=[io["gainsT"]], writes=[gT])
    cw = S.sb("cw", [128, 3, 8], F32)
    for k in range(3):
        S.dma("sp", lambda e, k=k: e.dma_start(out=cw[:, k, :], in_=io["conv_w"].h[k].rearrange("(cc p) -> p cc", p=128), allow_slow_non_contiguous=True), cw, reads=[io["conv_w"]], writes=[cw])
    ag1 = S.sb("ag1", [128, 16], F32); a2 = S.sb("a2", [128, 16], F32); b2 = S.sb("b2", [128, 16], F32); ag3 = S.sb("ag3", [128, 16], F32)
    S.op("dve", lambda e: e.tensor_tensor(out=ag1[:], in0=modT[:, 32:48], in1=gT[:, 16:32], op=ALU.mult), reads=[modT, gT], writes=[ag1])
    S.op("dve", lambda e: e.scalar_tensor_tensor(out=a2[:], in0=modT[:, 64:80], scalar=1.0, in1=gT[:, 32:48], op0=ALU.add, op1=ALU.mult), reads=[modT, gT], writes=[a2])
    S.op("dve", lambda e: e.tensor_copy(out=b2[:], in_=modT[:, 48:64]), reads=[modT], writes=[b2])
    S.op("dve", lambda e: e.tensor_tensor(out=ag3[:], in0=modT[:, 80:96], in1=gT[:, 48:64], op=ALU.mult), reads=[modT, gT], writes=[ag3])

    xt = S.sb("xt", [128, KC, NT], F32)
    ysb = S.sb("ysb", [128, KC, NT], F32)
    hid = S.sb("hid", [128, 64, NT], BF16)
    mg = S.sb("mg", [128, KC, NT], BF16)
    wts = Pool(S, "wt", 2, [128, KC, 512], BF16)
    P = {"ps": Pool(S, "ps", 8, [128, 512], F32, psum=True), "sq": Pool(S, "sq", 2, [128, NT], BF16),
         "rstd": Pool(S, "rstd", 1, [128, NT], F32), "tmp": Pool(S, "tmp", 3, [128, NT], F32)}
    P["eps"] = S.sb("epsc", [128, 1], F32)
    S.op("dve", lambda e: e.memset(P["eps"][:], EPS), writes=[P["eps"]])
    uext = Pool(S, "uext", 2, [128, 4, 130], F32)
    bgc = Pool(S, "bgc", 2, [128, NT], F32)
    zt = Pool(S, "zt", 2, [128, 4, 128], F32)
    gch = Pool(S, "gch", 4, [128, NT], F32)

    def load_w(wd, r0, nk, c0):
        wt = wts.get()
        src = wd.h[r0:r0 + nk * 128, c0:c0 + 512].rearrange("(kc p) n -> p kc n", p=128)
        hk = nk // 2
        for half in range(2):
            S.dma("pool", lambda e, half=half: e.dma_start(out=wt[:, half * hk:(half + 1) * hk, :], in_=src[:, half * hk:(half + 1) * hk, :]), wt, reads=[wd], writes=[wt])
        return wt

    def post_norm_residual(coef):
        ss = P["ps"].get()
        for kc in range(KC):
            s = P["sq"].get()
            S.op("act", lambda e, s=s, kc=kc: e.activation(out=s[:], in_=ysb[:, kc, :], func=AF.Square), reads=[ysb], writes=[s])
            S.op("pe", lambda e, s=s, kc=kc: e.matmul(ss[:], ones_b[:], s[:], start=(kc == 0), stop=(kc == KC - 1)), reads=[s, ones_b], writes=[ss])
        r = P["rstd"].get()
        S.op("act", lambda e: e.activation(out=r[:], in_=ss[:], func=AF.Sqrt, bias=P["eps"][:], scale=1.0 / D), reads=[ss, P["eps"]], writes=[r])
        S.op("dve", lambda e: e.reciprocal(out=r[:], in_=r[:]), reads=[r], writes=[r])
        for kc in range(KC):
            t = P["tmp"].get()
            S.op("dve", lambda e, t=t, kc=kc: e.scalar_tensor_tensor(out=t[:], in0=ysb[:, kc, :], scalar=coef[:, kc:kc + 1], in1=r[:], op0=ALU.mult, op1=ALU.mult), reads=[ysb, coef, r], writes=[t])
            S.op("pool", lambda e, t=t, kc=kc: e.tensor_tensor(out=xt[:, kc, :], in0=xt[:, kc, :], in1=t[:], op=ALU.add), reads=[xt, t], writes=[xt])

    def tile(ti):
        tsl = slice(ti * NT, (ti + 1) * NT)
        load_x_tile(S, xT_d, xt, ti)
        for q4 in range(4):
            S.dma("sp", lambda e, q4=q4: e.dma_start(out=hid[:, q4 * 4:(q4 + 1) * 4, :], in_=attnT.h[q4 * 512:(q4 + 1) * 512, tsl].rearrange("(hc p) t -> p hc t", p=128)), hid, reads=[attnT], writes=[hid])
        for cc in range(8):
            ue = uext.get(); bg = bgc.get(); z = zt.get()
            S.dma("sp", lambda e, ue=ue, cc=cc: e.dma_start(out=ue[:, :, 2:130], in_=uT.h[cc * 128:(cc + 1) * 128, tsl].rearrange("p (b t) -> p b t", t=128)), ue, reads=[uT], writes=[ue])
            S.dma("sp", lambda e, ue=ue, cc=cc: e.dma_start(out=ue[:, :, 0:2], in_=utail.h[cc * 128:(cc + 1) * 128, ti * 4:(ti + 1) * 4, :]), ue, reads=[utail], writes=[ue])
            S.dma("sp", lambda e, bg=bg, cc=cc: e.dma_start(out=bg[:], in_=bgT.h[cc * 128:(cc + 1) * 128, tsl]), bg, reads=[bgT], writes=[bg])
            S.op("dve", lambda e, ue=ue, z=z, cc=cc: e.tensor_scalar(out=z[:], in0=ue[:, :, 2:130], scalar1=cw[:, 2, cc:cc + 1], scalar2=None, op0=ALU.mult), reads=[ue, cw], writes=[z])
            S.op("dve", lambda e, ue=ue, z=z, cc=cc: e.scalar_tensor_tensor(out=z[:], in0=ue[:, :, 1:129], scalar=cw[:, 1, cc:cc + 1], in1=z[:], op0=ALU.mult, op1=ALU.add), reads=[ue, cw, z], writes=[z])
            S.op("dve", lambda e, ue=ue, z=z, cc=cc: e.scalar_tensor_tensor(out=z[:], in0=ue[:, :, 0:128], scalar=cw[:, 0, cc:cc + 1], in1=z[:], op0=ALU.mult, op1=ALU.add), reads=[ue, cw, z], writes=[z])
            S.op("dve", lambda e, bg=bg, z=z, cc=cc: e.tensor_tensor(out=hid[:, 16 + cc, :], in0=z[:].rearrange("p b t -> p (b t)"), in1=bg[:], op=ALU.mult), reads=[z, bg], writes=[hid])
        wa = load_w(io["w_conv_out"], 0, 8, 0)
        wb = load_w(io["w_nsa_out"], 0, 16, 0)
        for og in range(4):
            pas = []; pbs = []
            for c4 in range(4):
                pa = P["ps"].get(); pas.append(pa)
                for cc in range(8):
                    S.op("pe", lambda e, cc=cc, pa=pa, c4=c4, wa=wa: e.matmul(pa[:], wa[:, cc, c4 * 128:(c4 + 1) * 128], hid[:, 16 + cc, :], start=(cc == 0), stop=(cc == 7)), reads=[wa, hid], writes=[pa])
            wa_n = load_w(io["w_conv_out"], 0, 8, (og + 1) * 512) if og < 3 else None
            for c4 in range(4):
                pb = P["ps"].get(); pbs.append(pb)
                for hc in range(16):
                    S.op("pe", lambda e, hc=hc, pb=pb, c4=c4, wb=wb: e.matmul(pb[:], wb[:, hc, c4 * 128:(c4 + 1) * 128], hid[:, hc, :], start=(hc == 0), stop=(hc == 15)), reads=[wb, hid], writes=[pb])
            wb_n = load_w(io["w_nsa_out"], 0, 16, (og + 1) * 512) if og < 3 else None
            for c4 in range(4):
                oc = og * 4 + c4
                pa = pas[c4]; pb = pbs[c4]
                ga = gch.get(); gb = gch.get()
                S.dma("sp", lambda e, ga=ga, oc=oc: e.dma_start(out=ga[:], in_=gaT.h[oc * 128:(oc + 1) * 128, tsl]), ga, reads=[gaT], writes=[ga])
                S.dma("sp", lambda e, gb=gb, oc=oc: e.dma_start(out=gb[:], in_=gbT.h[oc * 128:(oc + 1) * 128, tsl]), gb, reads=[gbT], writes=[gb])
                S.op("dve", lambda e, ga=ga, pa=pa: e.tensor_tensor(out=ga[:], in0=pa[:], in1=ga[:], op=ALU.mult), reads=[pa, ga], writes=[ga])
                S.op("dve", lambda e, gb=gb, pb=pb: e.tensor_tensor(out=gb[:], in0=pb[:], in1=gb[:], op=ALU.mult), reads=[pb, gb], writes=[gb])
                S.op("pool", lambda e, ga=ga, gb=gb, oc=oc: e.tensor_tensor(out=mg[:, oc, :], in0=ga[:], in1=gb[:], op=ALU.add), reads=[ga, gb], writes=[mg])
            wa, wb = wa_n, wb_n
        for og in range(4):
            wo = load_w(io["w_out"], 0, 16, og * 512)
            for c4 in range(4):
                ps = P["ps"].get()
                for kc in range(KC):
                    S.op("pe", lambda e, kc=kc, ps=ps, c4=c4, wo=wo: e.matmul(ps[:], wo[:, kc, c4 * 128:(c4 + 1) * 128], mg[:, kc, :], start=(kc == 0), stop=(kc == KC - 1)), reads=[wo, mg], writes=[ps])
                S.op("act", lambda e, ps=ps, og=og, c4=c4: e.activation(out=ysb[:, og * 4 + c4, :], in_=ps[:], func=AF.Copy), reads=[ps], writes=[ysb])
        post_norm_residual(ag1)
        if "xmid" in io:
            for q4 in range(4):
                S.dma("sp", lambda e, q4=q4: e.dma_start(out=io["xmid"].h.rearrange("(kc p) t -> p kc t", p=128)[:, q4 * 4:(q4 + 1) * 4, tsl], in_=xt[:, q4 * 4:(q4 + 1) * 4, :]), xt, reads=[xt], writes=[io["xmid"]])
        rms_affine(S, xt, mg, 0, a2, b2, ones_b, P)
        for fg in range(16):
            wu = load_w(io["w_mlp_up"], 0, 16, fg * 512)
            for c4 in range(4):
                ps = P["ps"].get()
                for kc in range(KC):
                    S.op("pe", lambda e, kc=kc, ps=ps, c4=c4, wu=wu: e.matmul(ps[:], wu[:, kc, c4 * 128:(c4 + 1) * 128], mg[:, kc, :], start=(kc == 0), stop=(kc == KC - 1)), reads=[wu, mg], writes=[ps])
                t = P["tmp"].get()
                S.op("act", lambda e, ps=ps, t=t: e.activation(out=t[:], in_=ps[:], func=AF.Relu), reads=[ps], writes=[t])
                S.op("dve", lambda e, t=t, fg=fg, c4=c4: e.tensor_tensor(out=hid[:, fg * 4 + c4, :], in0=t[:], in1=t[:], op=ALU.mult), reads=[t], writes=[hid])
        for og in range(4):
            pss = [P["ps"].get() for _ in range(4)]
            for slab in range(4):
                wd = load_w(io["w_mlp_down"], slab * 2048, 16, og * 512)
                for c4 in range(4):
                    for fcl in range(16):
                        S.op("pe", lambda e, fcl=fcl, c4=c4, wd=wd, slab=slab, pss=pss: e.matmul(pss[c4][:], wd[:, fcl, c4 * 128:(c4 + 1) * 128], hid[:, slab * 16 + fcl, :], start=(slab == 0 and fcl == 0), stop=(slab == 3 and fcl == 15)), reads=[wd, hid], writes=[pss[c4]])
            for c4 in range(4):
                S.op("act", lambda e, c4=c4, og=og, pss=pss: e.activation(out=ysb[:, og * 4 + c4, :], in_=pss[c4][:], func=AF.Copy), reads=[pss[c4]], writes=[ysb])
        post_norm_residual(ag3)
        for q4 in range(4):
            S.dma("sp", lambda e, q4=q4: e.dma_start(out=xo.h.rearrange("(kc p) t -> p kc t", p=128)[:, q4 * 4:(q4 + 1) * 4, tsl], in_=xt[:, q4 * 4:(q4 + 1) * 4, :]), xt, reads=[xt], writes=[xo])

    for ti in range(NTILE):
        tile(ti)
    S.final_wait("sp", [xo] + ([io["xmid"]] if "xmid" in io else []))


MCOLS = 6144


def build_M(S, io):
    cT = S.sb("cT", [128, 16], F32)
    S.dma("sp", lambda e: e.dma_start(out=cT[:], in_=io["cT"][:]), cT, reads=[io["cT"]], writes=[cT])
    S.op("act", lambda e: e.activation(out=cT[:], in_=cT[:], func=AF.Silu), reads=[cT], writes=[cT])
    brow = S.sb("brow", [1, MCOLS], F32)
    S.dma("sp", lambda e: e.dma_start(out=brow[:], in_=io["ada_b"][:]), brow, reads=[io["ada_b"]], writes=[brow])
    orow = S.sb("orow", [1, MCOLS], F32)
    wts = Pool(S, "wm", 2, [128, 16, 512], F32)
    pss = Pool(S, "psm", 2, [128, 512], F32, psum=True)
    wsrc = io["ada_w"].h.rearrange("(kc p) n -> p kc n", p=128)
    for gi in range(MCOLS // 512):
        wt = wts.get()
        for half in range(2):
            S.dma("sp", lambda e, wt=wt, half=half, gi=gi: e.dma_start(out=wt[:, half * 8:(half + 1) * 8, :], in_=wsrc[:, half * 8:(half + 1) * 8, gi * 512:(gi + 1) * 512]), wt, reads=[io["ada_w"]], writes=[wt])
        ps = pss.get()
        for kc in range(16):
            S.op("pe", lambda e, wt=wt, ps=ps, kc=kc: e.matmul(ps[0:1, :], cT[:, kc:kc + 1], wt[:, kc, :], start=(kc == 0), stop=(kc == 15)), reads=[cT, wt], writes=[ps])
        S.op("dve", lambda e, ps=ps, gi=gi: e.tensor_tensor(out=orow[:, gi * 512:(gi + 1) * 512], in0=ps[0:1, :], in1=brow[:, gi * 512:(gi + 1) * 512], op=ALU.add), reads=[ps, brow], writes=[orow])
    S.dma("sp", lambda e: e.dma_start(out=io["mod"][:], in_=orow[:]), orow, reads=[orow], writes=[io["mod"]])
    S.final_wait("sp", [io["mod"]])

bf = ml_dtypes.bfloat16
SEQ = 16384; TC = 2048

def tok_idx(c):
    return np.concatenate([np.arange((8 * j + c) * 128, (8 * j + c + 1) * 128) for j in range(16)])

def consts_common():
    n = np.arange(1024)[:, None]; jb = np.arange(256)[None, :]
    ov = np.clip(np.minimum(16 * n + 32, 64 * jb + 64) - np.maximum(16 * n, 64 * jb), 0, None).astype(np.float32) / 32
    ov[1023:] = 0
    Ov = np.ascontiguousarray(ov.reshape(8, 128, 256).transpose(1, 0, 2)).astype(bf)
    Eall = np.zeros((128, 64, 128), np.float32)
    for i in range(64):
        Eall[2 * i, i, 0:64] = 1; Eall[2 * i + 1, i, 64:128] = 1
    p = np.arange(128)[:, None]
    nthr = (16 * (np.arange(8)[None, :] * 128 + p) + 31).astype(np.float32)
    blkrow = np.broadcast_to(np.arange(256, dtype=np.float32)[None, :], (128, 256)).copy()
    baserow = -(blkrow + 2)
    invf = (500000.0 ** (-np.arange(0, 32, 2, dtype=np.float32) / 32)).astype(np.float32)
    invf2 = np.zeros((32, 2), np.float32); invf2[:, 0] = np.tile(invf, 2); invf2[:16, 1] = -1; invf2[16:, 1] = 1
    return {"Ov": Ov, "Eall": Eall.astype(bf), "nthr": nthr, "blkrow": blkrow, "baserow": baserow, "invf": invf2}

def consts_core(c):
    i = np.arange(128)[:, None]; ip = np.arange(128)[None, :]
    NEG = -30000.0
    causal = np.where(i <= ip, 0.0, NEG).astype(np.float32)
    band = np.where(i > ip, 0.0, NEG).astype(np.float32)
    full = np.zeros((128, 128), np.float32); none = np.full((128, 128), NEG, np.float32)
    DB = np.zeros((128, 8, 4, 128), np.float32)
    for kk in range(8):
        DB[:, kk] = (causal if kk == c else full)[:, None, :]
    WB = np.zeros((128, 12, 4, 128), np.float32)
    for m in range(12):
        d = m - 4 - c
        t = none if d < -4 else band if d == -4 else full if d < 0 else causal if d == 0 else none
        WB[:, m] = t[:, None, :]
    ti = tok_idx(c)
    tidxrow = np.broadcast_to(ti.astype(np.float32)[None, :], (128, TC)).copy()
    curcol = (ti.reshape(16, 128).T // 64).astype(np.float32)
    return {"DB4": DB.reshape(128, 8, 512).astype(bf), "WB4": WB.reshape(128, 12, 512).astype(bf), "tidxrow": tidxrow, "curcol": np.ascontiguousarray(curcol)}

def gelu_tanh(x):
    return 0.5 * x * (1 + np.tanh(np.sqrt(2 / np.pi) * (x + 0.044715 * x ** 3)))

def ref_compress(rawT, pe, w1, b1, w2, b2):
    kv = rawT.T
    idx = np.arange(1023)[:, None] * 16 + np.arange(32)[None, :]
    blocks = (kv[idx] + pe[None]).reshape(1023, 4096)
    h = gelu_tanh(blocks @ w1 + b1)
    return h @ w2 + b2

def ref_attn_block(gb, q, kc, vc, ks, vs, kw, vw, gate):
    scale = 128 ** -0.5
    t = gb * 128 + np.arange(128)
    sc = np.einsum('trd,nd->rtn', q, kc) * scale
    cmp_end = np.arange(1023) * 16 + 31
    m_c = cmp_end[None, :] <= t[:, None]
    scm = np.where(m_c[None], sc, -1e30)
    e = np.exp(scm - scm.max(-1, keepdims=True)); p = e / e.sum(-1, keepdims=True)
    p_c = np.where(m_c[None], p, 0.0)
    o_c = np.einsum('rtn,nd->trd', p_c, vc)
    n = np.arange(1023)[:, None]; jb = np.arange(256)[None, :]
    ov = np.clip(np.minimum(16 * n + 32, 64 * jb + 64) - np.maximum(16 * n, 64 * jb), 0, None) / 32.0
    imp = np.einsum('rtn,nj->tj', p_c, ov)
    cur = t // 64
    blk = np.arange(256)
    forced = (blk[None, :] == cur[:, None]) | (blk[None, :] == 0)
    valid = blk[None, :] <= cur[:, None]
    imp = np.where(forced, 1e30, np.where(valid, imp, -1e30))
    sel = np.argsort(-imp, axis=-1, kind='stable')[:, :16]
    o_s = np.zeros((128, 4, 128));
    for i in range(128):
        kpos = (sel[i][:, None] * 64 + np.arange(64)[None, :]).reshape(-1)
        msk = kpos <= t[i]
        s = (q[i] @ ks[kpos].T) * scale
        s = np.where(msk[None], s, -1e30)
        e = np.exp(s - s.max(-1, keepdims=True)); pp = e / e.sum(-1, keepdims=True)
        o_s[i] = pp @ vs[kpos]
    o_w = np.zeros((128, 4, 128))
    for i in range(128):
        lo = max(0, t[i] - 511)
        s = (q[i] @ kw[lo:t[i] + 1].T) * scale
        e = np.exp(s - s.max(-1, keepdims=True)); pp = e / e.sum(-1, keepdims=True)
        o_w[i] = pp @ vw[lo:t[i] + 1]
    return gate[..., 0:1] * o_c + gate[..., 1:2] * o_s + gate[..., 2:3] * o_w, (o_c, o_s, o_w, sel)

_PROGS = {}
NCORES = 8


def _prog(name):
    if name in _PROGS:
        return _PROGS[name]
    nc = bass.Bass("TRN2", target_bir_lowering=False)
    with ExitStack() as es:
        S = Sched(nc, es)
        io = {}

        def din(n, shape, dt):
            io[n] = S.dram(n, shape, dt, kind="ExternalInput")

        def dout(n, shape, dt):
            io[n] = S.dram(n, shape, dt, kind="ExternalOutput")
        if name == "M":
            din("cT", [128, 16], F32); din("ada_w", [2048, MCOLS], F32); din("ada_b", [1, MCOLS], F32)
            dout("mod", [1, MCOLS], F32)
            build_M(S, io)
        elif name == "A":
            din("xT", [D, TC], F32); din("modT", [128, 96], F32); din("gainsT", [128, 64], F32)
            din("pos", [32, TC], I32); din("invf", [32, 2], F32); din("w_in", [1, D, INW], F32)
            dout("bgT", [1024, TC], F32); dout("uT", [1024, TC], F32); dout("qT", [2048, TC], BF16)
            dout("kT", [4, 4, 128, TC], BF16); dout("vtok", [2, TC, 512], BF16); dout("gn", [TC, 48], F32)
            dout("gaT", [2048, TC], F32); dout("gbT", [2048, TC], F32)
            build_A(S, io, 0)
        elif name == "B2":
            din("Kfull", [4, 4, 128, SEQ], BF16); din("Vfull", [2, SEQ, 512], BF16); din("qT", [2048, TC], BF16); din("gn", [TC, 48], F32)
            din("cmp_pe", [2, 32, 128], F32); din("cmp_w1", [2, 4096, 256], F32); din("cmp_b1", [2, 256], F32)
            din("cmp_w2", [2, 256, 128], F32); din("cmp_b2", [2, 128], F32)
            din("Ov", [128, 8, 256], BF16); din("Eall", [128, 64, 128], BF16); din("DB4", [128, 8, 512], BF16); din("WB4", [128, 12, 512], BF16)
            din("tidxrow", [128, TC], F32); din("curcol", [128, 16], F32); din("nthr", [128, 8], F32)
            din("blkrow", [128, 256], F32); din("baserow", [128, 256], F32)
            dout("attnT", [2048, TC], BF16)
            build_B2(S, io)
        elif name == "B3":
            din("xT", [D, TC], F32); din("bgT", [1024, TC], F32); din("uT", [1024, TC], F32); din("utail", [1024, 16, 2], F32)
            din("attnT", [2048, TC], BF16); din("gaT", [2048, TC], F32); din("gbT", [2048, TC], F32)
            din("modT", [128, 96], F32); din("gainsT", [128, 64], F32); din("conv_w", [3, 1024], F32)
            din("w_conv_out", [1024, 2048], F32); din("w_nsa_out", [2048, 2048], F32); din("w_out", [2048, 2048], F32)
            din("w_mlp_up", [2048, DFF], F32); din("w_mlp_down", [DFF, 2048], F32)
            dout("xT_out", [D, TC], F32)
            build_B3(S, io)
        S.emit()
    _PROGS[name] = nc
    return nc


def _run(name, in_maps):
    nc = _prog(name)
    res = run_bass_kernel_spmd(nc, in_maps, core_ids=list(range(NCORES)))
    return res.results


def kernel(x, c, positions, ada_w, ada_b, norm_gains, w_in, conv_w, w_conv_out, cmp_pe, cmp_w1, cmp_b1,
           cmp_w2, cmp_b2, w_nsa_out, w_out, w_mlp_up, w_mlp_down):
    f32 = lambda a: np.ascontiguousarray(np.asarray(a), dtype=np.float32)
    x = f32(x); c = f32(c); positions = np.asarray(positions).astype(np.int32)
    ada_w = np.asarray(ada_w); ada_b = np.asarray(ada_b); norm_gains = f32(norm_gains)
    DEPTH = ada_w.shape[0]
    toks = [tok_idx(cc) for cc in range(NCORES)]
    cc_ = consts_common()
    ccore = [consts_core(cc) for cc in range(NCORES)]
    cT = np.ascontiguousarray(c[0].reshape(16, 128).T)
    in_maps = []
    per_layer = 12288 // MCOLS
    for cc in range(NCORES):
        l, h = cc // per_layer, cc % per_layer
        in_maps.append({"cT": cT, "ada_w": f32(ada_w[l][:, h * MCOLS:(h + 1) * MCOLS]), "ada_b": f32(ada_b[l][None, h * MCOLS:(h + 1) * MCOLS])})
    resM = _run("M", in_maps)
    mods = [np.concatenate([resM[l * per_layer + h]["mod"][0] for h in range(per_layer)]) for l in range(DEPTH)]
    xT = [np.ascontiguousarray(x[0][toks[cc]].T) for cc in range(NCORES)]
    pos = [np.ascontiguousarray(np.broadcast_to(positions[0][toks[cc]][None, :], (32, TC))).astype(np.int32) for cc in range(NCORES)]
    for l in range(DEPTH):
        modT = np.ascontiguousarray(mods[l].reshape(96, 128).T)
        gainsT = np.ascontiguousarray(norm_gains[l].reshape(64, 128).T)
        w_in_l = f32(np.asarray(w_in)[l:l + 1])
        resA = _run("A", [{"xT": xT[cc], "modT": modT, "gainsT": gainsT, "pos": pos[cc], "invf": cc_["invf"], "w_in": w_in_l} for cc in range(NCORES)])
        Kfull = np.zeros((4, 4, 128, SEQ), bf); Vfull = np.zeros((2, SEQ, 512), bf)
        for cc in range(NCORES):
            Kfull[:, :, :, toks[cc]] = resA[cc]["kT"]
            Vfull[:, toks[cc], :] = resA[cc]["vtok"]
        utails = []
        for cc in range(NCORES):
            ut = np.zeros((1024, 16, 2), np.float32)
            for j in range(16):
                gb = 8 * j + cc
                if gb == 0:
                    continue
                pc, pj = (gb - 1) % 8, (gb - 1) // 8
                ut[:, j, :] = resA[pc]["uT"][:, pj * 128 + 126:pj * 128 + 128]
            utails.append(ut)
        cmpw = {k: f32(np.asarray(v)[l]) for k, v in (("cmp_pe", cmp_pe), ("cmp_w1", cmp_w1), ("cmp_b1", cmp_b1), ("cmp_w2", cmp_w2), ("cmp_b2", cmp_b2))}
        in_maps = []
        for cc in range(NCORES):
            m = {"Kfull": Kfull, "Vfull": Vfull, "qT": resA[cc]["qT"], "gn": resA[cc]["gn"]}
            m.update(cmpw)
            for k in ("Ov", "Eall", "nthr", "blkrow", "baserow"):
                m[k] = cc_[k]
            m.update(ccore[cc])
            in_maps.append(m)
        resB2 = _run("B2", in_maps)
        wl = {k: f32(np.asarray(v)[l]) for k, v in (("conv_w", conv_w), ("w_conv_out", w_conv_out), ("w_nsa_out", w_nsa_out), ("w_out", w_out), ("w_mlp_up", w_mlp_up), ("w_mlp_down", w_mlp_down))}
        in_maps = []
        for cc in range(NCORES):
            m = {"xT": xT[cc], "bgT": resA[cc]["bgT"], "uT": resA[cc]["uT"], "utail": utails[cc], "attnT": resB2[cc]["attnT"],
                 "gaT": resA[cc]["gaT"], "gbT": resA[cc]["gbT"], "modT": modT, "gainsT": gainsT}
            m.update(wl)
            in_maps.append(m)
        resB3 = _run("B3", in_maps)
        xT = [resB3[cc]["xT_out"] for cc in range(NCORES)]
    out = np.zeros((SEQ, D), np.float32)
    for cc in range(NCORES):
        out[toks[cc]] = xT[cc].T
    return out.reshape(1, SEQ, D)
```
